# Optimizing a Trainium2 kernel written in Bass

```python
import jax
import jax.numpy as jnp
from jax import lax
import numpy as np

D_MODEL = 1024
BATCH = 32
SEQ = 256
DEPTH = 4
DEC_BATCH = 8
DEC_SEQ = 4096
PAST_LEN = 256

MIX_W = D_MODEL
ATTN_W = MIX_W // 4
POOL_W = MIX_W // 4
CONV_W = MIX_W // 4
GM_W = MIX_W - ATTN_W - POOL_W - CONV_W
HEAD_DIM = 64
N_Q_HEADS = ATTN_W // HEAD_DIM
N_KV_HEADS = N_Q_HEADS // 2
Q_PER_KV = N_Q_HEADS // N_KV_HEADS
KV_W = N_KV_HEADS * HEAD_DIM
WINDOW = 128
ATTN_BLOCK = 128
GRID_W = 64
ROPE_BASE = 10000.0
ROPE_PAIRS = HEAD_DIM // 4
POOL_GROUPS = 4
POOL_GROUP_W = POOL_W // POOL_GROUPS
POOL_SIZES = (2, 4, 8, 16)
CONV_WIDTH = 31
GM_GROUPS = 4
GM_GROUP_W = GM_W // GM_GROUPS
GM_CHUNK = 128
D_FF = 4 * D_MODEL
N_MOD = 6
EPS = 1e-6
NEG_INF = -1e30
IN_SPLITS = (ATTN_W, ATTN_W + KV_W, ATTN_W + 2 * KV_W, ATTN_W + 2 * KV_W + POOL_W,
             ATTN_W + 2 * KV_W + POOL_W + 2 * CONV_W)
IN_W = IN_SPLITS[-1] + 2 * GM_W

kernel_name = 'hybrid_diffusion_parallel_heads_step'


def _rmsnorm(x, g):
    xf = x.astype(jnp.float32)
    y = xf * lax.rsqrt(jnp.mean(jnp.square(xf), axis=-1, keepdims=True) + EPS)
    return (y * g.astype(jnp.float32)).astype(x.dtype)


def _axial_rope_tables(rows):
    row = jnp.repeat(jnp.arange(rows), GRID_W).astype(jnp.float32)
    col = jnp.tile(jnp.arange(GRID_W), rows).astype(jnp.float32)
    inv = ROPE_BASE ** (-jnp.arange(ROPE_PAIRS, dtype=jnp.float32) / ROPE_PAIRS)
    ang = jnp.concatenate([row[:, None] * inv, col[:, None] * inv], axis=-1)
    return jnp.cos(ang), jnp.sin(ang)


def _apply_rope(x, cos, sin):
    xf = x.astype(jnp.float32)
    x1, x2 = xf[..., :HEAD_DIM // 2], xf[..., HEAD_DIM // 2:]
    cos = cos[None, :, None, :]
    sin = sin[None, :, None, :]
    return jnp.concatenate([x1 * cos - x2 * sin, x1 * sin + x2 * cos], axis=-1).astype(x.dtype)


def _softmax_with_sink(s, sink):
    sk = jnp.broadcast_to(sink.astype(jnp.float32).reshape(N_KV_HEADS, Q_PER_KV, 1, 1), s.shape[:-1] + (1,))
    p = jax.nn.softmax(jnp.concatenate([s, sk], axis=-1), axis=-1)
    return p[..., :-1]


def _context_attention(q, k, v, sink):
    b, n, _, _ = q.shape
    nb = n // ATTN_BLOCK
    scale = HEAD_DIM ** -0.5
    qb = jnp.moveaxis(q.reshape(b, nb, ATTN_BLOCK, N_KV_HEADS, Q_PER_KV, HEAD_DIM), 1, 0)

    def one_block(qblk):
        s = jnp.einsum('bqkgd,bskd->bkgqs', qblk, k).astype(jnp.float32) * scale
        p = _softmax_with_sink(s, sink)
        return jnp.einsum('bkgqs,bskd->bqkgd', p.astype(v.dtype), v)

    o = lax.map(one_block, qb)
    return jnp.moveaxis(o, 0, 1).reshape(b, n, ATTN_W)


def _latent_attention(q, k, v, ck, cv, sink):
    b, n, _, _ = q.shape
    nb = n // ATTN_BLOCK
    scale = HEAD_DIM ** -0.5
    pad = ((0, 0), (ATTN_BLOCK, ATTN_BLOCK), (0, 0), (0, 0))
    kp = jnp.pad(k, pad).reshape(b, nb + 2, ATTN_BLOCK, N_KV_HEADS, HEAD_DIM)
    vp = jnp.pad(v, pad).reshape(b, nb + 2, ATTN_BLOCK, N_KV_HEADS, HEAD_DIM)
    kb = jnp.concatenate([kp[:, :-2], kp[:, 1:-1], kp[:, 2:]], axis=2)
    vb = jnp.concatenate([vp[:, :-2], vp[:, 1:-1], vp[:, 2:]], axis=2)
    qb = q.reshape(b, nb, ATTN_BLOCK, N_KV_HEADS, Q_PER_KV, HEAD_DIM)
    blk = jnp.arange(nb)[:, None, None]
    qpos = blk * ATTN_BLOCK + jnp.arange(ATTN_BLOCK)[None, :, None]
    kpos = (blk - 1) * ATTN_BLOCK + jnp.arange(3 * ATTN_BLOCK)[None, None, :]
    valid = (jnp.abs(qpos - kpos) <= WINDOW) & (kpos >= 0) & (kpos < n)
    s_loc = jnp.einsum('bnqkgd,bnskd->bnkgqs', qb, kb).astype(jnp.float32) * scale
    s_loc = jnp.where(valid[None, :, None, None], s_loc, NEG_INF)
    s_ctx = jnp.einsum('bnqkgd,bskd->bnkgqs', qb, ck).astype(jnp.float32) * scale
    p = _softmax_with_sink(jnp.concatenate([s_loc, s_ctx], axis=-1), sink).astype(v.dtype)
    n_loc = 3 * ATTN_BLOCK
    o = (jnp.einsum('bnkgqs,bnskd->bnqkgd', p[..., :n_loc], vb)
         + jnp.einsum('bnkgqs,bskd->bnqkgd', p[..., n_loc:], cv))
    return o.reshape(b, n, ATTN_W)


def _centred_mean(x, size):
    n = x.shape[1]
    xf = x.astype(jnp.float32)
    cs = jnp.pad(jnp.cumsum(xf, axis=1), ((0, 0), (1, 0), (0, 0)))
    t = jnp.arange(n)
    half = size // 2
    lo = jnp.clip(t - half, 0, n)
    hi = jnp.clip(t + half, 0, n)
    cnt = (hi - lo).astype(jnp.float32)
    return (cs[:, hi] - cs[:, lo]) / cnt[None, :, None]


def _pool_mixer(xp, pool_w, pool_scale):
    b, n, _ = xp.shape
    groups = jnp.split(xp, POOL_GROUPS, axis=-1)
    pooled = jnp.concatenate([_centred_mean(g, s) for g, s in zip(groups, POOL_SIZES)], axis=-1)
    pooled = pooled.astype(xp.dtype) - xp
    y = jnp.einsum('blgc,gcd->blgd', pooled.reshape(b, n, POOL_GROUPS, POOL_GROUP_W), pool_w)
    return y.reshape(b, n, POOL_W) * pool_scale


def _conv_mixer(xc, conv_dw, conv_b, conv_norm, conv_pw):
    a, gate = jnp.split(xc, 2, axis=-1)
    u = a * jax.nn.sigmoid(gate)
    half = CONV_WIDTH // 2
    y = lax.conv_general_dilated(u, conv_dw[:, None, :], window_strides=(1,),
                                 padding=[(half, half)], dimension_numbers=('NWC', 'WIO', 'NWC'),
                                 feature_group_count=CONV_W) + conv_b
    y = jax.nn.silu(_rmsnorm(y, conv_norm))
    return y @ conv_pw


def _gmlp_mixer(xg, gm_norm, gm_ws, gm_b):
    b, n, _ = xg.shape
    u, v = jnp.split(jax.nn.gelu(xg), 2, axis=-1)
    v = _rmsnorm(v, gm_norm)
    vc = v.reshape(b, n // GM_CHUNK, GM_CHUNK, GM_GROUPS, GM_GROUP_W)
    sv = jnp.einsum('gpq,bnqgc->bnpgc', gm_ws, vc) + gm_b.T[None, None, :, :, None]
    return u * sv.reshape(b, n, GM_W)


def _token_mixers(h, p, ctx_kv, rope):
    b, n, _ = h.shape
    proj = h @ p['w_in']
    q, k, v, xp, xc, xg = jnp.split(proj, IN_SPLITS, axis=-1)
    q = q.reshape(b, n, N_Q_HEADS, HEAD_DIM)
    k = k.reshape(b, n, N_KV_HEADS, HEAD_DIM)
    v = v.reshape(b, n, N_KV_HEADS, HEAD_DIM)
    if ctx_kv is None:
        attn = _context_attention(q, k, v, p['attn_sink'])
    else:
        cos, sin = rope
        q = _apply_rope(q, cos, sin)
        k = _apply_rope(k, cos, sin)
        attn = _latent_attention(q, k, v, ctx_kv[0], ctx_kv[1], p['attn_sink'])
    pool = _pool_mixer(xp, p['pool_w'], p['pool_scale'])
    conv = _conv_mixer(xc, p['conv_dw'], p['conv_b'], p['conv_norm'], p['conv_pw'])
    gm = _gmlp_mixer(xg, p['gm_norm'], p['gm_ws'], p['gm_b'])
    mix = jnp.concatenate([attn, pool, conv, gm], axis=-1) @ p['w_out']
    return mix, k, v


def _layer(x, mod, p, ctx_kv, rope):
    sh1, sc1, g1, sh2, sc2, g2 = jnp.split(mod[:, None, :].astype(x.dtype), N_MOD, axis=-1)
    h = _rmsnorm(x, p['norm1']) * (1 + sc1) + sh1
    mix, k, v = _token_mixers(h, p, ctx_kv, rope)
    x = x + g1 * mix
    h = _rmsnorm(x, p['norm2']) * (1 + sc2) + sh2
    f = jnp.square(jax.nn.relu(h @ p['w_mlp1'])) @ p['w_mlp2']
    x = x + g2 * f
    return x, k, v


def setup_inputs(seed: int = 0) -> dict:
    key = jax.random.key(seed)
    ks = jax.random.split(key, 25)

    def nrm(k, shape, scale):
        return jax.random.normal(k, shape, jnp.float32) * scale

    def gain(k, shape):
        return 1.0 + 0.05 * jax.random.normal(k, shape, jnp.float32)

    cache_shape = (DEC_BATCH, DEPTH, PAST_LEN, N_KV_HEADS, HEAD_DIM)
    return {
        'x_prompt': nrm(ks[0], (BATCH, SEQ, D_MODEL), 1.0),
        'x_sample': nrm(ks[1], (DEC_BATCH, DEC_SEQ, D_MODEL), 1.0),
        'cache_k': nrm(ks[2], cache_shape, 1.0),
        'cache_v': nrm(ks[3], cache_shape, 1.0),
        'c': nrm(ks[4], (DEC_BATCH, D_MODEL), 1.0),
        'c_ctx': nrm(ks[5], (D_MODEL,), 1.0),
        'w_ada': nrm(ks[6], (DEPTH, D_MODEL, N_MOD * D_MODEL), 0.5 * D_MODEL ** -0.5),
        'b_ada': nrm(ks[7], (DEPTH, N_MOD * D_MODEL), 0.02),
        'norm1': gain(ks[8], (DEPTH, D_MODEL)),
        'norm2': gain(ks[9], (DEPTH, D_MODEL)),
        'w_in': nrm(ks[10], (DEPTH, D_MODEL, IN_W), D_MODEL ** -0.5),
        'w_out': nrm(ks[11], (DEPTH, MIX_W, D_MODEL), MIX_W ** -0.5),
        'attn_sink': nrm(ks[12], (DEPTH, N_Q_HEADS), 0.5),
        'pool_w': nrm(ks[13], (DEPTH, POOL_GROUPS, POOL_GROUP_W, POOL_GROUP_W), POOL_GROUP_W ** -0.5),
        'pool_scale': gain(ks[14], (DEPTH, POOL_W)),
        'conv_dw': nrm(ks[15], (DEPTH, CONV_WIDTH, CONV_W), CONV_WIDTH ** -0.5),
        'conv_b': nrm(ks[16], (DEPTH, CONV_W), 0.02),
        'conv_norm': gain(ks[17], (DEPTH, CONV_W)),
        'conv_pw': nrm(ks[18], (DEPTH, CONV_W, CONV_W), CONV_W ** -0.5),
        'gm_norm': gain(ks[19], (DEPTH, GM_W // 2 * 2 // 2 * 1 if False else GM_W)),
        'gm_ws': nrm(ks[20], (DEPTH, GM_GROUPS, GM_CHUNK, GM_CHUNK), GM_CHUNK ** -0.5),
        'gm_b': gain(ks[21], (DEPTH, GM_GROUPS, GM_CHUNK)),
        'w_mlp1': nrm(ks[22], (DEPTH, D_MODEL, D_FF), D_MODEL ** -0.5),
        'w_mlp2': nrm(ks[23], (DEPTH, D_FF, D_MODEL), D_FF ** -0.5),
        'final_norm': gain(ks[24], (D_MODEL,)),
    }


def reference(x_prompt, x_sample, cache_k, cache_v, c, c_ctx, w_ada, b_ada, norm1, norm2,
              w_in, w_out, attn_sink, pool_w, pool_scale, conv_dw, conv_b, conv_norm, conv_pw,
              gm_norm, gm_ws, gm_b, w_mlp1, w_mlp2, final_norm):
    rows = x_sample.shape[1] // GRID_W
    rope = _axial_rope_tables(rows)
    silu_ctx = jax.nn.silu(c_ctx)[None, :]
    silu_c = jax.nn.silu(c)
    xc_stream = x_prompt
    xs_stream = x_sample
    ks_out = []
    vs_out = []
    for l in range(DEPTH):
        p = {'norm1': norm1[l], 'norm2': norm2[l], 'w_in': w_in[l], 'w_out': w_out[l],
             'attn_sink': attn_sink[l], 'pool_w': pool_w[l], 'pool_scale': pool_scale[l],
             'conv_dw': conv_dw[l], 'conv_b': conv_b[l], 'conv_norm': conv_norm[l],
             'conv_pw': conv_pw[l], 'gm_norm': gm_norm[l], 'gm_ws': gm_ws[l], 'gm_b': gm_b[l],
             'w_mlp1': w_mlp1[l], 'w_mlp2': w_mlp2[l]}
        mod_ctx = silu_ctx @ w_ada[l] + b_ada[l]
        mod_lat = silu_c @ w_ada[l] + b_ada[l]
        xc_stream, k_ctx, v_ctx = _layer(xc_stream, mod_ctx, p, None, None)
        ks_out.append(k_ctx)
        vs_out.append(v_ctx)
        xs_stream, _, _ = _layer(xs_stream, mod_lat, p, (cache_k[:, l], cache_v[:, l]), rope)
    y_prompt = _rmsnorm(xc_stream, final_norm)
    y_sample = _rmsnorm(xs_stream, final_norm)
    new_k = jnp.stack(ks_out, axis=1)
    new_v = jnp.stack(vs_out, axis=1)
    return (y_prompt, y_sample, new_k, new_v)
```

```python
from contextlib import ExitStack
import numpy as np
import concourse.bass as bass
import concourse.mybir as mybir
from concourse.bass_utils import run_bass_kernel_spmd

F32 = mybir.dt.float32
BF16 = mybir.dt.bfloat16
AF = mybir.ActivationFunctionType
ALU = mybir.AluOpType

D = 1024
DEPTH = 4
NP_TOK = 1024
NS_TOK = 4096
NTOK = NP_TOK + NS_TOK
IN_W = 1792
EPS = 1e-6
N_CORES = 8

SAME_ENGINE_SYNC = True
N_DMA_SEMS = 24
N_CV_SEMS = 6
NW = 4


class Sched:
    ENGS = ("pe", "act", "dve", "pool", "sp")

    def __init__(self, nc, stack):
        self.nc = nc
        self.prog = {e: [] for e in self.ENGS}
        self.cnt = {e: 0 for e in self.ENGS}
        self.sem = {e: stack.enter_context(nc.semaphore("s_" + e)) for e in self.ENGS}
        self.dsem = [stack.enter_context(nc.semaphore("d%d" % i)) for i in range(N_DMA_SEMS + N_CV_SEMS)]
        self.dcnt = [0] * (N_DMA_SEMS + N_CV_SEMS)
        self.dnext = {"hw": 0, "sw": 0, "cv": 0}
        self.waited = {}
        self.lastw = {}
        self.readers = {}

    def _need(self, eng, ev):
        if ev is None:
            return
        sk, val, prod = ev
        if prod == eng and (eng == "pe" or not SAME_ENGINE_SYNC):
            return
        if self.waited.get((eng, sk), 0) >= val:
            return
        self.waited[(eng, sk)] = val
        semh = self.sem[sk] if isinstance(sk, str) else self.dsem[sk]
        self.prog[eng].append(("wait", semh, val))

    def _deps(self, eng, reads, writes, is_dma=False):
        for r in reads:
            self._need(eng, self.lastw.get(r))
            if isinstance(r, tuple) and r[0] == "ps":
                for ev in self.readers.get(r, ()):
                    if ev[2] != eng:
                        self._need(eng, ev)
        for w in writes:
            ev = self.lastw.get(w)
            if ev is not None and (is_dma or ev[2] != eng):
                self._need(eng, ev)
            for ev in self.readers.get(w, ()):
                if is_dma or ev[2] != eng:
                    self._need(eng, ev)

    def _commit(self, ev, reads, writes):
        for r in reads:
            lst = self.readers.setdefault(r, [])
            lst[:] = [x for x in lst if x[0] != ev[0]]
            lst.append(ev)
        for w in writes:
            self.lastw[w] = ev
            self.readers[w] = []

    def op(self, eng, fn, reads=(), writes=()):
        self._deps(eng, reads, writes)
        self.cnt[eng] += 1
        ev = (eng, self.cnt[eng], eng)
        self.prog[eng].append(("op", fn, self.sem[eng], 1))
        self._commit(ev, reads, writes)
        return ev

    def dma(self, q, fn, reads=(), writes=(), cv=False):
        half = N_DMA_SEMS // 2
        if cv:
            i = N_DMA_SEMS + self.dnext["cv"]
            self.dnext["cv"] = (self.dnext["cv"] + 1) % N_CV_SEMS
        else:
            kq = "sw" if q == "pool" else "hw"
            i = self.dnext[kq] + (half if kq == "sw" else 0)
            self.dnext[kq] = (self.dnext[kq] + 1) % half
        if self.dcnt[i] > 0:
            self._need(q, (i, self.dcnt[i], None))
        self._deps(q, reads, writes, is_dma=True)
        self.dcnt[i] += 16
        ev = (i, self.dcnt[i], None)
        self.prog[q].append(("op", fn, self.dsem[i], 16))
        self._commit(ev, reads, writes)
        return ev

    def wait_all(self, eng):
        for i in range(N_DMA_SEMS + N_CV_SEMS):
            if self.dcnt[i]:
                self._need(eng, (i, self.dcnt[i], None))
        for e in self.ENGS:
            if e != eng and self.cnt[e]:
                self._need(eng, (e, self.cnt[e], e))

    def emit(self, block):
        mapping = {"pe": block.tensor, "act": block.scalar, "dve": block.vector,
                   "pool": block.gpsimd, "sp": block.sync}
        for e in self.ENGS:
            prog = self.prog[e]

            def body(engine, prog=prog):
                for it in prog:
                    if it[0] == "wait":
                        engine.wait_ge(it[1], it[2])
                    else:
                        it[1](engine).then_inc(it[2], it[3])

            mapping[e](body)


TILES = [("p", 0), ("p", 1)] + [("s", t) for t in range(8)]
N_LAYERS = DEPTH
STOP_AT = None


class _Stop(Exception):
    pass


def ckpt(name):
    if STOP_AT == name:
        raise _Stop()


def build_program(pieces=None):
    record = pieces is None
    if record:
        pieces = []
    nc = bass.Bass("TRN2", target_bir_lowering=False)
    dt_in = lambda name, shape: nc.dram_tensor(name, list(shape), F32, kind="ExternalInput").ap()
    dt_out = lambda name, shape: nc.dram_tensor(name, list(shape), F32, kind="ExternalOutput").ap()

    xs_d = dt_in("xs", [NS_TOK, D])
    xp_d = dt_in("xp", [NP_TOK, D])
    ck_d = dt_in("ck", [DEPTH, 256, 128])
    cv_d = dt_in("cv", [DEPTH, 256, 128])
    cc_d = dt_in("cc", [2, D])
    wd = {
        "w_ada": dt_in("w_ada", [DEPTH, D, 6 * D]),
        "w_in": dt_in("w_in", [DEPTH, D, IN_W]),
        "w_out": dt_in("w_out", [DEPTH, D, D]),
        "w_mlp1": dt_in("w_mlp1", [DEPTH, D, 4 * D]),
        "w_mlp2": dt_in("w_mlp2", [DEPTH, 4 * D, D]),
    }
    b_ada_d = dt_in("b_ada", [DEPTH, 6 * D])
    norm1_d = dt_in("norm1", [DEPTH, D])
    norm2_d = dt_in("norm2", [DEPTH, D])
    sink_d = dt_in("attn_sink", [DEPTH, 4])
    pool_w_d = dt_in("pool_w", [DEPTH, 4, 64, 64])
    pool_scale_d = dt_in("pool_scale", [DEPTH, 256])
    conv_dw_d = dt_in("conv_dw", [DEPTH, 31, 256])
    conv_b_d = dt_in("conv_b", [DEPTH, 256])
    conv_norm_d = dt_in("conv_norm", [DEPTH, 256])
    conv_pw_d = dt_in("conv_pw", [DEPTH, 256, 256])
    gm_norm_d = dt_in("gm_norm", [DEPTH, 256])
    gm_ws_d = dt_in("gm_ws", [DEPTH, 4, 128, 128])
    gm_b_d = dt_in("gm_b", [DEPTH, 4, 128])
    fnorm_d = dt_in("final_norm", [1, D])
    ident_d = dt_in("ident", [128, 128])
    rope_d = dt_in("rope", [2, 128, NS_TOK])
    invc_d = dt_in("invcnt", [4, 128, 2, 512])
    mask_d = dt_in("masks", [2, 128, 256])

    yp_d = dt_out("yp", [NP_TOK, D])
    ys_d = dt_out("ys", [NS_TOK, D])
    nk_d = dt_out("nk", [4, DEPTH, 256, 128])
    nv_d = dt_out("nv", [4, DEPTH, 256, 128])
    rs_d = [nc.dram_tensor("rs%d" % i, [NTOK, D], F32).ap() for i in range(2)]
    wshape = {"w_in": [D, IN_W], "w_out": [D, D], "w_mlp1": [D, 4 * D], "w_mlp2": [4 * D, D]}
    wbf = {k: nc.dram_tensor("bf_" + k, [DEPTH] + v, BF16).ap() for k, v in wshape.items()}
    CV_ROWS = {"w_in": 512, "w_out": 512, "w_mlp1": 256, "w_mlp2": 1024}

    with ExitStack() as st:
        S = Sched(nc, st)
        sb = lambda name, shape, dt=F32: st.enter_context(nc.sbuf_tensor(name, list(shape), dt))
        pst = lambda name, shape, dt=F32: st.enter_context(nc.psum_tensor(name, list(shape), dt))

        X = sb("X", [128, 4, D])
        XS = sb("XS", [128, 2, D])
        XN = [sb("XN%d" % i, [128, D], BF16) for i in range(2)]
        XNT1 = sb("XNT1", [128, 8, 768], BF16)
        XNT2 = sb("XNT2", [128, 8, 512], BF16)
        WB = [sb("WB%d" % i, [128, 8, 512], BF16) for i in range(NW)]
        HID = sb("HID", [128, 16, 512], BF16)
        RELU = [sb("RELU%d" % i, [128, 512], BF16) for i in range(2)]
        MIXT = sb("MIXT", [128, 8, 512], BF16)
        QZ = sb("QZ", [128, 2, 4, 256], BF16)
        KT = sb("KT", [128, 2, 768], BF16)
        VV = sb("VV", [128, 6, 256], BF16)
        ROPE = sb("ROPE", [128, 2, 768])
        T1 = sb("T1", [128, 512])
        T2 = sb("T2", [128, 512])
        PW = 608
        XP = sb("XP", [128, 2, PW])
        PSA = sb("PSA", [128, PW])
        PSBF = sb("PSBF", [128, PW])
        INVC = sb("INVC", [128, 2, 512])
        PM = sb("PM", [128, 2, 512], BF16)
        U = sb("U", [128, 2, PW], BF16)
        DG = sb("DG", [128, 2, 31, 128], BF16)
        CACC = sb("CACC", [128, 2, 512])
        CSQ = sb("CSQ", [128, 2, 512], BF16)
        CZ = sb("CZ", [128, 2, 512], BF16)
        GU = sb("GU", [128, 2, 512])
        GV = sb("GV", [128, 256])
        GVZ = sb("GVZ", [128, 4, 2, 2, 128], BF16)
        PT = sb("PT", [128, 5, 256], BF16)
        DEN = sb("DEN", [128, 256])
        TMPO = [sb("TMPO%d" % i, [128, 512]) for i in range(2)]
        SS = sb("SS", [128, 8])
        RSTD = sb("RSTD", [128, 8])
        SS2 = sb("SS2", [128, 4])
        RSTD2 = sb("RSTD2", [128, 4])
        JUNK = CZ[:].rearrange("p c n -> p (c n)")
        SG = T1
        CR = T2
        SV = T1
        KVO = CACC[:].rearrange("p c (h n) -> p (c h) n", h=2)
        IDF = sb("IDF", [128, 128])
        IDB = sb("IDB", [128, 128], BF16)
        ONESB = sb("ONESB", [128, 128], BF16)
        MASK = sb("MASK", [128, 2, 256], BF16)
        ROWS = sb("ROWS", [128, 128])
        COLS = sb("COLS", [128, 128])
        DWROW = sb("DWROW", [32, 256])
        DWT = sb("DWT", [128, 2, 32])
        EPSC = sb("EPSC", [128, 1])
        SILUT = sb("SILUT", [128, 8, 2], BF16)
        SILUBC = sb("SILUBC", [128, 8, 128], BF16)
        BADA = [sb("BADA%d" % i, [1, 512], BF16) for i in range(2)]
        MODT = sb("MODT", [128, 2, 32, 2])
        AB = sb("AB", [128, 2, 2, 2, 8, 2])
        GBC = sb("GBC", [128, 2, D])
        FNBC = GU[:].rearrange("p c n -> p (c n)")
        CPW = sb("CPW", [128, 2, 256], BF16)
        PBD = sb("PBD", [128, 2, 128], BF16)
        WSR = PSA[:, 0:512].rearrange("p (g q) -> p g q", g=4)
        WST = sb("WST", [128, 4, 128], BF16)
        GBT = sb("GBT", [128, 2, 128])
        CKR = GV[:].rearrange("p (c f) -> p c f", c=2)
        CKD = sb("CKD", [128, 2, 2, 2, 64], BF16)
        KTC = sb("KTC", [128, 2, 256], BF16)
        VVC = sb("VVC", [128, 2, 256], BF16)
        SINK = sb("SINK", [128, 4])

        PSB = [pst("PSB%d" % i, [128, 512]) for i in range(8)]

        block = st.enter_context(nc.Block())

        bank_rr = [0]

        def nbank():
            b = bank_rr[0]
            bank_rr[0] = (b + 1) % 4
            return b

        def psv16(b):
            return PSB[b][:].bitcast(BF16)

        wstate = {"issued": 0, "used": 0}

        held = {}

        def rel(slot):
            held.pop(slot, None)

        def issue_piece(i):
            name, l, r0, c0, ncol = pieces[i]
            slot = i % NW
            assert record or slot not in held, ("weight ring slot still in use", i, pieces[i], held)
            dst = WB[slot][:, :, 0:ncol]
            if name != "w_ada":
                src = wbf[name][l, r0:r0 + 1024, c0:c0 + ncol].rearrange("(k p) c -> p k c", p=128)
                cr = CV_ROWS[name]
                rk = [("wbf", name, l, j) for j in range(r0 // cr, (r0 + 1024) // cr)]
                S.dma("sp", lambda e, dst=dst, src=src: e.dma_start(out=dst, in_=src),
                      reads=rk, writes=[("WB", slot)])
            else:
                src = wd[name][l, r0:r0 + 1024, c0:c0 + ncol].rearrange("(k p) c -> p k c", p=128)
                S.dma("pool", lambda e, dst=dst, src=src: e.dma_start(out=dst, in_=src),
                      writes=[("WB", slot)])

        def next_piece(name, l, r0, c0, ncol):
            i = wstate["used"]
            if record:
                pieces.append((name, l, r0, c0, ncol))
            assert pieces[i] == (name, l, r0, c0, ncol), (pieces[i], name, l, r0, c0, ncol)
            while wstate["issued"] < min(len(pieces), i + NW - 1):
                issue_piece(wstate["issued"])
                wstate["issued"] += 1
            wstate["used"] += 1
            held[i % NW] = pieces[i]
            return i % NW

        def mm(out, lhsT, rhs, start, stop, reads, writes, sgc=False):
            S.op("pe", lambda e: e.matmul(out, lhsT=lhsT, rhs=rhs, start=start, stop=stop, skip_group_check=sgc),
                 reads=reads, writes=writes)

        def tr(out, in_, ident, reads, writes):
            S.op("pe", lambda e: e.transpose(out, in_, ident), reads=reads, writes=writes)

        def act(out, in_, func, reads, writes, **kw):
            S.op("act", lambda e: e.activation(out=out, in_=in_, func=func, **kw), reads=reads, writes=writes)

        def ts(eng, out, in0, s1, s2, op0, op1, reads, writes):
            if s2 is None:
                S.op(eng, lambda e: e.tensor_scalar(out=out, in0=in0, scalar1=s1, scalar2=None, op0=op0),
                     reads=reads, writes=writes)
            else:
                S.op(eng, lambda e: e.tensor_scalar(out=out, in0=in0, scalar1=s1, scalar2=s2, op0=op0, op1=op1),
                     reads=reads, writes=writes)

        def tt(eng, out, in0, in1, op, reads, writes):
            S.op(eng, lambda e: e.tensor_tensor(out=out, in0=in0, in1=in1, op=op), reads=reads, writes=writes)

        def stt(eng, out, in0, scalar, in1, op0, op1, reads, writes):
            S.op(eng, lambda e: e.scalar_tensor_tensor(out=out, in0=in0, scalar=scalar, in1=in1, op0=op0, op1=op1),
                 reads=reads, writes=writes)

        def cp(eng, out, in_, reads, writes):
            if eng == "act":
                S.op("act", lambda e: e.copy(out=out, in_=in_), reads=reads, writes=writes)
            else:
                S.op(eng, lambda e: e.tensor_copy(out=out, in_=in_), reads=reads, writes=writes)

        def recip(ap, key):
            S.op("dve", lambda e: e.reciprocal(out=ap, in_=ap), reads=[key], writes=[key])

        def memset(eng, ap, val, writes):
            S.op(eng, lambda e: e.memset(ap, val), writes=writes)

        def dma(q, out, in_, reads, writes):
            S.dma(q, lambda e: e.dma_start(out=out, in_=in_), reads=reads, writes=writes)

        dma("sp", IDF[:], ident_d, [], ["IDF"])
        dma("pool", IDB[:], ident_d, [], ["IDB"])
        dma("pool", MASK[:], mask_d.rearrange("m p q -> p m q"), [], ["MASK"])
        memset("dve", ONESB[:], 1.0, ["ONESB"])
        memset("dve", EPSC[:], EPS, ["EPSC"])
        memset("dve", ROWS[:], 0.0, ["ROWS"])
        memset("pool", QZ[:], 0.0, [("QZ", 0), ("QZ", 1)])
        memset("pool", VV[:], 1.0, [("VV", e_) for e_ in range(6)])
        memset("pool", VVC[:], 1.0, ["VVC"])
        memset("pool", GVZ[:], 0.0, [("GVZ", i) for i in range(4)])
        memset("pool", XP[:], 0.0, [("XP", 0), ("XP", 1)])
        memset("pool", U[:], 0.0, [("U", 0), ("U", 1)])
        memset("dve", DWROW[:], 0.0, ["DWROW"])
        for l in range(DEPTH):
            r = l * 24
            dma("sp", ROWS[r:r + 8, :], norm1_d[l].rearrange("(c p) -> c p", p=128), [], ["ROWS"])
            dma("sp", ROWS[r + 8:r + 16, :], norm2_d[l].rearrange("(c p) -> c p", p=128), [], ["ROWS"])
            for j, v in enumerate([pool_scale_d, conv_b_d, conv_norm_d, gm_norm_d]):
                dma("sp", ROWS[r + 16 + 2 * j:r + 18 + 2 * j, :], v[l].rearrange("(c p) -> c p", p=128), [], ["ROWS"])
        dma("sp", ROWS[104:112, :], cc_d[0].rearrange("(c p) -> c p", p=128), [], ["ROWS"])
        dma("sp", ROWS[112:120, :], cc_d[1].rearrange("(c p) -> c p", p=128), [], ["ROWS"])
        b = nbank()
        tr(PSB[b][:, 0:128], ROWS[:], IDF[:], ["ROWS", "IDF"], [("ps", b)])
        cp("dve", COLS[:], PSB[b][:, 0:128], [("ps", b)], ["COLS"])
        for s in range(2):
            act(SILUT[:, :, s], COLS[:, 104 + 8 * s:112 + 8 * s], AF.Silu, ["COLS"], ["SILUT"])

        def gates_for_stream(l, s, piece_ids):
            cp("dve", SILUBC[:], SILUT[:, :, s].unsqueeze(2).to_broadcast([128, 8, 128]), ["SILUT"], ["SILUBC"])
            for n, pi in enumerate(piece_ids):
                slot = next_piece("w_ada", l, 0, pi * 512, 512)
                gi = 0 if pi < 6 else 1
                half = pi % 2
                bb = BADA[n % 2]
                dma("pool", bb[:], b_ada_d[l:l + 1, pi * 512:(pi + 1) * 512], [], [("BADA", n % 2)])
                b = nbank()
                for k in range(8):
                    mm(PSB[b][:], SILUBC[:, k, :], WB[slot][:, k, :], k == 0, False,
                       ["SILUBC", ("WB", slot)], [("ps", b)])
                mm(PSB[b][:], ONESB[0:1, :], bb[0:1, :], False, True, ["ONESB", ("BADA", n % 2)], [("ps", b)])
                cp("act", GBC[:, gi, half * 512:(half + 1) * 512], PSB[b][:], [("ps", b)], ["GBC"])
                rel(slot)

        def ada_fm_gen(l):
            r = l * 24
            par = l % 2
            for pi in (0, 1, 2, 3, 6, 7, 8, 9):
                kind = pi // 2
                slot = next_piece("w_ada", l, 0, pi * 512, 512)
                mi = {0: 0, 1: 1, 3: 2, 4: 3}[kind]
                bb = BADA[pi % 2]
                dma("pool", bb[:], b_ada_d[l:l + 1, pi * 512:(pi + 1) * 512], [], [("BADA", pi % 2)])
                b = nbank()
                for fc in range(4):
                    for k in range(8):
                        mm(PSB[b][:, fc * 2:fc * 2 + 2], WB[slot][:, k, fc * 128:(fc + 1) * 128], SILUT[:, k, :],
                           k == 0 and fc == 0, False, ["SILUT", ("WB", slot)], [("ps", b)], sgc=True)
                    mm(PSB[b][:, fc * 2:fc * 2 + 2], bb[0:1, fc * 128:(fc + 1) * 128],
                       ONESB[0:1, 0:2], False, True, ["ONESB", ("BADA", pi % 2)], [("ps", b)], sgc=True)
                    if fc < 3:
                        yield
                rel(slot)
                c0 = mi * 8 + (pi % 2) * 4
                cp("dve", MODT[:, par, c0:c0 + 4, :].rearrange("p a b -> p (a b)"), PSB[b][:, 0:8], [("ps", b)], [("MODT", par)])
                yield
            for n in range(2):
                for s_ in range(2):
                    stt("dve", AB[:, par, n, 0, :, s_], MODT[:, par, (2 * n + 1) * 8:(2 * n + 2) * 8, s_], 1.0,
                        COLS[:, r + 8 * n:r + 8 * n + 8], ALU.add, ALU.mult, [("MODT", par), "COLS"], [("AB", par)])
                    cp("dve", AB[:, par, n, 1, :, s_], MODT[:, par, (2 * n) * 8:(2 * n + 1) * 8, s_], [("MODT", par)], [("AB", par)])
            yield

        def small_setup_gen(l):
            r = l * 24
            dma("pool", CPW[:], conv_pw_d[l].rearrange("(k p) c -> p k c", p=128), [], ["CPW"])
            memset("dve", PBD[:], 0.0, ["PBD"])
            for g in range(4):
                c, gl = g // 2, g % 2
                dma("pool", PBD[gl * 64:(gl + 1) * 64, c, gl * 64:(gl + 1) * 64], pool_w_d[l, g], [], ["PBD"])
            dma("sp", DWROW[0:31, :], conv_dw_d[l], [], ["DWROW"])
            for c in range(2):
                b = nbank()
                tr(PSB[b][:, 0:32], DWROW[:, c * 128:(c + 1) * 128], IDF[0:32, 0:32], ["DWROW", "IDF"], [("ps", b)])
                cp("dve", DWT[:, c, :], PSB[b][:, 0:32], [("ps", b)], ["DWT"])
                tt("dve", DG[:, c], IDB[:].unsqueeze(1).to_broadcast([128, 31, 128]),
                   DWT[:, c, 0:31].unsqueeze(2).to_broadcast([128, 31, 128]), ALU.mult, ["IDB", "DWT"], ["DG"])
            yield
            dma("sp", WSR, gm_ws_d[l].rearrange("g p q -> p g q"), [], ["PSA"])
            for g in range(4):
                b = nbank()
                tr(PSB[b][:, 0:128], WSR[:, g, :], IDF[:], ["PSA", "IDF"], [("ps", b)])
                cp("dve", WST[:, g, :], PSB[b][:, 0:128], [("ps", b)], ["WST"])
            for g in range(4):
                c, gl = g // 2, g % 2
                dma("sp", GBT[gl * 64:(gl + 1) * 64, c, :], gm_b_d[l, g:g + 1, :].partition_broadcast(64), [], ["GBT"])
            yield
            dma("sp", CKR, ck_d[l].rearrange("(c s) f -> s c f", s=128), [], ["GV"])
            for c in range(2):
                cp("dve", CKD[:, c], CKR[:, c, :].rearrange("s (g d) -> s g d", g=2).unsqueeze(2).to_broadcast([128, 2, 2, 64]),
                   ["GV"], ["CKD"])
            for c in range(2):
                for g in range(2):
                    b = nbank()
                    pv = psv16(b)
                    tr(pv[:, 0:128], CKD[:, c, g].rearrange("s j d -> s (j d)"), IDB[:], ["CKD", "IDB"], [("ps", b)])
                    cp("dve", KTC[:, g, c * 128:(c + 1) * 128], pv[:, 0:128], [("ps", b)], ["KTC"])
            dma("sp", CKR, cv_d[l].rearrange("(c s) f -> s c f", s=128), [], ["GV"])
            for c in range(2):
                cp("dve", VVC[:, c, :].rearrange("s (g j d) -> s g j d", g=2, j=2)[:, :, 0, :],
                   CKR[:, c, :].rearrange("s (g d) -> s g d", g=2),
                   ["GV"], ["VVC"])
            dma("sp", SINK[:], sink_d[l:l + 1, :].partition_broadcast(128), [], ["SINK"])
            act(SINK[:], SINK[:], AF.Exp, ["SINK"], ["SINK"])
            yield

        def xn_transposes(xn, xk, dst_fn, dkey, n, s, par, par_l):
            b = nbank()
            pv = psv16(b)
            for k in range(8):
                tr(pv[:, k * 128:(k + 1) * 128], xn[:, k * 128:(k + 1) * 128], IDB[:], [xk, "IDB"], [("ps", b)])
            for k in range(8):
                if True:
                    ts("dve", dst_fn(k), pv[:, k * 128:(k + 1) * 128],
                       AB[:, par_l, n, 0, k, s:s + 1], AB[:, par_l, n, 1, k, s:s + 1], ALU.mult, ALU.add,
                       [("ps", b), ("AB", par_l)], [dkey])
                else:
                    act(dst_fn(k), pv[:, k * 128:(k + 1) * 128], AF.Identity, [("ps", b), ("AB", par_l)], [dkey],
                        scale=AB[:, par_l, n, 0, k, s:s + 1], bias=AB[:, par_l, n, 1, k, s:s + 1])

        def norm1_block(e, slot, s, par_l):
            xsk = ("XS", slot)
            act(JUNK, XS[:, slot, :], AF.Square, [xsk], [("CZ", 0), ("CZ", 1), ("SS", e)], accum_out=SS[:, e:e + 1])
            act(RSTD[:, e:e + 1], SS[:, e:e + 1], AF.Sqrt, [("SS", e), "EPSC"], [("RSTD", e)], scale=1.0 / D, bias=EPSC[:, 0:1])
            recip(RSTD[:, e:e + 1], ("RSTD", e))
            xn, xk = XN[e % 2], ("XN", e % 2)
            act(xn[:], XS[:, slot, :], AF.Copy, [xsk, ("RSTD", e)], [xk], scale=RSTD[:, e:e + 1])
            yield
            xn_transposes(xn, xk, lambda k: XNT1[:, k, e * 128:(e + 1) * 128], ("XNT", e), 0, s, e % 2, par_l)
            yield

        def norm2_blocks(s, par_l):
            for i in range(4):
                act(JUNK, X[:, i, :], AF.Square, [("X", i + 1)], [("CZ", 0), ("CZ", 1), ("SS2", i + 1)], accum_out=SS2[:, i:i + 1])
            act(RSTD2[:], SS2[:], AF.Sqrt, [("SS2", i + 1) for i in range(4)] + ["EPSC"], ["RSTD2"], scale=1.0 / D, bias=EPSC[:, 0:1])
            recip(RSTD2[:], "RSTD2")
            yield
            for i in range(4):
                xn, xk = XN[i % 2], ("XN", i % 2)
                act(xn[:], X[:, i, :], AF.Copy, [("X", i + 1), "RSTD2"], [xk], scale=RSTD2[:, i:i + 1])
                yield
                xn_transposes(xn, xk, lambda k: XNT2[:, k, i * 128:(i + 1) * 128], ("XNT2", i), 1, s, i % 2, par_l)
                yield

        def tile_geom(kind, ti):
            if kind == "p":
                return ti * 512, 128, 640, [(128, 256), (384, 256)]
            return NP_TOK + ti * 512, (0 if ti > 0 else 128), (768 if ti < 7 else 640), [(128, 512)]

        def tile_id(kind, ti):
            return ti if kind == "p" else 2 + ti

        def src_rows(l, r0, n):
            if l == 0:
                if r0 < NP_TOK:
                    return xp_d[r0:r0 + n, :]
                return xs_d[r0 - NP_TOK:r0 - NP_TOK + n, :]
            return rs_d[(l - 1) % 2][r0:r0 + n, :]

        def rs_key(l, tid, e):
            if l == 0:
                return []
            if e == 0:
                return [("rs", (l - 1) % 2, tid - 1, 4)]
            if e == 5:
                return [("rs", (l - 1) % 2, tid + 1, 1)]
            return [("rs", (l - 1) % 2, tid, e)]

        xs_rr = [0]

        def phases_A(l, kind, ti):
            s = 0 if kind == "p" else 1
            r = l * 24
            row0, lo, hi, segs = tile_geom(kind, ti)
            eblocks = list(range(lo // 128, hi // 128))
            cblocks = [1, 2, 3, 4]
            gap = lambda s0: 32 if (kind == "p" and s0 == 384) else 0
            colof = lambda tok, s0: tok - 96 + gap(s0)
            tid = tile_id(kind, ti)
            xnt_r = [("XNT", e) for e in eblocks]

            def fm_proj(wslot, lhs_fn, t_lo, t_hi):
                b = nbank()
                for k in range(8):
                    mm(PSB[b][:, 0:t_hi - t_lo], lhs_fn(k), XNT1[:, k, t_lo:t_hi], k == 0, k == 7,
                       [("WB", wslot)] + xnt_r, [("ps", b)])
                return b

            def rope_t1t2(bp, n, a, bnd):
                tt("dve", T1[:, 0:n], PSB[bp][:, 0:n], ROPE[:, 0, a:bnd], ALU.mult, [("ps", bp), "ROPE"], ["T1"])
                for (o, i) in [(0, 32), (32, 0), (64, 96), (96, 64)]:
                    tt("dve", T2[o:o + 32, 0:n], PSB[bp][i:i + 32, 0:n], ROPE[o:o + 32, 1, a:bnd], ALU.mult,
                       [("ps", bp), "ROPE"], ["T2"])

            def ph_norm1():
                if kind == "s":
                    t0 = ti * 512 - 128 + lo
                    dma("pool", ROPE[:, :, lo:hi], rope_d[:, :, t0:t0 + hi - lo].rearrange("t p n -> p t n"), [], ["ROPE"])
                vi = 0 if kind == "s" and 0 < ti < 7 else (1 if kind == "s" and ti == 0 else (2 if kind == "s" else 3))
                dma("pool", INVC[:], invc_d[vi], [], ["INVC"])
                for e in eblocks:
                    slot = xs_rr[0] % 2
                    xs_rr[0] += 1
                    dma("pool", XS[:, slot, :], src_rows(l, row0 - 128 + e * 128, 128), rs_key(l, tid, e), [("XS", slot)])
                    yield from norm1_block(e, slot, s, l % 2)

            def ph_proj():
                w0 = next_piece("w_in", l, 0, 0, 512)
                W0 = WB[w0]
                for g in range(2):
                    bq = fm_proj(w0, lambda k: W0[:, k, g * 128:(g + 1) * 128], 128, 640)
                    if kind == "s":
                        rope_t1t2(bq, 512, 128, 640)
                        for j in range(2):
                            tt("pool", QZ[j * 64:(j + 1) * 64, g, :, j * 128:(j + 1) * 128],
                               T1[j * 64:(j + 1) * 64, :].rearrange("p (i q) -> p i q", i=4),
                               T2[j * 64:(j + 1) * 64, :].rearrange("p (i q) -> p i q", i=4), ALU.add,
                               ["T1", "T2"], [("QZ", g)])
                    else:
                        for j in range(2):
                            cp("act", QZ[j * 64:(j + 1) * 64, g, :, j * 128:(j + 1) * 128],
                               PSB[bq][j * 64:(j + 1) * 64, :].rearrange("p (i q) -> p i q", i=4),
                               [("ps", bq)], [("QZ", g)])
                    yield
                kranges = [(lo, 384), (384, hi)]
                for (a, bnd) in kranges:
                    n = bnd - a
                    bk = fm_proj(w0, lambda k: W0[:, k, 256:384], a, bnd)
                    if kind == "s":
                        rope_t1t2(bk, n, a, bnd)
                        for g in range(2):
                            for j in range(2):
                                tt("pool", KT[j * 64:(j + 1) * 64, g, a:bnd], T1[g * 64:(g + 1) * 64, 0:n],
                                   T2[g * 64:(g + 1) * 64, 0:n], ALU.add, ["T1", "T2"], [("KT", g)])
                    else:
                        for g in range(2):
                            for j in range(2):
                                cp("act", KT[j * 64:(j + 1) * 64, g, a:bnd], PSB[bk][g * 64:(g + 1) * 64, 0:n],
                                   [("ps", bk)], [("KT", g)])
                    yield
                for e in eblocks:
                    b = nbank()
                    for k in range(8):
                        mm(PSB[b][:, 0:256], XNT1[:, k, e * 128:(e + 1) * 128], W0[:, k, 256:512], k == 0, k == 7,
                           [("WB", w0), ("XNT", e)], [("ps", b)])
                    cp("dve", VV[:, e, :].rearrange("s (g j d) -> s g j d", g=2, j=2)[:, :, 0, :],
                       PSB[b][:, 128:256].rearrange("s (g d) -> s g d", g=2),
                       [("ps", b)], [("VV", e)])
                    if kind == "p":
                        cp("act", KVO[:, e - 1, :], PSB[b][:, 0:256], [("ps", b)], [("CACC", 0), ("CACC", 1)])
                    if e % 2 == 0:
                        yield
                rel(w0)
                if kind == "p":
                    for sq in range(2):
                        seq = ti * 2 + sq
                        dma("sp", nk_d[seq, l].rearrange("(c s) f -> s c f", s=128), KVO[:, 2 * sq:2 * sq + 2, 0:128],
                            [("CACC", 0), ("CACC", 1)], [])
                        dma("sp", nv_d[seq, l].rearrange("(c s) f -> s c f", s=128), KVO[:, 2 * sq:2 * sq + 2, 128:256],
                            [("CACC", 0), ("CACC", 1)], [])

                yield
                w1 = next_piece("w_in", l, 0, 512, 512)
                w2 = next_piece("w_in", l, 0, 1024, 512)
                hranges = [(max(lo, 112), 384, 128), (384, min(hi, 656), 384 if kind == "p" else 128)]
                for c in range(2):
                    for (a, bnd, s0) in hranges:
                        n = bnd - a
                        bx = fm_proj(w1, lambda k: WB[w1][:, k, c * 128:(c + 1) * 128], a, bnd)
                        cp("dve", XP[:, c, colof(a, s0):colof(bnd, s0)], PSB[bx][:, 0:n], [("ps", bx)], [("XP", c)])
                    yield
                for c in range(2):
                    for (a, bnd, s0) in hranges:
                        n = bnd - a
                        bg = fm_proj(w2, lambda k: WB[w2][:, k, c * 128:(c + 1) * 128], a, bnd)
                        act(SG[:, 0:n], PSB[bg][:, 0:n], AF.Sigmoid, [("ps", bg)], ["T1"])
                        ba = fm_proj(w1, lambda k: WB[w1][:, k, 256 + c * 128:256 + (c + 1) * 128], a, bnd)
                        tt("dve", U[:, c, colof(a, s0):colof(bnd, s0)], PSB[ba][:, 0:n], SG[:, 0:n], ALU.mult,
                           [("ps", ba), "T1"], [("U", c)])
                        yield
                rel(w1)
                for c in range(2):
                    bu = fm_proj(w2, lambda k: WB[w2][:, k, 256 + c * 128:256 + (c + 1) * 128], 128, 640)
                    act(GU[:, c, :], PSB[bu][:], AF.Gelu, [("ps", bu)], [("GU", c)])
                    yield
                rel(w2)
                w3 = next_piece("w_in", l, 0, 1536, 256)
                for i, e in enumerate(cblocks):
                    b = nbank()
                    for k in range(8):
                        mm(PSB[b][:, 0:256], XNT1[:, k, e * 128:(e + 1) * 128], WB[w3][:, k, 0:256], k == 0, k == 7,
                           [("WB", w3), ("XNT", e)], [("ps", b)])
                    act(GV[:], PSB[b][:, 0:256], AF.Gelu, [("ps", b)], ["GV"])
                    act(JUNK[:, 0:256], GV[:], AF.Square, ["GV"], [("CZ", 0), ("CZ", 1), "GSS"], accum_out=SS[:, 6:7])
                    act(RSTD[:, 6:7], SS[:, 6:7], AF.Sqrt, ["GSS", "EPSC"], ["GRS"], scale=1.0 / 256, bias=EPSC[:, 0:1])
                    recip(RSTD[:, 6:7], "GRS")
                    for c in range(2):
                        for gl in range(2):
                            ts("dve", GVZ[:, i, c, gl, gl * 64:(gl + 1) * 64], GV[:, c * 128 + gl * 64:c * 128 + (gl + 1) * 64],
                               RSTD[:, 6:7], None, ALU.mult, None, ["GV", "GRS"], [("GVZ", i)])
                    if i % 2 == 1:
                        yield
                rel(w3)
                yield

            def ph_mix():
                xpk = [("XP", 0), ("XP", 1)]
                uk = [("U", 0), ("U", 1)]
                for (s0, sl) in segs:
                    c0 = colof(s0, s0)
                    if kind == "p" or lo == 128:
                        memset("dve", XP[:, :, c0 - 16:c0], 0.0, xpk)
                        memset("dve", U[:, :, c0 - 16:c0], 0.0, uk)
                    if kind == "p" or hi == 640:
                        memset("dve", XP[:, :, c0 + sl:c0 + sl + 16], 0.0, xpk)
                        memset("dve", U[:, :, c0 + sl:c0 + sl + 16], 0.0, uk)

                def pool_chain(c):
                    for (s0, sl) in segs:
                        c0 = colof(s0, s0)
                        o = s0 - 128
                        A_, B_ = c0 - 8, c0 + sl + 8
                        tt("dve", PSA[:, A_:B_], XP[:, c, A_ - 1:B_ - 1], XP[:, c, A_:B_], ALU.add, xpk, ["PSA"])
                        if c == 0:
                            cp("dve", PSBF[0:64, c0:c0 + sl], PSA[0:64, c0:c0 + sl], ["PSA"], ["PSBF"])
                            tt("dve", PSBF[64:128, c0:c0 + sl], PSA[64:128, c0 - 1:c0 + sl - 1], PSA[64:128, c0 + 1:c0 + sl + 1],
                               ALU.add, ["PSA"], ["PSBF"])
                        else:
                            A_, B_ = c0 - 6, c0 + sl + 6
                            tt("dve", PSBF[:, A_:B_], PSA[:, A_ - 1:B_ - 1], PSA[:, A_ + 1:B_ + 1], ALU.add, ["PSA"], ["PSBF"])
                            A_, B_ = c0 - 4, c0 + sl + 4
                            tt("dve", PSA[:, A_:B_], PSBF[:, A_ - 2:B_ - 2], PSBF[:, A_ + 2:B_ + 2], ALU.add, ["PSBF"], ["PSA"])
                            cp("dve", PSBF[0:64, c0:c0 + sl], PSA[0:64, c0:c0 + sl], ["PSA"], ["PSBF"])
                            tt("dve", PSBF[64:128, c0:c0 + sl], PSA[64:128, c0 - 4:c0 + sl - 4], PSA[64:128, c0 + 4:c0 + sl + 4],
                               ALU.add, ["PSA"], ["PSBF"])
                        tt("dve", PSBF[:, c0:c0 + sl], PSBF[:, c0:c0 + sl], INVC[:, c, o:o + sl], ALU.mult, ["PSBF", "INVC"], ["PSBF"])
                        tt("dve", PM[:, c, o:o + sl], PSBF[:, c0:c0 + sl], XP[:, c, c0:c0 + sl], ALU.subtract, ["PSBF"] + xpk, [("PM", c)])

                def conv_mm(c):
                    b = nbank()
                    for (s0, sl) in segs:
                        o = s0 - 128
                        c0 = colof(s0, s0)
                        for j in range(31):
                            mm(PSB[b][:, o:o + sl], DG[:, c, j, :], U[:, c, c0 + j - 15:c0 + j - 15 + sl], j == 0, j == 30,
                               ["DG", ("U", c)], [("ps", b)])
                    act(CACC[:, c, :], PSB[b][:], AF.Identity, [("ps", b), "COLS"], [("CACC", c)],
                        bias=COLS[:, r + 18 + c:r + 19 + c])
                    act(CSQ[:, c, :], CACC[:, c, :], AF.Square, [("CACC", c)], [("CSQ", c)])

                def gmlp_c(c):
                    b = nbank()
                    for i in range(4):
                        for gl in range(2):
                            mm(PSB[b][:, i * 128:(i + 1) * 128], GVZ[:, i, c, gl, :], WST[:, c * 2 + gl, :],
                               gl == 0, gl == 1, [("GVZ", i), "WST"], [("ps", b)])
                    stt("dve", SV[:].rearrange("p (i q) -> p i q", i=4), PSB[b][:].rearrange("p (i q) -> p i q", i=4),
                        COLS[:, r + 22 + c:r + 23 + c], GBT[:, c, :].unsqueeze(1).to_broadcast([128, 4, 128]),
                        ALU.mult, ALU.add, [("ps", b), "COLS", "GBT"], ["T1"])
                    tt("dve", MIXT[:, 6 + c, :], GU[:, c, :], SV[:], ALU.mult, [("GU", c), "T1"], [("MIXT", 6 + c)])

                pool_chain(0)
                conv_mm(0)
                yield
                pool_chain(1)
                conv_mm(1)
                yield
                gmlp_c(0)
                yield
                for c in range(2):
                    b = nbank()
                    mm(PSB[b][:], PBD[:, c, :], PM[:, c, :], True, True, ["PBD", ("PM", c)], [("ps", b)])
                    act(MIXT[:, 2 + c, :], PSB[b][:], AF.Identity, [("ps", b), "COLS"], [("MIXT", 2 + c)],
                        scale=COLS[:, r + 16 + c:r + 17 + c])
                b = nbank()
                for c in range(2):
                    mm(PSB[b][:], ONESB[:], CSQ[:, c, :], c == 0, c == 1, ["ONESB", ("CSQ", c)], [("ps", b)])
                act(CR[:], PSB[b][:], AF.Sqrt, [("ps", b), "EPSC"], ["T2"], scale=1.0 / 256, bias=EPSC[:, 0:1])
                recip(CR[:], "T2")
                for c in range(2):
                    tt("dve", CACC[:, c, :], CACC[:, c, :], CR[:], ALU.mult, [("CACC", c), "T2"], [("CACC", c)])
                    act(CZ[:, c, :], CACC[:, c, :], AF.Silu, [("CACC", c), "COLS"], [("CZ", c)],
                        scale=COLS[:, r + 20 + c:r + 21 + c])
                yield
                gmlp_c(1)
                yield
                for m in range(2):
                    b = nbank()
                    for k in range(2):
                        mm(PSB[b][:], CPW[:, k, m * 128:(m + 1) * 128], CZ[:, k, :], k == 0, k == 1,
                           ["CPW", ("CZ", k)], [("ps", b)])
                    cp("act", MIXT[:, 4 + m, :], PSB[b][:], [("ps", b)], [("MIXT", 4 + m)])
                yield

            def ph_attn():
                for i, e in enumerate(cblocks):
                    if kind == "s":
                        chunks = []
                        if (e - 1) * 128 >= lo:
                            chunks.append(("loc", e - 1, 0))
                        chunks.append(("loc", e, None))
                        if (e + 1) * 128 < hi:
                            chunks.append(("loc", e + 1, 1))
                        chunks += [("ctx", 0, None), ("ctx", 1, None)]
                    else:
                        s0 = 1 if e <= 2 else 3
                        chunks = [("loc", s0, None), ("loc", s0 + 1, None)]
                    nch = len(chunks)
                    for g in range(2):
                        b5 = 7
                        for ci, (ck, idx, mk) in enumerate(chunks):
                            pb = 4 + ci // 2 if ci < 4 else b5
                            col = (ci % 2) * 256
                            if ck == "loc":
                                lhs, rd = KT[:, g, idx * 128:(idx + 1) * 128], [("KT", g)]
                            else:
                                lhs, rd = KTC[:, g, idx * 128:(idx + 1) * 128], ["KTC"]
                            mm(PSB[pb][:, col:col + 256], lhs, QZ[:, g, i, :], True, True, rd + [("QZ", g)], [("ps", pb)])
                        for ci, (ck, idx, mk) in enumerate(chunks):
                            pb = 4 + ci // 2 if ci < 4 else b5
                            col = (ci % 2) * 256
                            act(PT[:, ci, :], PSB[pb][:, col:col + 256], AF.Exp, [("ps", pb)], [("PT", ci)], scale=0.125)
                            if mk is not None:
                                tt("pool", PT[:, ci, :], PT[:, ci, :], MASK[:, mk, :], ALU.mult, [("PT", ci), "MASK"], [("PT", ci)])
                        yield
                        for ci, (ck, idx, mk) in enumerate(chunks):
                            if ck == "loc":
                                lv, rd = VV[:, idx, g * 128:(g + 1) * 128], [("VV", idx)]
                            else:
                                lv, rd = VVC[:, idx, g * 128:(g + 1) * 128], ["VVC"]
                            mm(PSB[6][:, 0:256], lv, PT[:, ci, :], ci == 0, ci == nch - 1, rd + [("PT", ci)], [("ps", 6)])
                        snk = lambda p0: SINK[p0:p0 + 64, 2 * g:2 * g + 2].unsqueeze(2).to_broadcast([64, 2, 128])
                        for p0 in (0, 64):
                            tt("dve", DEN[p0:p0 + 64, :].rearrange("p (j q) -> p j q", j=2),
                               PSB[6][64:128, 0:256].rearrange("p (j q) -> p j q", j=2), snk(p0), ALU.add,
                               [("ps", 6), "SINK"], ["DEN"])
                        recip(DEN[:], "DEN")
                        for j in range(2):
                            tt("dve", MIXT[j * 64:(j + 1) * 64, g, i * 128:(i + 1) * 128],
                               PSB[6][0:64, j * 128:(j + 1) * 128],
                               DEN[j * 64:(j + 1) * 64, j * 128:(j + 1) * 128], ALU.mult,
                               [("ps", 6), "DEN"], [("MIXT", g)])
                        yield


            return [ph_norm1(), ph_proj(), ph_mix(), ph_attn()]

        def phases_B(l, kind, ti):
            s = 0 if kind == "p" else 1
            last = (l == N_LAYERS - 1)
            r = l * 24
            row0, lo, hi, segs = tile_geom(kind, ti)
            cblocks = [1, 2, 3, 4]
            tid = tile_id(kind, ti)
            tcount = [0]

            def resid_update(b, e, hf, gi):
                t = TMPO[tcount[0] % 2]
                tk = ("TMPO", tcount[0] % 2)
                tcount[0] += 1
                tt("dve", t[:], PSB[b][:], GBC[:, gi, hf * 512:(hf + 1) * 512], ALU.mult,
                   [("ps", b), "GBC"], [tk])
                tt("pool", X[:, e - 1, hf * 512:(hf + 1) * 512], X[:, e - 1, hf * 512:(hf + 1) * 512], t[:], ALU.add,
                   [tk, ("X", e)], [("X", e)])

            def ph_wout():
                for e in cblocks:
                    dma("sp", X[:, e - 1, :], src_rows(l, row0 + (e - 1) * 128, 128), rs_key(l, tid, e), [("X", e)])
                if (kind, ti) == TILES[0] or [(kind, ti)] == [t for t in TILES if t[0] == "s"][:1]:
                    gates_for_stream(l, s, (4, 5, 10, 11))
                for hf in range(2):
                    wo = next_piece("w_out", l, 0, hf * 512, 512)
                    for i, e in enumerate(cblocks):
                        b = nbank()
                        for k in range(8):
                            mm(PSB[b][:], MIXT[:, k, i * 128:(i + 1) * 128], WB[wo][:, k, :], k == 0, k == 7,
                               [("WB", wo), ("MIXT", k)], [("ps", b)])
                        resid_update(b, e, hf, 0)
                        if i == 3:
                            rel(wo)
                        yield
                yield from norm2_blocks(s, l % 2)

            def ph_mlp1(hh):
                for pi in range(4):
                    w = next_piece("w_mlp1", l, 0, (hh * 4 + pi) * 512, 512)
                    for fc in range(4):
                        hc = pi * 4 + fc
                        b = nbank()
                        for k in range(8):
                            mm(PSB[b][:], WB[w][:, k, fc * 128:(fc + 1) * 128], XNT2[:, k, :], k == 0, k == 7,
                               [("WB", w)] + [("XNT2", i_) for i_ in range(4)], [("ps", b)])
                        rl = RELU[hc % 2]
                        rk = ("RELU", hc % 2)
                        act(rl[:], PSB[b][:], AF.Relu, [("ps", b)], [rk])
                        tt("dve", HID[:, hc, :], rl[:], rl[:], ALU.mult, [rk], [("HID", hc)])
                        if fc == 3:
                            rel(w)
                        yield

            def ph_mlp2(hh):
                for hf in range(2):
                    for pc in range(2):
                        w = next_piece("w_mlp2", l, (hh * 2 + pc) * 1024, hf * 512, 512)
                        for i in range(4):
                            for kk in range(8):
                                hc = pc * 8 + kk
                                mm(PSB[4 + i][:], HID[:, hc, i * 128:(i + 1) * 128], WB[w][:, kk, :],
                                   pc == 0 and kk == 0, pc == 1 and kk == 7, [("WB", w), ("HID", hc)], [("ps", 4 + i)])
                            if i == 3:
                                rel(w)
                            yield
                    for i, e in enumerate(cblocks):
                        resid_update(4 + i, e, hf, 1)
                if hh == 1:
                    if not last:
                        for e in cblocks:
                            dma("sp", rs_d[l % 2][row0 + (e - 1) * 128:row0 + e * 128, :], X[:, e - 1, :],
                                [("X", e)], [("rs", l % 2, tile_id(kind, ti), e)])
                    else:
                        for e in cblocks:
                            act(JUNK, X[:, e - 1, :], AF.Square, [("X", e)], [("CZ", 0), ("CZ", 1), ("SS2", e)], accum_out=SS2[:, e - 1:e])
                        act(RSTD2[:], SS2[:], AF.Sqrt, [("SS2", e) for e in cblocks] + ["EPSC"], ["RSTD2"],
                            scale=1.0 / D, bias=EPSC[:, 0:1])
                        recip(RSTD2[:], "RSTD2")
                        dma("sp", FNBC, fnorm_d.partition_broadcast(128), [], [("GU", 0), ("GU", 1)])
                        for e in cblocks:
                            stt("dve", X[:, e - 1, :], X[:, e - 1, :], RSTD2[:, e - 1:e], FNBC, ALU.mult, ALU.mult,
                                [("X", e), "RSTD2", ("GU", 0), ("GU", 1)], [("X", e)])
                            dst = yp_d if kind == "p" else ys_d
                            dma("sp", dst[ti * 512 + (e - 1) * 128:ti * 512 + e * 128, :], X[:, e - 1, :], [("X", e)], [])


            return [ph_wout(), ph_mlp1(0), ph_mlp2(0), ph_mlp1(1), ph_mlp2(1)]

        def interleave(*gens):
            live = [g for g in gens if g is not None]
            while live:
                for g in list(live):
                    try:
                        next(g)
                    except StopIteration:
                        live.remove(g)

        def conversions(l):
            out = []
            for name in ("w_in", "w_out", "w_mlp1", "w_mlp2"):
                cr = CV_ROWS[name]
                nrows = wshape[name][0]
                for j in range(nrows // cr):
                    def th(name=name, j=j, cr=cr):
                        src = wd[name][l, j * cr:(j + 1) * cr, :]
                        dst = wbf[name][l, j * cr:(j + 1) * cr, :]
                        S.dma("pool", lambda e: e.dma_start(out=dst, in_=src), writes=[("wbf", name, l, j)], cv=True)
                    out.append(th)
            return out

        def run_all():
            G = [(l, kind, ti) for l in range(N_LAYERS) for (kind, ti) in TILES]
            cvq = []
            nt = len(TILES)
            ng = len(G)
            cv0 = conversions(0)
            for th in cv0[:2]:
                th()
            interleave(ada_fm_gen(0))
            for th in cv0[2:]:
                th()
            interleave(small_setup_gen(0))
            pa = {0: phases_A(*G[0])}
            for g in pa[0]:
                interleave(g)
            if ng > 1:
                pa[1] = phases_A(*G[1])
                interleave(pa[1][0])
            for m in range(ng):
                l, kind, ti = G[m]
                tpos = m % nt
                pb = phases_B(*G[m])
                nx = pa.get(m + 1)
                if m + 2 < ng:
                    pa[m + 2] = phases_A(*G[m + 2])
                nxt_l = l + 1 if l + 1 < N_LAYERS else None
                ada = ada_fm_gen(nxt_l) if (nxt_l is not None and tpos == max(nt - 4, 0)) else None
                small = small_setup_gen(nxt_l) if (nxt_l is not None and tpos == nt - 2) else None
                if nt == 1 and nxt_l is not None:
                    ada, small = ada_fm_gen(nxt_l), None
                if nxt_l is not None:
                    if tpos == 0:
                        cvq = conversions(nxt_l)
                    for _ in range(3 if tpos < nt - 1 else len(cvq)):
                        if cvq:
                            cvq.pop(0)()
                interleave(pb[0], nx[1] if nx else None)
                interleave(pb[1], nx[2] if nx else None)
                interleave(pb[2], ada)
                interleave(pb[3], nx[3] if nx else None)
                if nt == 1 and nxt_l is not None:
                    interleave(small_setup_gen(nxt_l))
                interleave(pb[4], small, pa[m + 2][0] if m + 2 < ng else None)

        try:
            run_all()
        except _Stop:
            pass
        S.wait_all("sp")
        S.emit(block)
    if record:
        return pieces
    return nc


def _constants():
    ident = np.eye(128, dtype=np.float32)
    t = np.arange(NS_TOK)
    row = (t // 64).astype(np.float32)
    col = (t % 64).astype(np.float32)
    inv = (10000.0 ** (-np.arange(16, dtype=np.float32) / 16)).astype(np.float32)
    ang = np.concatenate([row[:, None] * inv[None, :], col[:, None] * inv[None, :]], axis=1)
    cos = np.cos(ang).astype(np.float32).T
    sin = np.sin(ang).astype(np.float32).T
    cos64 = np.concatenate([cos, cos], 0)
    sin64 = np.concatenate([-sin, sin], 0)
    rope = np.stack([np.concatenate([cos64, cos64], 0), np.concatenate([sin64, sin64], 0)], 0).astype(np.float32)
    sizes = (2, 4, 8, 16)

    def inv_tab(n, t0, length):
        tt_ = np.arange(t0, t0 + length)
        out = np.zeros((128, 2, length), np.float32)
        for g, sz in enumerate(sizes):
            h = sz // 2
            lo = np.clip(tt_ - h, 0, n)
            hi = np.clip(tt_ + h, 0, n)
            c, gl = g // 2, g % 2
            out[gl * 64:(gl + 1) * 64, c, :] = (1.0 / (hi - lo).astype(np.float32))[None, :]
        return out

    invc = np.stack([
        inv_tab(4096, 512, 512),
        inv_tab(4096, 0, 512),
        inv_tab(4096, 4096 - 512, 512),
        np.concatenate([inv_tab(256, 0, 256), inv_tab(256, 0, 256)], axis=2),
    ], 0).astype(np.float32)
    sidx = np.arange(128)[:, None]
    qidx = np.arange(128)[None, :]
    mp = (sidx >= qidx).astype(np.float32)
    mn = (sidx <= qidx).astype(np.float32)
    masks = np.stack([np.concatenate([mp, mp], 1), np.concatenate([mn, mn], 1)], 0).astype(np.float32)
    return ident, rope, invc, masks


_NC_CACHE = {}


def kernel(x_prompt, x_sample, cache_k, cache_v, c, c_ctx, w_ada, b_ada, norm1, norm2,
           w_in, w_out, attn_sink, pool_w, pool_scale, conv_dw, conv_b, conv_norm, conv_pw,
           gm_norm, gm_ws, gm_b, w_mlp1, w_mlp2, final_norm):
    f = lambda a: np.ascontiguousarray(np.asarray(a, dtype=np.float32))
    if "nc" not in _NC_CACHE:
        _NC_CACHE["nc"] = build_program(build_program())
    nc = _NC_CACHE["nc"]
    ident, rope, invc, masks = _constants()
    shared = {
        "w_ada": f(w_ada), "w_in": f(w_in), "w_out": f(w_out), "w_mlp1": f(w_mlp1), "w_mlp2": f(w_mlp2),
        "b_ada": f(b_ada), "norm1": f(norm1), "norm2": f(norm2), "attn_sink": f(attn_sink),
        "pool_w": f(pool_w), "pool_scale": f(pool_scale), "conv_dw": f(conv_dw), "conv_b": f(conv_b),
        "conv_norm": f(conv_norm), "conv_pw": f(conv_pw), "gm_norm": f(gm_norm), "gm_ws": f(gm_ws),
        "gm_b": f(gm_b), "final_norm": f(final_norm).reshape(1, D),
        "ident": ident, "rope": rope, "invcnt": invc, "masks": masks,
    }
    x_prompt = f(x_prompt)
    x_sample = f(x_sample)
    cache_k = f(cache_k)
    cache_v = f(cache_v)
    c = f(c)
    c_ctx = f(c_ctx)
    in_maps = []
    for i in range(N_CORES):
        m = dict(shared)
        m["xs"] = x_sample[i]
        m["xp"] = x_prompt[4 * i:4 * i + 4].reshape(NP_TOK, D)
        m["ck"] = cache_k[i].reshape(DEPTH, 256, 128)
        m["cv"] = cache_v[i].reshape(DEPTH, 256, 128)
        m["cc"] = np.stack([c_ctx, c[i]], 0)
        in_maps.append(m)
    res = run_bass_kernel_spmd(nc, in_maps, core_ids=list(range(N_CORES)))
    rr = res.results
    y_prompt = np.concatenate([r["yp"].reshape(4, 256, D) for r in rr], 0)
    y_sample = np.stack([r["ys"] for r in rr], 0)
    new_k = np.concatenate([r["nk"].reshape(4, DEPTH, 256, 2, 64) for r in rr], 0)
    new_v = np.concatenate([r["nv"].reshape(4, DEPTH, 256, 2, 64) for r in rr], 0)
    return (y_prompt.astype(np.float32), y_sample.astype(np.float32),
            new_k.astype(np.float32), new_v.astype(np.float32))
```

```python
from contextlib import ExitStack
import numpy as np
import concourse.bass as bass
import concourse.mybir as mybir
from concourse.bass_utils import run_bass_kernel_spmd

F32 = mybir.dt.float32
BF16 = mybir.dt.bfloat16
AF = mybir.ActivationFunctionType
ALU = mybir.AluOpType

D = 1024
DEPTH = 4
NP_TOK = 1024
NS_TOK = 4096
NTOK = NP_TOK + NS_TOK
IN_W = 1792
EPS = 1e-6
N_CORES = 8

SAME_ENGINE_SYNC = True
N_DMA_SEMS = 24
N_CV_SEMS = 6
NW = 4


class Sched:
    ENGS = ("pe", "act", "dve", "pool", "sp")

    def __init__(self, nc, stack):
        self.nc = nc
        self.prog = {e: [] for e in self.ENGS}
        self.cnt = {e: 0 for e in self.ENGS}
        self.sem = {e: stack.enter_context(nc.semaphore("s_" + e)) for e in self.ENGS}
        self.dsem = [stack.enter_context(nc.semaphore("d%d" % i)) for i in range(N_DMA_SEMS + N_CV_SEMS)]
        self.dcnt = [0] * (N_DMA_SEMS + N_CV_SEMS)
        self.dnext = {"hw": 0, "sw": 0, "cv": 0}
        self.waited = {}
        self.lastw = {}
        self.readers = {}

    def _need(self, eng, ev):
        if ev is None:
            return
        sk, val, prod = ev
        if prod == eng and (eng == "pe" or not SAME_ENGINE_SYNC):
            return
        if self.waited.get((eng, sk), 0) >= val:
            return
        self.waited[(eng, sk)] = val
        semh = self.sem[sk] if isinstance(sk, str) else self.dsem[sk]
        self.prog[eng].append(("wait", semh, val))

    def _deps(self, eng, reads, writes, is_dma=False):
        for r in reads:
            self._need(eng, self.lastw.get(r))
            if isinstance(r, tuple) and r[0] == "ps":
                for ev in self.readers.get(r, ()):
                    if ev[2] != eng:
                        self._need(eng, ev)
        for w in writes:
            ev = self.lastw.get(w)
            if ev is not None and (is_dma or ev[2] != eng):
                self._need(eng, ev)
            for ev in self.readers.get(w, ()):
                if is_dma or ev[2] != eng:
                    self._need(eng, ev)

    def _commit(self, ev, reads, writes):
        for r in reads:
            lst = self.readers.setdefault(r, [])
            lst[:] = [x for x in lst if x[0] != ev[0]]
            lst.append(ev)
        for w in writes:
            self.lastw[w] = ev
            self.readers[w] = []

    def op(self, eng, fn, reads=(), writes=()):
        self._deps(eng, reads, writes)
        self.cnt[eng] += 1
        ev = (eng, self.cnt[eng], eng)
        self.prog[eng].append(("op", fn, self.sem[eng], 1))
        self._commit(ev, reads, writes)
        return ev

    def dma(self, q, fn, reads=(), writes=(), cv=False):
        half = N_DMA_SEMS // 2
        if cv:
            i = N_DMA_SEMS + self.dnext["cv"]
            self.dnext["cv"] = (self.dnext["cv"] + 1) % N_CV_SEMS
        else:
            kq = "sw" if q == "pool" else "hw"
            i = self.dnext[kq] + (half if kq == "sw" else 0)
            self.dnext[kq] = (self.dnext[kq] + 1) % half
        if self.dcnt[i] > 0:
            self._need(q, (i, self.dcnt[i], None))
        self._deps(q, reads, writes, is_dma=True)
        self.dcnt[i] += 16
        ev = (i, self.dcnt[i], None)
        self.prog[q].append(("op", fn, self.dsem[i], 16))
        self._commit(ev, reads, writes)
        return ev

    def wait_all(self, eng):
        for i in range(N_DMA_SEMS + N_CV_SEMS):
            if self.dcnt[i]:
                self._need(eng, (i, self.dcnt[i], None))
        for e in self.ENGS:
            if e != eng and self.cnt[e]:
                self._need(eng, (e, self.cnt[e], e))

    def emit(self, block):
        mapping = {"pe": block.tensor, "act": block.scalar, "dve": block.vector,
                   "pool": block.gpsimd, "sp": block.sync}
        for e in self.ENGS:
            prog = self.prog[e]

            def body(engine, prog=prog):
                for it in prog:
                    if it[0] == "wait":
                        engine.wait_ge(it[1], it[2])
                    else:
                        it[1](engine).then_inc(it[2], it[3])

            mapping[e](body)


TILES = [("p", 0), ("p", 1)] + [("s", t) for t in range(8)]
N_LAYERS = DEPTH
STOP_AT = None


class _Stop(Exception):
    pass


def ckpt(name):
    if STOP_AT == name:
        raise _Stop()


def build_program(pieces=None):
    record = pieces is None
    if record:
        pieces = []
    nc = bass.Bass("TRN2", target_bir_lowering=False)
    dt_in = lambda name, shape: nc.dram_tensor(name, list(shape), F32, kind="ExternalInput").ap()
    dt_out = lambda name, shape: nc.dram_tensor(name, list(shape), F32, kind="ExternalOutput").ap()

    xs_d = dt_in("xs", [NS_TOK, D])
    xp_d = dt_in("xp", [NP_TOK, D])
    ck_d = dt_in("ck", [DEPTH, 256, 128])
    cv_d = dt_in("cv", [DEPTH, 256, 128])
    cc_d = dt_in("cc", [2, D])
    wd = {
        "w_ada": dt_in("w_ada", [DEPTH, D, 6 * D]),
        "w_in": dt_in("w_in", [DEPTH, D, IN_W]),
        "w_out": dt_in("w_out", [DEPTH, D, D]),
        "w_mlp1": dt_in("w_mlp1", [DEPTH, D, 4 * D]),
        "w_mlp2": dt_in("w_mlp2", [DEPTH, 4 * D, D]),
    }
    b_ada_d = dt_in("b_ada", [DEPTH, 6 * D])
    norm1_d = dt_in("norm1", [DEPTH, D])
    norm2_d = dt_in("norm2", [DEPTH, D])
    sink_d = dt_in("attn_sink", [DEPTH, 4])
    pool_w_d = dt_in("pool_w", [DEPTH, 4, 64, 64])
    pool_scale_d = dt_in("pool_scale", [DEPTH, 256])
    conv_dw_d = dt_in("conv_dw", [DEPTH, 31, 256])
    conv_b_d = dt_in("conv_b", [DEPTH, 256])
    conv_norm_d = dt_in("conv_norm", [DEPTH, 256])
    conv_pw_d = dt_in("conv_pw", [DEPTH, 256, 256])
    gm_norm_d = dt_in("gm_norm", [DEPTH, 256])
    gm_ws_d = dt_in("gm_ws", [DEPTH, 4, 128, 128])
    gm_b_d = dt_in("gm_b", [DEPTH, 4, 128])
    fnorm_d = dt_in("final_norm", [1, D])
    ident_d = dt_in("ident", [128, 128])
    rope_d = dt_in("rope", [2, 128, NS_TOK])
    invc_d = dt_in("invcnt", [4, 128, 2, 512])
    mask_d = dt_in("masks", [2, 128, 256])

    yp_d = dt_out("yp", [NP_TOK, D])
    ys_d = dt_out("ys", [NS_TOK, D])
    nk_d = dt_out("nk", [4, DEPTH, 256, 128])
    nv_d = dt_out("nv", [4, DEPTH, 256, 128])
    rs_d = [nc.dram_tensor("rs%d" % i, [NTOK, D], F32).ap() for i in range(2)]
    wshape = {"w_in": [D, IN_W], "w_out": [D, D], "w_mlp1": [D, 4 * D], "w_mlp2": [4 * D, D]}
    wbf = {k: nc.dram_tensor("bf_" + k, [DEPTH] + v, BF16).ap() for k, v in wshape.items()}
    CV_ROWS = {"w_in": 512, "w_out": 512, "w_mlp1": 256, "w_mlp2": 1024}

    with ExitStack() as st:
        S = Sched(nc, st)
        sb = lambda name, shape, dt=F32: st.enter_context(nc.sbuf_tensor(name, list(shape), dt))
        pst = lambda name, shape, dt=F32: st.enter_context(nc.psum_tensor(name, list(shape), dt))

        X = sb("X", [128, 4, D])
        XS = sb("XS", [128, 2, D])
        XN = [sb("XN%d" % i, [128, D], BF16) for i in range(2)]
        XNT1 = sb("XNT1", [128, 8, 768], BF16)
        XNT2 = sb("XNT2", [128, 8, 512], BF16)
        WB = [sb("WB%d" % i, [128, 8, 512], BF16) for i in range(NW)]
        HID = sb("HID", [128, 16, 512], BF16)
        RELU = [sb("RELU%d" % i, [128, 512], BF16) for i in range(2)]
        MIXT = sb("MIXT", [128, 8, 512], BF16)
        QZ = sb("QZ", [128, 2, 4, 256], BF16)
        KT = sb("KT", [128, 2, 768], BF16)
        VV = sb("VV", [128, 6, 256], BF16)
        ROPE = sb("ROPE", [128, 2, 768])
        T1 = sb("T1", [128, 512])
        T2 = sb("T2", [128, 512])
        PW = 608
        XP = sb("XP", [128, 2, PW])
        PSA = sb("PSA", [128, PW])
        PSBF = sb("PSBF", [128, PW])
        INVC = sb("INVC", [128, 2, 512])
        PM = sb("PM", [128, 2, 512], BF16)
        U = sb("U", [128, 2, PW], BF16)
        DG = sb("DG", [128, 2, 31, 128], BF16)
        CACC = sb("CACC", [128, 2, 512])
        CSQ = sb("CSQ", [128, 2, 512], BF16)
        CZ = sb("CZ", [128, 2, 512], BF16)
        GU = sb("GU", [128, 2, 512])
        GV = sb("GV", [128, 256])
        GVZ = sb("GVZ", [128, 4, 2, 2, 128], BF16)
        PT = sb("PT", [128, 5, 256], BF16)
        DEN = sb("DEN", [128, 256])
        TMPO = [sb("TMPO%d" % i, [128, 512]) for i in range(2)]
        SS = sb("SS", [128, 8])
        RSTD = sb("RSTD", [128, 8])
        SS2 = sb("SS2", [128, 4])
        RSTD2 = sb("RSTD2", [128, 4])
        JUNK = CZ[:].rearrange("p c n -> p (c n)")
        SG = T1
        CR = T2
        SV = T1
        KVO = CACC[:].rearrange("p c (h n) -> p (c h) n", h=2)
        IDF = sb("IDF", [128, 128])
        IDB = sb("IDB", [128, 128], BF16)
        ONESB = sb("ONESB", [128, 128], BF16)
        MASK = sb("MASK", [128, 2, 256], BF16)
        ROWS = sb("ROWS", [128, 128])
        COLS = sb("COLS", [128, 128])
        DWROW = sb("DWROW", [32, 256])
        DWT = sb("DWT", [128, 2, 32])
        EPSC = sb("EPSC", [128, 1])
        SILUT = sb("SILUT", [128, 8, 2], BF16)
        SILUBC = sb("SILUBC", [128, 8, 128], BF16)
        BADA = [sb("BADA%d" % i, [1, 512], BF16) for i in range(2)]
        MODT = sb("MODT", [128, 2, 32, 2])
        AB = sb("AB", [128, 2, 2, 2, 8, 2])
        GBC = sb("GBC", [128, 2, D])
        FNBC = GU[:].rearrange("p c n -> p (c n)")
        CPW = sb("CPW", [128, 2, 256], BF16)
        PBD = sb("PBD", [128, 2, 128], BF16)
        WSR = PSA[:, 0:512].rearrange("p (g q) -> p g q", g=4)
        WST = sb("WST", [128, 4, 128], BF16)
        GBT = sb("GBT", [128, 2, 128])
        CKR = GV[:].rearrange("p (c f) -> p c f", c=2)
        CKD = sb("CKD", [128, 2, 2, 2, 64], BF16)
        KTC = sb("KTC", [128, 2, 256], BF16)
        VVC = sb("VVC", [128, 2, 256], BF16)
        SINK = sb("SINK", [128, 4])

        PSB = [pst("PSB%d" % i, [128, 512]) for i in range(8)]

        block = st.enter_context(nc.Block())

        bank_rr = [0]

        def nbank():
            b = bank_rr[0]
            bank_rr[0] = (b + 1) % 4
            return b

        def psv16(b):
            return PSB[b][:].bitcast(BF16)

        wstate = {"issued": 0, "used": 0}

        held = {}

        def rel(slot):
            held.pop(slot, None)

        def issue_piece(i):
            name, l, r0, c0, ncol = pieces[i]
            slot = i % NW
            assert record or slot not in held, ("weight ring slot still in use", i, pieces[i], held)
            dst = WB[slot][:, :, 0:ncol]
            if name != "w_ada" and l >= 1:
                src = wbf[name][l, r0:r0 + 1024, c0:c0 + ncol].rearrange("(k p) c -> p k c", p=128)
                cr = CV_ROWS[name]
                rk = [("wbf", name, l, j) for j in range(r0 // cr, (r0 + 1024) // cr)]
                S.dma("sp", lambda e, dst=dst, src=src: e.dma_start(out=dst, in_=src),
                      reads=rk, writes=[("WB", slot)])
            else:
                src = wd[name][l, r0:r0 + 1024, c0:c0 + ncol].rearrange("(k p) c -> p k c", p=128)
                S.dma("pool", lambda e, dst=dst, src=src: e.dma_start(out=dst, in_=src),
                      writes=[("WB", slot)])

        def next_piece(name, l, r0, c0, ncol):
            i = wstate["used"]
            if record:
                pieces.append((name, l, r0, c0, ncol))
            assert pieces[i] == (name, l, r0, c0, ncol), (pieces[i], name, l, r0, c0, ncol)
            while wstate["issued"] < min(len(pieces), i + NW - 1):
                issue_piece(wstate["issued"])
                wstate["issued"] += 1
            wstate["used"] += 1
            held[i % NW] = pieces[i]
            return i % NW

        def mm(out, lhsT, rhs, start, stop, reads, writes, sgc=False):
            S.op("pe", lambda e: e.matmul(out, lhsT=lhsT, rhs=rhs, start=start, stop=stop, skip_group_check=sgc),
                 reads=reads, writes=writes)

        def tr(out, in_, ident, reads, writes):
            S.op("pe", lambda e: e.transpose(out, in_, ident), reads=reads, writes=writes)

        def act(out, in_, func, reads, writes, **kw):
            S.op("act", lambda e: e.activation(out=out, in_=in_, func=func, **kw), reads=reads, writes=writes)

        def ts(eng, out, in0, s1, s2, op0, op1, reads, writes):
            if s2 is None:
                S.op(eng, lambda e: e.tensor_scalar(out=out, in0=in0, scalar1=s1, scalar2=None, op0=op0),
                     reads=reads, writes=writes)
            else:
                S.op(eng, lambda e: e.tensor_scalar(out=out, in0=in0, scalar1=s1, scalar2=s2, op0=op0, op1=op1),
                     reads=reads, writes=writes)

        def tt(eng, out, in0, in1, op, reads, writes):
            S.op(eng, lambda e: e.tensor_tensor(out=out, in0=in0, in1=in1, op=op), reads=reads, writes=writes)

        def stt(eng, out, in0, scalar, in1, op0, op1, reads, writes):
            S.op(eng, lambda e: e.scalar_tensor_tensor(out=out, in0=in0, scalar=scalar, in1=in1, op0=op0, op1=op1),
                 reads=reads, writes=writes)

        def cp(eng, out, in_, reads, writes):
            if eng == "act":
                S.op("act", lambda e: e.copy(out=out, in_=in_), reads=reads, writes=writes)
            else:
                S.op(eng, lambda e: e.tensor_copy(out=out, in_=in_), reads=reads, writes=writes)

        def recip(ap, key):
            S.op("dve", lambda e: e.reciprocal(out=ap, in_=ap), reads=[key], writes=[key])

        def memset(eng, ap, val, writes):
            S.op(eng, lambda e: e.memset(ap, val), writes=writes)

        def dma(q, out, in_, reads, writes):
            S.dma(q, lambda e: e.dma_start(out=out, in_=in_), reads=reads, writes=writes)

        dma("sp", IDF[:], ident_d, [], ["IDF"])
        dma("pool", IDB[:], ident_d, [], ["IDB"])
        dma("pool", MASK[:], mask_d.rearrange("m p q -> p m q"), [], ["MASK"])
        memset("dve", ONESB[:], 1.0, ["ONESB"])
        memset("dve", EPSC[:], EPS, ["EPSC"])
        memset("dve", ROWS[:], 0.0, ["ROWS"])
        memset("pool", QZ[:], 0.0, [("QZ", 0), ("QZ", 1)])
        memset("pool", VV[:], 1.0, [("VV", e_) for e_ in range(6)])
        memset("pool", VVC[:], 1.0, ["VVC"])
        memset("pool", GVZ[:], 0.0, [("GVZ", i) for i in range(4)])
        memset("pool", XP[:], 0.0, [("XP", 0), ("XP", 1)])
        memset("pool", U[:], 0.0, [("U", 0), ("U", 1)])
        memset("dve", DWROW[:], 0.0, ["DWROW"])
        for l in range(DEPTH):
            r = l * 24
            dma("sp", ROWS[r:r + 8, :], norm1_d[l].rearrange("(c p) -> c p", p=128), [], ["ROWS"])
            dma("sp", ROWS[r + 8:r + 16, :], norm2_d[l].rearrange("(c p) -> c p", p=128), [], ["ROWS"])
            for j, v in enumerate([pool_scale_d, conv_b_d, conv_norm_d, gm_norm_d]):
                dma("sp", ROWS[r + 16 + 2 * j:r + 18 + 2 * j, :], v[l].rearrange("(c p) -> c p", p=128), [], ["ROWS"])
        dma("sp", ROWS[104:112, :], cc_d[0].rearrange("(c p) -> c p", p=128), [], ["ROWS"])
        dma("sp", ROWS[112:120, :], cc_d[1].rearrange("(c p) -> c p", p=128), [], ["ROWS"])
        b = nbank()
        tr(PSB[b][:, 0:128], ROWS[:], IDF[:], ["ROWS", "IDF"], [("ps", b)])
        cp("dve", COLS[:], PSB[b][:, 0:128], [("ps", b)], ["COLS"])
        for s in range(2):
            act(SILUT[:, :, s], COLS[:, 104 + 8 * s:112 + 8 * s], AF.Silu, ["COLS"], ["SILUT"])

        def gates_for_stream(l, s, piece_ids):
            cp("dve", SILUBC[:], SILUT[:, :, s].unsqueeze(2).to_broadcast([128, 8, 128]), ["SILUT"], ["SILUBC"])
            for n, pi in enumerate(piece_ids):
                slot = next_piece("w_ada", l, 0, pi * 512, 512)
                gi = 0 if pi < 6 else 1
                half = pi % 2
                bb = BADA[n % 2]
                dma("pool", bb[:], b_ada_d[l:l + 1, pi * 512:(pi + 1) * 512], [], [("BADA", n % 2)])
                b = nbank()
                for k in range(8):
                    mm(PSB[b][:], SILUBC[:, k, :], WB[slot][:, k, :], k == 0, False,
                       ["SILUBC", ("WB", slot)], [("ps", b)])
                mm(PSB[b][:], ONESB[0:1, :], bb[0:1, :], False, True, ["ONESB", ("BADA", n % 2)], [("ps", b)])
                cp("act", GBC[:, gi, half * 512:(half + 1) * 512], PSB[b][:], [("ps", b)], ["GBC"])
                rel(slot)

        def ada_fm_gen(l):
            r = l * 24
            par = l % 2
            for pi in (0, 1, 2, 3, 6, 7, 8, 9):
                kind = pi // 2
                slot = next_piece("w_ada", l, 0, pi * 512, 512)
                mi = {0: 0, 1: 1, 3: 2, 4: 3}[kind]
                bb = BADA[pi % 2]
                dma("pool", bb[:], b_ada_d[l:l + 1, pi * 512:(pi + 1) * 512], [], [("BADA", pi % 2)])
                b = nbank()
                for fc in range(4):
                    for k in range(8):
                        mm(PSB[b][:, fc * 2:fc * 2 + 2], WB[slot][:, k, fc * 128:(fc + 1) * 128], SILUT[:, k, :],
                           k == 0 and fc == 0, False, ["SILUT", ("WB", slot)], [("ps", b)], sgc=True)
                    mm(PSB[b][:, fc * 2:fc * 2 + 2], bb[0:1, fc * 128:(fc + 1) * 128],
                       ONESB[0:1, 0:2], False, True, ["ONESB", ("BADA", pi % 2)], [("ps", b)], sgc=True)
                    if fc < 3:
                        yield
                rel(slot)
                c0 = mi * 8 + (pi % 2) * 4
                cp("dve", MODT[:, par, c0:c0 + 4, :].rearrange("p a b -> p (a b)"), PSB[b][:, 0:8], [("ps", b)], [("MODT", par)])
                yield
            for n in range(2):
                for s_ in range(2):
                    stt("dve", AB[:, par, n, 0, :, s_], MODT[:, par, (2 * n + 1) * 8:(2 * n + 2) * 8, s_], 1.0,
                        COLS[:, r + 8 * n:r + 8 * n + 8], ALU.add, ALU.mult, [("MODT", par), "COLS"], [("AB", par)])
                    cp("dve", AB[:, par, n, 1, :, s_], MODT[:, par, (2 * n) * 8:(2 * n + 1) * 8, s_], [("MODT", par)], [("AB", par)])
            yield

        def small_setup_gen(l):
            r = l * 24
            dma("pool", CPW[:], conv_pw_d[l].rearrange("(k p) c -> p k c", p=128), [], ["CPW"])
            memset("dve", PBD[:], 0.0, ["PBD"])
            for g in range(4):
                c, gl = g // 2, g % 2
                dma("pool", PBD[gl * 64:(gl + 1) * 64, c, gl * 64:(gl + 1) * 64], pool_w_d[l, g], [], ["PBD"])
            dma("sp", DWROW[0:31, :], conv_dw_d[l], [], ["DWROW"])
            for c in range(2):
                b = nbank()
                tr(PSB[b][:, 0:32], DWROW[:, c * 128:(c + 1) * 128], IDF[0:32, 0:32], ["DWROW", "IDF"], [("ps", b)])
                cp("dve", DWT[:, c, :], PSB[b][:, 0:32], [("ps", b)], ["DWT"])
                tt("dve", DG[:, c], IDB[:].unsqueeze(1).to_broadcast([128, 31, 128]),
                   DWT[:, c, 0:31].unsqueeze(2).to_broadcast([128, 31, 128]), ALU.mult, ["IDB", "DWT"], ["DG"])
            yield
            dma("sp", WSR, gm_ws_d[l].rearrange("g p q -> p g q"), [], ["PSA"])
            for g in range(4):
                b = nbank()
                tr(PSB[b][:, 0:128], WSR[:, g, :], IDF[:], ["PSA", "IDF"], [("ps", b)])
                cp("dve", WST[:, g, :], PSB[b][:, 0:128], [("ps", b)], ["WST"])
            for g in range(4):
                c, gl = g // 2, g % 2
                dma("sp", GBT[gl * 64:(gl + 1) * 64, c, :], gm_b_d[l, g:g + 1, :].partition_broadcast(64), [], ["GBT"])
            yield
            dma("sp", CKR, ck_d[l].rearrange("(c s) f -> s c f", s=128), [], ["GV"])
            for c in range(2):
                cp("dve", CKD[:, c], CKR[:, c, :].rearrange("s (g d) -> s g d", g=2).unsqueeze(2).to_broadcast([128, 2, 2, 64]),
                   ["GV"], ["CKD"])
            for c in range(2):
                for g in range(2):
                    b = nbank()
                    pv = psv16(b)
                    tr(pv[:, 0:128], CKD[:, c, g].rearrange("s j d -> s (j d)"), IDB[:], ["CKD", "IDB"], [("ps", b)])
                    cp("dve", KTC[:, g, c * 128:(c + 1) * 128], pv[:, 0:128], [("ps", b)], ["KTC"])
            dma("sp", CKR, cv_d[l].rearrange("(c s) f -> s c f", s=128), [], ["GV"])
            for c in range(2):
                cp("dve", VVC[:, c, :].rearrange("s (g j d) -> s g j d", g=2, j=2)[:, :, 0, :],
                   CKR[:, c, :].rearrange("s (g d) -> s g d", g=2),
                   ["GV"], ["VVC"])
            dma("sp", SINK[:], sink_d[l:l + 1, :].partition_broadcast(128), [], ["SINK"])
            act(SINK[:], SINK[:], AF.Exp, ["SINK"], ["SINK"])
            yield

        def xn_transposes(xn, xk, dst_fn, dkey, n, s, par, par_l):
            b = nbank()
            pv = psv16(b)
            for k in range(8):
                tr(pv[:, k * 128:(k + 1) * 128], xn[:, k * 128:(k + 1) * 128], IDB[:], [xk, "IDB"], [("ps", b)])
            for k in range(8):
                if True:
                    ts("dve", dst_fn(k), pv[:, k * 128:(k + 1) * 128],
                       AB[:, par_l, n, 0, k, s:s + 1], AB[:, par_l, n, 1, k, s:s + 1], ALU.mult, ALU.add,
                       [("ps", b), ("AB", par_l)], [dkey])
                else:
                    act(dst_fn(k), pv[:, k * 128:(k + 1) * 128], AF.Identity, [("ps", b), ("AB", par_l)], [dkey],
                        scale=AB[:, par_l, n, 0, k, s:s + 1], bias=AB[:, par_l, n, 1, k, s:s + 1])

        def norm1_block(e, slot, s, par_l):
            xsk = ("XS", slot)
            act(JUNK, XS[:, slot, :], AF.Square, [xsk], [("CZ", 0), ("CZ", 1), ("SS", e)], accum_out=SS[:, e:e + 1])
            act(RSTD[:, e:e + 1], SS[:, e:e + 1], AF.Sqrt, [("SS", e), "EPSC"], [("RSTD", e)], scale=1.0 / D, bias=EPSC[:, 0:1])
            recip(RSTD[:, e:e + 1], ("RSTD", e))
            xn, xk = XN[e % 2], ("XN", e % 2)
            act(xn[:], XS[:, slot, :], AF.Copy, [xsk, ("RSTD", e)], [xk], scale=RSTD[:, e:e + 1])
            yield
            xn_transposes(xn, xk, lambda k: XNT1[:, k, e * 128:(e + 1) * 128], ("XNT", e), 0, s, e % 2, par_l)
            yield

        def norm2_blocks(s, par_l):
            for i in range(4):
                act(JUNK, X[:, i, :], AF.Square, [("X", i + 1)], [("CZ", 0), ("CZ", 1), ("SS2", i + 1)], accum_out=SS2[:, i:i + 1])
            act(RSTD2[:], SS2[:], AF.Sqrt, [("SS2", i + 1) for i in range(4)] + ["EPSC"], ["RSTD2"], scale=1.0 / D, bias=EPSC[:, 0:1])
            recip(RSTD2[:], "RSTD2")
            yield
            for i in range(4):
                xn, xk = XN[i % 2], ("XN", i % 2)
                act(xn[:], X[:, i, :], AF.Copy, [("X", i + 1), "RSTD2"], [xk], scale=RSTD2[:, i:i + 1])
                yield
                xn_transposes(xn, xk, lambda k: XNT2[:, k, i * 128:(i + 1) * 128], ("XNT2", i), 1, s, i % 2, par_l)
                yield

        def tile_geom(kind, ti):
            if kind == "p":
                return ti * 512, 128, 640, [(128, 256), (384, 256)]
            return NP_TOK + ti * 512, (0 if ti > 0 else 128), (768 if ti < 7 else 640), [(128, 512)]

        def tile_id(kind, ti):
            return ti if kind == "p" else 2 + ti

        def src_rows(l, r0, n):
            if l == 0:
                if r0 < NP_TOK:
                    return xp_d[r0:r0 + n, :]
                return xs_d[r0 - NP_TOK:r0 - NP_TOK + n, :]
            return rs_d[(l - 1) % 2][r0:r0 + n, :]

        def rs_key(l, tid, e):
            if l == 0:
                return []
            if e == 0:
                return [("rs", (l - 1) % 2, tid - 1, 4)]
            if e == 5:
                return [("rs", (l - 1) % 2, tid + 1, 1)]
            return [("rs", (l - 1) % 2, tid, e)]

        xs_rr = [0]

        def phases_A(l, kind, ti):
            s = 0 if kind == "p" else 1
            r = l * 24
            row0, lo, hi, segs = tile_geom(kind, ti)
            eblocks = list(range(lo // 128, hi // 128))
            cblocks = [1, 2, 3, 4]
            gap = lambda s0: 32 if (kind == "p" and s0 == 384) else 0
            colof = lambda tok, s0: tok - 96 + gap(s0)
            tid = tile_id(kind, ti)
            xnt_r = [("XNT", e) for e in eblocks]

            def fm_proj(wslot, lhs_fn, t_lo, t_hi):
                b = nbank()
                for k in range(8):
                    mm(PSB[b][:, 0:t_hi - t_lo], lhs_fn(k), XNT1[:, k, t_lo:t_hi], k == 0, k == 7,
                       [("WB", wslot)] + xnt_r, [("ps", b)])
                return b

            def rope_t1t2(bp, n, a, bnd):
                tt("dve", T1[:, 0:n], PSB[bp][:, 0:n], ROPE[:, 0, a:bnd], ALU.mult, [("ps", bp), "ROPE"], ["T1"])
                for (o, i) in [(0, 32), (32, 0), (64, 96), (96, 64)]:
                    tt("dve", T2[o:o + 32, 0:n], PSB[bp][i:i + 32, 0:n], ROPE[o:o + 32, 1, a:bnd], ALU.mult,
                       [("ps", bp), "ROPE"], ["T2"])

            def ph_norm1():
                if kind == "s":
                    t0 = ti * 512 - 128 + lo
                    dma("sp", ROPE[:, :, lo:hi], rope_d[:, :, t0:t0 + hi - lo].rearrange("t p n -> p t n"), [], ["ROPE"])
                vi = 0 if kind == "s" and 0 < ti < 7 else (1 if kind == "s" and ti == 0 else (2 if kind == "s" else 3))
                dma("sp", INVC[:], invc_d[vi], [], ["INVC"])
                for e in eblocks:
                    slot = xs_rr[0] % 2
                    xs_rr[0] += 1
                    dma("sp", XS[:, slot, :], src_rows(l, row0 - 128 + e * 128, 128), rs_key(l, tid, e), [("XS", slot)])
                    yield from norm1_block(e, slot, s, l % 2)

            def ph_proj():
                w0 = next_piece("w_in", l, 0, 0, 512)
                W0 = WB[w0]
                for g in range(2):
                    bq = fm_proj(w0, lambda k: W0[:, k, g * 128:(g + 1) * 128], 128, 640)
                    if kind == "s":
                        rope_t1t2(bq, 512, 128, 640)
                        for j in range(2):
                            tt("pool", QZ[j * 64:(j + 1) * 64, g, :, j * 128:(j + 1) * 128],
                               T1[j * 64:(j + 1) * 64, :].rearrange("p (i q) -> p i q", i=4),
                               T2[j * 64:(j + 1) * 64, :].rearrange("p (i q) -> p i q", i=4), ALU.add,
                               ["T1", "T2"], [("QZ", g)])
                    else:
                        for j in range(2):
                            cp("act", QZ[j * 64:(j + 1) * 64, g, :, j * 128:(j + 1) * 128],
                               PSB[bq][j * 64:(j + 1) * 64, :].rearrange("p (i q) -> p i q", i=4),
                               [("ps", bq)], [("QZ", g)])
                    yield
                kranges = [(lo, 384), (384, hi)]
                for (a, bnd) in kranges:
                    n = bnd - a
                    bk = fm_proj(w0, lambda k: W0[:, k, 256:384], a, bnd)
                    if kind == "s":
                        rope_t1t2(bk, n, a, bnd)
                        for g in range(2):
                            for j in range(2):
                                tt("pool", KT[j * 64:(j + 1) * 64, g, a:bnd], T1[g * 64:(g + 1) * 64, 0:n],
                                   T2[g * 64:(g + 1) * 64, 0:n], ALU.add, ["T1", "T2"], [("KT", g)])
                    else:
                        for g in range(2):
                            for j in range(2):
                                cp("act", KT[j * 64:(j + 1) * 64, g, a:bnd], PSB[bk][g * 64:(g + 1) * 64, 0:n],
                                   [("ps", bk)], [("KT", g)])
                    yield
                for e in eblocks:
                    b = nbank()
                    for k in range(8):
                        mm(PSB[b][:, 0:256], XNT1[:, k, e * 128:(e + 1) * 128], W0[:, k, 256:512], k == 0, k == 7,
                           [("WB", w0), ("XNT", e)], [("ps", b)])
                    cp("dve", VV[:, e, :].rearrange("s (g j d) -> s g j d", g=2, j=2)[:, :, 0, :],
                       PSB[b][:, 128:256].rearrange("s (g d) -> s g d", g=2),
                       [("ps", b)], [("VV", e)])
                    if kind == "p":
                        cp("act", KVO[:, e - 1, :], PSB[b][:, 0:256], [("ps", b)], [("CACC", 0), ("CACC", 1)])
                    if e % 2 == 0:
                        yield
                rel(w0)
                if kind == "p":
                    for sq in range(2):
                        seq = ti * 2 + sq
                        dma("sp", nk_d[seq, l].rearrange("(c s) f -> s c f", s=128), KVO[:, 2 * sq:2 * sq + 2, 0:128],
                            [("CACC", 0), ("CACC", 1)], [])
                        dma("sp", nv_d[seq, l].rearrange("(c s) f -> s c f", s=128), KVO[:, 2 * sq:2 * sq + 2, 128:256],
                            [("CACC", 0), ("CACC", 1)], [])

                yield
                w1 = next_piece("w_in", l, 0, 512, 512)
                w2 = next_piece("w_in", l, 0, 1024, 512)
                hranges = [(max(lo, 112), 384, 128), (384, min(hi, 656), 384 if kind == "p" else 128)]
                for c in range(2):
                    for (a, bnd, s0) in hranges:
                        n = bnd - a
                        bx = fm_proj(w1, lambda k: WB[w1][:, k, c * 128:(c + 1) * 128], a, bnd)
                        cp("dve", XP[:, c, colof(a, s0):colof(bnd, s0)], PSB[bx][:, 0:n], [("ps", bx)], [("XP", c)])
                    yield
                for c in range(2):
                    for (a, bnd, s0) in hranges:
                        n = bnd - a
                        bg = fm_proj(w2, lambda k: WB[w2][:, k, c * 128:(c + 1) * 128], a, bnd)
                        act(SG[:, 0:n], PSB[bg][:, 0:n], AF.Sigmoid, [("ps", bg)], ["T1"])
                        ba = fm_proj(w1, lambda k: WB[w1][:, k, 256 + c * 128:256 + (c + 1) * 128], a, bnd)
                        tt("dve", U[:, c, colof(a, s0):colof(bnd, s0)], PSB[ba][:, 0:n], SG[:, 0:n], ALU.mult,
                           [("ps", ba), "T1"], [("U", c)])
                        yield
                rel(w1)
                for c in range(2):
                    bu = fm_proj(w2, lambda k: WB[w2][:, k, 256 + c * 128:256 + (c + 1) * 128], 128, 640)
                    act(GU[:, c, :], PSB[bu][:], AF.Gelu, [("ps", bu)], [("GU", c)])
                    yield
                rel(w2)
                w3 = next_piece("w_in", l, 0, 1536, 256)
                for i, e in enumerate(cblocks):
                    b = nbank()
                    for k in range(8):
                        mm(PSB[b][:, 0:256], XNT1[:, k, e * 128:(e + 1) * 128], WB[w3][:, k, 0:256], k == 0, k == 7,
                           [("WB", w3), ("XNT", e)], [("ps", b)])
                    act(GV[:], PSB[b][:, 0:256], AF.Gelu, [("ps", b)], ["GV"])
                    act(JUNK[:, 0:256], GV[:], AF.Square, ["GV"], [("CZ", 0), ("CZ", 1), "GSS"], accum_out=SS[:, 6:7])
                    act(RSTD[:, 6:7], SS[:, 6:7], AF.Sqrt, ["GSS", "EPSC"], ["GRS"], scale=1.0 / 256, bias=EPSC[:, 0:1])
                    recip(RSTD[:, 6:7], "GRS")
                    for c in range(2):
                        for gl in range(2):
                            ts("dve", GVZ[:, i, c, gl, gl * 64:(gl + 1) * 64], GV[:, c * 128 + gl * 64:c * 128 + (gl + 1) * 64],
                               RSTD[:, 6:7], None, ALU.mult, None, ["GV", "GRS"], [("GVZ", i)])
                    if i % 2 == 1:
                        yield
                rel(w3)
                yield

            def ph_mix():
                xpk = [("XP", 0), ("XP", 1)]
                uk = [("U", 0), ("U", 1)]
                for (s0, sl) in segs:
                    c0 = colof(s0, s0)
                    if kind == "p" or lo == 128:
                        memset("dve", XP[:, :, c0 - 16:c0], 0.0, xpk)
                        memset("dve", U[:, :, c0 - 16:c0], 0.0, uk)
                    if kind == "p" or hi == 640:
                        memset("dve", XP[:, :, c0 + sl:c0 + sl + 16], 0.0, xpk)
                        memset("dve", U[:, :, c0 + sl:c0 + sl + 16], 0.0, uk)

                def pool_chain(c):
                    for (s0, sl) in segs:
                        c0 = colof(s0, s0)
                        o = s0 - 128
                        A_, B_ = c0 - 8, c0 + sl + 8
                        tt("dve", PSA[:, A_:B_], XP[:, c, A_ - 1:B_ - 1], XP[:, c, A_:B_], ALU.add, xpk, ["PSA"])
                        if c == 0:
                            cp("dve", PSBF[0:64, c0:c0 + sl], PSA[0:64, c0:c0 + sl], ["PSA"], ["PSBF"])
                            tt("dve", PSBF[64:128, c0:c0 + sl], PSA[64:128, c0 - 1:c0 + sl - 1], PSA[64:128, c0 + 1:c0 + sl + 1],
                               ALU.add, ["PSA"], ["PSBF"])
                        else:
                            A_, B_ = c0 - 6, c0 + sl + 6
                            tt("dve", PSBF[:, A_:B_], PSA[:, A_ - 1:B_ - 1], PSA[:, A_ + 1:B_ + 1], ALU.add, ["PSA"], ["PSBF"])
                            A_, B_ = c0 - 4, c0 + sl + 4
                            tt("dve", PSA[:, A_:B_], PSBF[:, A_ - 2:B_ - 2], PSBF[:, A_ + 2:B_ + 2], ALU.add, ["PSBF"], ["PSA"])
                            cp("dve", PSBF[0:64, c0:c0 + sl], PSA[0:64, c0:c0 + sl], ["PSA"], ["PSBF"])
                            tt("dve", PSBF[64:128, c0:c0 + sl], PSA[64:128, c0 - 4:c0 + sl - 4], PSA[64:128, c0 + 4:c0 + sl + 4],
                               ALU.add, ["PSA"], ["PSBF"])
                        tt("dve", PSBF[:, c0:c0 + sl], PSBF[:, c0:c0 + sl], INVC[:, c, o:o + sl], ALU.mult, ["PSBF", "INVC"], ["PSBF"])
                        tt("dve", PM[:, c, o:o + sl], PSBF[:, c0:c0 + sl], XP[:, c, c0:c0 + sl], ALU.subtract, ["PSBF"] + xpk, [("PM", c)])

                def conv_mm(c):
                    b = nbank()
                    for (s0, sl) in segs:
                        o = s0 - 128
                        c0 = colof(s0, s0)
                        for j in range(31):
                            mm(PSB[b][:, o:o + sl], DG[:, c, j, :], U[:, c, c0 + j - 15:c0 + j - 15 + sl], j == 0, j == 30,
                               ["DG", ("U", c)], [("ps", b)])
                    act(CACC[:, c, :], PSB[b][:], AF.Identity, [("ps", b), "COLS"], [("CACC", c)],
                        bias=COLS[:, r + 18 + c:r + 19 + c])
                    act(CSQ[:, c, :], CACC[:, c, :], AF.Square, [("CACC", c)], [("CSQ", c)])

                def gmlp_c(c):
                    b = nbank()
                    for i in range(4):
                        for gl in range(2):
                            mm(PSB[b][:, i * 128:(i + 1) * 128], GVZ[:, i, c, gl, :], WST[:, c * 2 + gl, :],
                               gl == 0, gl == 1, [("GVZ", i), "WST"], [("ps", b)])
                    stt("dve", SV[:].rearrange("p (i q) -> p i q", i=4), PSB[b][:].rearrange("p (i q) -> p i q", i=4),
                        COLS[:, r + 22 + c:r + 23 + c], GBT[:, c, :].unsqueeze(1).to_broadcast([128, 4, 128]),
                        ALU.mult, ALU.add, [("ps", b), "COLS", "GBT"], ["T1"])
                    tt("dve", MIXT[:, 6 + c, :], GU[:, c, :], SV[:], ALU.mult, [("GU", c), "T1"], [("MIXT", 6 + c)])

                pool_chain(0)
                conv_mm(0)
                yield
                pool_chain(1)
                conv_mm(1)
                yield
                gmlp_c(0)
                yield
                for c in range(2):
                    b = nbank()
                    mm(PSB[b][:], PBD[:, c, :], PM[:, c, :], True, True, ["PBD", ("PM", c)], [("ps", b)])
                    act(MIXT[:, 2 + c, :], PSB[b][:], AF.Identity, [("ps", b), "COLS"], [("MIXT", 2 + c)],
                        scale=COLS[:, r + 16 + c:r + 17 + c])
                b = nbank()
                for c in range(2):
                    mm(PSB[b][:], ONESB[:], CSQ[:, c, :], c == 0, c == 1, ["ONESB", ("CSQ", c)], [("ps", b)])
                act(CR[:], PSB[b][:], AF.Sqrt, [("ps", b), "EPSC"], ["T2"], scale=1.0 / 256, bias=EPSC[:, 0:1])
                recip(CR[:], "T2")
                for c in range(2):
                    tt("dve", CACC[:, c, :], CACC[:, c, :], CR[:], ALU.mult, [("CACC", c), "T2"], [("CACC", c)])
                    act(CZ[:, c, :], CACC[:, c, :], AF.Silu, [("CACC", c), "COLS"], [("CZ", c)],
                        scale=COLS[:, r + 20 + c:r + 21 + c])
                yield
                gmlp_c(1)
                yield
                for m in range(2):
                    b = nbank()
                    for k in range(2):
                        mm(PSB[b][:], CPW[:, k, m * 128:(m + 1) * 128], CZ[:, k, :], k == 0, k == 1,
                           ["CPW", ("CZ", k)], [("ps", b)])
                    cp("act", MIXT[:, 4 + m, :], PSB[b][:], [("ps", b)], [("MIXT", 4 + m)])
                yield

            def ph_attn():
                for i, e in enumerate(cblocks):
                    if kind == "s":
                        chunks = []
                        if (e - 1) * 128 >= lo:
                            chunks.append(("loc", e - 1, 0))
                        chunks.append(("loc", e, None))
                        if (e + 1) * 128 < hi:
                            chunks.append(("loc", e + 1, 1))
                        chunks += [("ctx", 0, None), ("ctx", 1, None)]
                    else:
                        s0 = 1 if e <= 2 else 3
                        chunks = [("loc", s0, None), ("loc", s0 + 1, None)]
                    nch = len(chunks)
                    for g in range(2):
                        b5 = 7
                        for ci, (ck, idx, mk) in enumerate(chunks):
                            pb = 4 + ci // 2 if ci < 4 else b5
                            col = (ci % 2) * 256
                            if ck == "loc":
                                lhs, rd = KT[:, g, idx * 128:(idx + 1) * 128], [("KT", g)]
                            else:
                                lhs, rd = KTC[:, g, idx * 128:(idx + 1) * 128], ["KTC"]
                            mm(PSB[pb][:, col:col + 256], lhs, QZ[:, g, i, :], True, True, rd + [("QZ", g)], [("ps", pb)])
                        for ci, (ck, idx, mk) in enumerate(chunks):
                            pb = 4 + ci // 2 if ci < 4 else b5
                            col = (ci % 2) * 256
                            act(PT[:, ci, :], PSB[pb][:, col:col + 256], AF.Exp, [("ps", pb)], [("PT", ci)], scale=0.125)
                            if mk is not None:
                                tt("pool", PT[:, ci, :], PT[:, ci, :], MASK[:, mk, :], ALU.mult, [("PT", ci), "MASK"], [("PT", ci)])
                        yield
                        for ci, (ck, idx, mk) in enumerate(chunks):
                            if ck == "loc":
                                lv, rd = VV[:, idx, g * 128:(g + 1) * 128], [("VV", idx)]
                            else:
                                lv, rd = VVC[:, idx, g * 128:(g + 1) * 128], ["VVC"]
                            mm(PSB[6][:, 0:256], lv, PT[:, ci, :], ci == 0, ci == nch - 1, rd + [("PT", ci)], [("ps", 6)])
                        snk = lambda p0: SINK[p0:p0 + 64, 2 * g:2 * g + 2].unsqueeze(2).to_broadcast([64, 2, 128])
                        for p0 in (0, 64):
                            tt("dve", DEN[p0:p0 + 64, :].rearrange("p (j q) -> p j q", j=2),
                               PSB[6][64:128, 0:256].rearrange("p (j q) -> p j q", j=2), snk(p0), ALU.add,
                               [("ps", 6), "SINK"], ["DEN"])
                        recip(DEN[:], "DEN")
                        for j in range(2):
                            tt("dve", MIXT[j * 64:(j + 1) * 64, g, i * 128:(i + 1) * 128],
                               PSB[6][0:64, j * 128:(j + 1) * 128],
                               DEN[j * 64:(j + 1) * 64, j * 128:(j + 1) * 128], ALU.mult,
                               [("ps", 6), "DEN"], [("MIXT", g)])
                        yield


            return [ph_norm1(), ph_proj(), ph_mix(), ph_attn()]

        def phases_B(l, kind, ti):
            s = 0 if kind == "p" else 1
            last = (l == N_LAYERS - 1)
            r = l * 24
            row0, lo, hi, segs = tile_geom(kind, ti)
            cblocks = [1, 2, 3, 4]
            tid = tile_id(kind, ti)
            tcount = [0]

            def resid_update(b, e, hf, gi):
                t = TMPO[tcount[0] % 2]
                tk = ("TMPO", tcount[0] % 2)
                tcount[0] += 1
                tt("dve", t[:], PSB[b][:], GBC[:, gi, hf * 512:(hf + 1) * 512], ALU.mult,
                   [("ps", b), "GBC"], [tk])
                tt("pool" if tcount[0] % 2 == 1 else "dve", X[:, e - 1, hf * 512:(hf + 1) * 512], X[:, e - 1, hf * 512:(hf + 1) * 512], t[:], ALU.add,
                   [tk, ("X", e)], [("X", e)])

            def ph_wout():
                for e in cblocks:
                    dma("sp", X[:, e - 1, :], src_rows(l, row0 + (e - 1) * 128, 128), rs_key(l, tid, e), [("X", e)])
                if (kind, ti) == TILES[0] or [(kind, ti)] == [t for t in TILES if t[0] == "s"][:1]:
                    gates_for_stream(l, s, (4, 5, 10, 11))
                for hf in range(2):
                    wo = next_piece("w_out", l, 0, hf * 512, 512)
                    for i, e in enumerate(cblocks):
                        b = nbank()
                        for k in range(8):
                            mm(PSB[b][:], MIXT[:, k, i * 128:(i + 1) * 128], WB[wo][:, k, :], k == 0, k == 7,
                               [("WB", wo), ("MIXT", k)], [("ps", b)])
                        resid_update(b, e, hf, 0)
                        if i == 3:
                            rel(wo)
                        yield
                yield from norm2_blocks(s, l % 2)

            def ph_mlp1(hh):
                for pi in range(4):
                    w = next_piece("w_mlp1", l, 0, (hh * 4 + pi) * 512, 512)
                    for fc in range(4):
                        hc = pi * 4 + fc
                        b = nbank()
                        for k in range(8):
                            mm(PSB[b][:], WB[w][:, k, fc * 128:(fc + 1) * 128], XNT2[:, k, :], k == 0, k == 7,
                               [("WB", w)] + [("XNT2", i_) for i_ in range(4)], [("ps", b)])
                        rl = RELU[hc % 2]
                        rk = ("RELU", hc % 2)
                        act(rl[:], PSB[b][:], AF.Relu, [("ps", b)], [rk])
                        tt("dve", HID[:, hc, :], rl[:], rl[:], ALU.mult, [rk], [("HID", hc)])
                        if fc == 3:
                            rel(w)
                        yield

            def ph_mlp2(hh):
                for hf in range(2):
                    for pc in range(2):
                        w = next_piece("w_mlp2", l, (hh * 2 + pc) * 1024, hf * 512, 512)
                        for i in range(4):
                            for kk in range(8):
                                hc = pc * 8 + kk
                                mm(PSB[4 + i][:], HID[:, hc, i * 128:(i + 1) * 128], WB[w][:, kk, :],
                                   pc == 0 and kk == 0, pc == 1 and kk == 7, [("WB", w), ("HID", hc)], [("ps", 4 + i)])
                            if i == 3:
                                rel(w)
                            yield
                    for i, e in enumerate(cblocks):
                        resid_update(4 + i, e, hf, 1)
                if hh == 1:
                    if not last:
                        for e in cblocks:
                            dma("sp", rs_d[l % 2][row0 + (e - 1) * 128:row0 + e * 128, :], X[:, e - 1, :],
                                [("X", e)], [("rs", l % 2, tile_id(kind, ti), e)])
                    else:
                        for e in cblocks:
                            act(JUNK, X[:, e - 1, :], AF.Square, [("X", e)], [("CZ", 0), ("CZ", 1), ("SS2", e)], accum_out=SS2[:, e - 1:e])
                        act(RSTD2[:], SS2[:], AF.Sqrt, [("SS2", e) for e in cblocks] + ["EPSC"], ["RSTD2"],
                            scale=1.0 / D, bias=EPSC[:, 0:1])
                        recip(RSTD2[:], "RSTD2")
                        dma("sp", FNBC, fnorm_d.partition_broadcast(128), [], [("GU", 0), ("GU", 1)])
                        for e in cblocks:
                            stt("dve", X[:, e - 1, :], X[:, e - 1, :], RSTD2[:, e - 1:e], FNBC, ALU.mult, ALU.mult,
                                [("X", e), "RSTD2", ("GU", 0), ("GU", 1)], [("X", e)])
                            dst = yp_d if kind == "p" else ys_d
                            dma("sp", dst[ti * 512 + (e - 1) * 128:ti * 512 + e * 128, :], X[:, e - 1, :], [("X", e)], [])


            return [ph_wout(), ph_mlp1(0), ph_mlp2(0), ph_mlp1(1), ph_mlp2(1)]

        def interleave(*gens):
            live = [g for g in gens if g is not None]
            while live:
                for g in list(live):
                    try:
                        next(g)
                    except StopIteration:
                        live.remove(g)

        def conversions(l):
            out = []
            for name in ("w_in", "w_out", "w_mlp1", "w_mlp2"):
                cr = CV_ROWS[name]
                nrows = wshape[name][0]
                for j in range(nrows // cr):
                    def th(name=name, j=j, cr=cr):
                        src = wd[name][l, j * cr:(j + 1) * cr, :]
                        dst = wbf[name][l, j * cr:(j + 1) * cr, :]
                        S.dma("pool", lambda e: e.dma_start(out=dst, in_=src), writes=[("wbf", name, l, j)], cv=True)
                    out.append(th)
            return out

        def run_all():
            G = [(l, kind, ti) for l in range(N_LAYERS) for (kind, ti) in TILES]
            cvq = []
            nt = len(TILES)
            ng = len(G)
            interleave(ada_fm_gen(0))
            interleave(small_setup_gen(0))
            pa = {0: phases_A(*G[0])}
            for g in pa[0]:
                interleave(g)
            if ng > 1:
                pa[1] = phases_A(*G[1])
                interleave(pa[1][0])
            for m in range(ng):
                l, kind, ti = G[m]
                tpos = m % nt
                pb = phases_B(*G[m])
                nx = pa.get(m + 1)
                if m + 2 < ng:
                    pa[m + 2] = phases_A(*G[m + 2])
                nxt_l = l + 1 if l + 1 < N_LAYERS else None
                ada = ada_fm_gen(nxt_l) if (nxt_l is not None and tpos == max(nt - 4, 0)) else None
                small = small_setup_gen(nxt_l) if (nxt_l is not None and tpos == nt - 2) else None
                if nt == 1 and nxt_l is not None:
                    ada, small = ada_fm_gen(nxt_l), None
                if nxt_l is not None:
                    if tpos == 0:
                        cvq = conversions(nxt_l)
                    for _ in range(3 if tpos < nt - 1 else len(cvq)):
                        if cvq:
                            cvq.pop(0)()
                interleave(pb[0], nx[1] if nx else None)
                interleave(pb[1], nx[2] if nx else None)
                interleave(pb[2], ada)
                interleave(pb[3], nx[3] if nx else None)
                if nt == 1 and nxt_l is not None:
                    interleave(small_setup_gen(nxt_l))
                interleave(pb[4], small, pa[m + 2][0] if m + 2 < ng else None)

        try:
            run_all()
        except _Stop:
            pass
        S.wait_all("sp")
        S.emit(block)
    if record:
        return pieces
    return nc


def _constants():
    ident = np.eye(128, dtype=np.float32)
    t = np.arange(NS_TOK)
    row = (t // 64).astype(np.float32)
    col = (t % 64).astype(np.float32)
    inv = (10000.0 ** (-np.arange(16, dtype=np.float32) / 16)).astype(np.float32)
    ang = np.concatenate([row[:, None] * inv[None, :], col[:, None] * inv[None, :]], axis=1)
    cos = np.cos(ang).astype(np.float32).T
    sin = np.sin(ang).astype(np.float32).T
    cos64 = np.concatenate([cos, cos], 0)
    sin64 = np.concatenate([-sin, sin], 0)
    rope = np.stack([np.concatenate([cos64, cos64], 0), np.concatenate([sin64, sin64], 0)], 0).astype(np.float32)
    sizes = (2, 4, 8, 16)

    def inv_tab(n, t0, length):
        tt_ = np.arange(t0, t0 + length)
        out = np.zeros((128, 2, length), np.float32)
        for g, sz in enumerate(sizes):
            h = sz // 2
            lo = np.clip(tt_ - h, 0, n)
            hi = np.clip(tt_ + h, 0, n)
            c, gl = g // 2, g % 2
            out[gl * 64:(gl + 1) * 64, c, :] = (1.0 / (hi - lo).astype(np.float32))[None, :]
        return out

    invc = np.stack([
        inv_tab(4096, 512, 512),
        inv_tab(4096, 0, 512),
        inv_tab(4096, 4096 - 512, 512),
        np.concatenate([inv_tab(256, 0, 256), inv_tab(256, 0, 256)], axis=2),
    ], 0).astype(np.float32)
    sidx = np.arange(128)[:, None]
    qidx = np.arange(128)[None, :]
    mp = (sidx >= qidx).astype(np.float32)
    mn = (sidx <= qidx).astype(np.float32)
    masks = np.stack([np.concatenate([mp, mp], 1), np.concatenate([mn, mn], 1)], 0).astype(np.float32)
    return ident, rope, invc, masks


_NC_CACHE = {}


def kernel(x_prompt, x_sample, cache_k, cache_v, c, c_ctx, w_ada, b_ada, norm1, norm2,
           w_in, w_out, attn_sink, pool_w, pool_scale, conv_dw, conv_b, conv_norm, conv_pw,
           gm_norm, gm_ws, gm_b, w_mlp1, w_mlp2, final_norm):
    f = lambda a: np.ascontiguousarray(np.asarray(a, dtype=np.float32))
    if "nc" not in _NC_CACHE:
        _NC_CACHE["nc"] = build_program(build_program())
    nc = _NC_CACHE["nc"]
    ident, rope, invc, masks = _constants()
    shared = {
        "w_ada": f(w_ada), "w_in": f(w_in), "w_out": f(w_out), "w_mlp1": f(w_mlp1), "w_mlp2": f(w_mlp2),
        "b_ada": f(b_ada), "norm1": f(norm1), "norm2": f(norm2), "attn_sink": f(attn_sink),
        "pool_w": f(pool_w), "pool_scale": f(pool_scale), "conv_dw": f(conv_dw), "conv_b": f(conv_b),
        "conv_norm": f(conv_norm), "conv_pw": f(conv_pw), "gm_norm": f(gm_norm), "gm_ws": f(gm_ws),
        "gm_b": f(gm_b), "final_norm": f(final_norm).reshape(1, D),
        "ident": ident, "rope": rope, "invcnt": invc, "masks": masks,
    }
    x_prompt = f(x_prompt)
    x_sample = f(x_sample)
    cache_k = f(cache_k)
    cache_v = f(cache_v)
    c = f(c)
    c_ctx = f(c_ctx)
    in_maps = []
    for i in range(N_CORES):
        m = dict(shared)
        m["xs"] = x_sample[i]
        m["xp"] = x_prompt[4 * i:4 * i + 4].reshape(NP_TOK, D)
        m["ck"] = cache_k[i].reshape(DEPTH, 256, 128)
        m["cv"] = cache_v[i].reshape(DEPTH, 256, 128)
        m["cc"] = np.stack([c_ctx, c[i]], 0)
        in_maps.append(m)
    res = run_bass_kernel_spmd(nc, in_maps, core_ids=list(range(N_CORES)))
    rr = res.results
    y_prompt = np.concatenate([r["yp"].reshape(4, 256, D) for r in rr], 0)
    y_sample = np.stack([r["ys"] for r in rr], 0)
    new_k = np.concatenate([r["nk"].reshape(4, DEPTH, 256, 2, 64) for r in rr], 0)
    new_v = np.concatenate([r["nv"].reshape(4, DEPTH, 256, 2, 64) for r in rr], 0)
    return (y_prompt.astype(np.float32), y_sample.astype(np.float32),
            new_k.astype(np.float32), new_v.astype(np.float32))
```

```python
from contextlib import ExitStack
import numpy as np
import concourse.bass as bass
import concourse.mybir as mybir
from concourse.bass_utils import run_bass_kernel_spmd

F32 = mybir.dt.float32
BF16 = mybir.dt.bfloat16
AF = mybir.ActivationFunctionType
ALU = mybir.AluOpType

D = 1024
DEPTH = 4
NP_TOK = 1024
NS_TOK = 4096
NTOK = NP_TOK + NS_TOK
IN_W = 1792
EPS = 1e-6
N_CORES = 8

SAME_ENGINE_SYNC = True
N_DMA_SEMS = 24
STRICT_KEYS = {("CZ", 0), ("CZ", 1)}
N_CV_SEMS = 6
NW = 4


class Sched:
    ENGS = ("pe", "act", "dve", "pool", "sp")

    def __init__(self, nc, stack):
        self.nc = nc
        self.prog = {e: [] for e in self.ENGS}
        self.cnt = {e: 0 for e in self.ENGS}
        self.sem = {e: stack.enter_context(nc.semaphore("s_" + e)) for e in self.ENGS}
        self.dsem = [stack.enter_context(nc.semaphore("d%d" % i)) for i in range(N_DMA_SEMS + N_CV_SEMS)]
        self.dcnt = [0] * (N_DMA_SEMS + N_CV_SEMS)
        self.dnext = {"hw": 0, "sw": 0, "cv": 0}
        self.waited = {}
        self.lastw = {}
        self.readers = {}

    def _need(self, eng, ev):
        if ev is None:
            return
        sk, val, prod = ev
        if prod == eng and (eng == "pe" or not SAME_ENGINE_SYNC):
            return
        if self.waited.get((eng, sk), 0) >= val:
            return
        self.waited[(eng, sk)] = val
        semh = self.sem[sk] if isinstance(sk, str) else self.dsem[sk]
        self.prog[eng].append(("wait", semh, val))

    def _deps(self, eng, reads, writes, is_dma=False):
        for r in reads:
            self._need(eng, self.lastw.get(r))
            if isinstance(r, tuple) and r[0] == "ps":
                for ev in self.readers.get(r, ()):
                    if ev[2] != eng:
                        self._need(eng, ev)
        for w in writes:
            strict = is_dma or w in STRICT_KEYS
            ev = self.lastw.get(w)
            if ev is not None and (strict or ev[2] != eng):
                self._need(eng, ev)
            for ev in self.readers.get(w, ()):
                if strict or ev[2] != eng:
                    self._need(eng, ev)

    def _commit(self, ev, reads, writes):
        for r in reads:
            lst = self.readers.setdefault(r, [])
            lst[:] = [x for x in lst if x[0] != ev[0]]
            lst.append(ev)
        for w in writes:
            self.lastw[w] = ev
            self.readers[w] = []

    def op(self, eng, fn, reads=(), writes=()):
        self._deps(eng, reads, writes)
        self.cnt[eng] += 1
        ev = (eng, self.cnt[eng], eng)
        self.prog[eng].append(("op", fn, self.sem[eng], 1))
        self._commit(ev, reads, writes)
        return ev

    def dma(self, q, fn, reads=(), writes=(), cv=False):
        half = N_DMA_SEMS // 2
        if cv:
            i = N_DMA_SEMS + self.dnext["cv"]
            self.dnext["cv"] = (self.dnext["cv"] + 1) % N_CV_SEMS
        else:
            kq = "sw" if q == "pool" else "hw"
            i = self.dnext[kq] + (half if kq == "sw" else 0)
            self.dnext[kq] = (self.dnext[kq] + 1) % half
        if self.dcnt[i] > 0:
            self._need(q, (i, self.dcnt[i], None))
        self._deps(q, reads, writes, is_dma=True)
        self.dcnt[i] += 16
        ev = (i, self.dcnt[i], None)
        self.prog[q].append(("op", fn, self.dsem[i], 16))
        self._commit(ev, reads, writes)
        return ev

    def wait_all(self, eng):
        for i in range(N_DMA_SEMS + N_CV_SEMS):
            if self.dcnt[i]:
                self._need(eng, (i, self.dcnt[i], None))
        for e in self.ENGS:
            if e != eng and self.cnt[e]:
                self._need(eng, (e, self.cnt[e], e))

    def emit(self, block):
        mapping = {"pe": block.tensor, "act": block.scalar, "dve": block.vector,
                   "pool": block.gpsimd, "sp": block.sync}
        for e in self.ENGS:
            prog = self.prog[e]

            def body(engine, prog=prog):
                for it in prog:
                    if it[0] == "wait":
                        engine.wait_ge(it[1], it[2])
                    else:
                        it[1](engine).then_inc(it[2], it[3])

            mapping[e](body)


TILES = [("p", 0), ("p", 1)] + [("s", t) for t in range(8)]
N_LAYERS = DEPTH
STOP_AT = None


class _Stop(Exception):
    pass


def ckpt(name):
    if STOP_AT == name:
        raise _Stop()


def build_program(pieces=None):
    record = pieces is None
    if record:
        pieces = []
    nc = bass.Bass("TRN2", target_bir_lowering=False)
    dt_in = lambda name, shape: nc.dram_tensor(name, list(shape), F32, kind="ExternalInput").ap()
    dt_out = lambda name, shape: nc.dram_tensor(name, list(shape), F32, kind="ExternalOutput").ap()

    xs_d = dt_in("xs", [NS_TOK, D])
    xp_d = dt_in("xp", [NP_TOK, D])
    ck_d = dt_in("ck", [DEPTH, 256, 128])
    cv_d = dt_in("cv", [DEPTH, 256, 128])
    cc_d = dt_in("cc", [2, D])
    wd = {
        "w_ada": dt_in("w_ada", [DEPTH, D, 6 * D]),
        "w_in": dt_in("w_in", [DEPTH, D, IN_W]),
        "w_out": dt_in("w_out", [DEPTH, D, D]),
        "w_mlp1": dt_in("w_mlp1", [DEPTH, D, 4 * D]),
        "w_mlp2": dt_in("w_mlp2", [DEPTH, 4 * D, D]),
    }
    b_ada_d = dt_in("b_ada", [DEPTH, 6 * D])
    norm1_d = dt_in("norm1", [DEPTH, D])
    norm2_d = dt_in("norm2", [DEPTH, D])
    sink_d = dt_in("attn_sink", [DEPTH, 4])
    pool_w_d = dt_in("pool_w", [DEPTH, 4, 64, 64])
    pool_scale_d = dt_in("pool_scale", [DEPTH, 256])
    conv_dw_d = dt_in("conv_dw", [DEPTH, 31, 256])
    conv_b_d = dt_in("conv_b", [DEPTH, 256])
    conv_norm_d = dt_in("conv_norm", [DEPTH, 256])
    conv_pw_d = dt_in("conv_pw", [DEPTH, 256, 256])
    gm_norm_d = dt_in("gm_norm", [DEPTH, 256])
    gm_ws_d = dt_in("gm_ws", [DEPTH, 4, 128, 128])
    gm_b_d = dt_in("gm_b", [DEPTH, 4, 128])
    fnorm_d = dt_in("final_norm", [1, D])
    ident_d = dt_in("ident", [128, 128])
    rope_d = dt_in("rope", [2, 128, NS_TOK])
    invc_d = dt_in("invcnt", [4, 128, 2, 512])
    mask_d = dt_in("masks", [2, 128, 256])

    yp_d = dt_out("yp", [NP_TOK, D])
    ys_d = dt_out("ys", [NS_TOK, D])
    nk_d = dt_out("nk", [4, DEPTH, 256, 128])
    nv_d = dt_out("nv", [4, DEPTH, 256, 128])
    rs_d = [nc.dram_tensor("rs%d" % i, [NTOK, D], F32).ap() for i in range(2)]
    wshape = {"w_in": [D, IN_W], "w_out": [D, D], "w_mlp1": [D, 4 * D], "w_mlp2": [4 * D, D]}
    wbf = {k: nc.dram_tensor("bf_" + k, [DEPTH] + v, BF16).ap() for k, v in wshape.items()}
    CV_ROWS = {"w_in": 512, "w_out": 512, "w_mlp1": 256, "w_mlp2": 1024}

    with ExitStack() as st:
        S = Sched(nc, st)
        sb = lambda name, shape, dt=F32: st.enter_context(nc.sbuf_tensor(name, list(shape), dt))
        pst = lambda name, shape, dt=F32: st.enter_context(nc.psum_tensor(name, list(shape), dt))

        X = sb("X", [128, 4, D])
        XS = sb("XS", [128, 2, D])
        XN = [sb("XN%d" % i, [128, D], BF16) for i in range(2)]
        XNT1 = sb("XNT1", [128, 8, 768], BF16)
        XNT2 = sb("XNT2", [128, 8, 512], BF16)
        WB = [sb("WB%d" % i, [128, 8, 512], BF16) for i in range(NW)]
        HID = sb("HID", [128, 16, 512], BF16)
        RELU = [sb("RELU%d" % i, [128, 512], BF16) for i in range(2)]
        MIXT = sb("MIXT", [128, 8, 512], BF16)
        QZ = sb("QZ", [128, 2, 4, 256], BF16)
        KT = sb("KT", [128, 2, 768], BF16)
        VV = sb("VV", [128, 6, 256], BF16)
        ROPE = sb("ROPE", [128, 2, 768])
        T1 = sb("T1", [128, 512])
        T2 = sb("T2", [128, 512])
        PW = 608
        XP = sb("XP", [128, 2, PW])
        PSA = sb("PSA", [128, PW])
        PSBF = sb("PSBF", [128, PW])
        INVC = sb("INVC", [128, 2, 512])
        PM = sb("PM", [128, 2, 512], BF16)
        U = sb("U", [128, 2, PW], BF16)
        DG = sb("DG", [128, 2, 31, 128], BF16)
        CACC = sb("CACC", [128, 2, 512])
        CSQ = sb("CSQ", [128, 2, 512], BF16)
        CZ = sb("CZ", [128, 2, 512], BF16)
        GU = sb("GU", [128, 2, 512])
        GV = sb("GV", [128, 256])
        GVZ = sb("GVZ", [128, 4, 2, 2, 128], BF16)
        PT = sb("PT", [128, 5, 256], BF16)
        DEN = sb("DEN", [128, 256])
        TMPO = [sb("TMPO%d" % i, [128, 512]) for i in range(2)]
        SS = sb("SS", [128, 8])
        RSTD = sb("RSTD", [128, 8])
        SS2 = sb("SS2", [128, 4])
        RSTD2 = sb("RSTD2", [128, 4])
        JUNK = CZ[:].rearrange("p c n -> p (c n)")
        SG = T1
        CR = T2
        SV = T1
        KVO = CACC[:].rearrange("p c (h n) -> p (c h) n", h=2)
        IDF = sb("IDF", [128, 128])
        IDB = sb("IDB", [128, 128], BF16)
        ONESB = sb("ONESB", [128, 128], BF16)
        MASK = sb("MASK", [128, 2, 256], BF16)
        ROWS = sb("ROWS", [128, 128])
        COLS = sb("COLS", [128, 128])
        DWROW = sb("DWROW", [32, 256])
        DWT = sb("DWT", [128, 2, 32])
        EPSC = sb("EPSC", [128, 1])
        SILUT = sb("SILUT", [128, 8, 2], BF16)
        SILUBC = sb("SILUBC", [128, 8, 128], BF16)
        BADA = [sb("BADA%d" % i, [1, 512], BF16) for i in range(2)]
        MODT = sb("MODT", [128, 2, 32, 2])
        AB = sb("AB", [128, 2, 2, 2, 8, 2])
        GBC = sb("GBC", [128, 2, D])
        FNBC = GU[:].rearrange("p c n -> p (c n)")
        CPW = sb("CPW", [128, 2, 256], BF16)
        PBD = sb("PBD", [128, 2, 128], BF16)
        WSR = PSA[:, 0:512].rearrange("p (g q) -> p g q", g=4)
        WST = sb("WST", [128, 4, 128], BF16)
        GBT = sb("GBT", [128, 2, 128])
        CKR = GV[:].rearrange("p (c f) -> p c f", c=2)
        CKD = sb("CKD", [128, 2, 2, 2, 64], BF16)
        KTC = sb("KTC", [128, 2, 256], BF16)
        VVC = sb("VVC", [128, 2, 256], BF16)
        SINK = sb("SINK", [128, 4])

        PSB = [pst("PSB%d" % i, [128, 512]) for i in range(8)]

        block = st.enter_context(nc.Block())

        bank_rr = [0]

        def nbank():
            b = bank_rr[0]
            bank_rr[0] = (b + 1) % 4
            return b

        def psv16(b):
            return PSB[b][:].bitcast(BF16)

        wstate = {"issued": 0, "used": 0}

        held = {}

        def rel(slot):
            held.pop(slot, None)

        def issue_piece(i):
            name, l, r0, c0, ncol = pieces[i]
            slot = i % NW
            assert record or slot not in held, ("weight ring slot still in use", i, pieces[i], held)
            dst = WB[slot][:, :, 0:ncol]
            if name != "w_ada" and l >= 1:
                src = wbf[name][l, r0:r0 + 1024, c0:c0 + ncol].rearrange("(k p) c -> p k c", p=128)
                cr = CV_ROWS[name]
                rk = [("wbf", name, l, j) for j in range(r0 // cr, (r0 + 1024) // cr)]
                S.dma("sp", lambda e, dst=dst, src=src: e.dma_start(out=dst, in_=src),
                      reads=rk, writes=[("WB", slot)])
            else:
                src = wd[name][l, r0:r0 + 1024, c0:c0 + ncol].rearrange("(k p) c -> p k c", p=128)
                S.dma("pool", lambda e, dst=dst, src=src: e.dma_start(out=dst, in_=src),
                      writes=[("WB", slot)])

        def next_piece(name, l, r0, c0, ncol):
            i = wstate["used"]
            if record:
                pieces.append((name, l, r0, c0, ncol))
            assert pieces[i] == (name, l, r0, c0, ncol), (pieces[i], name, l, r0, c0, ncol)
            while wstate["issued"] < min(len(pieces), i + NW - 1):
                issue_piece(wstate["issued"])
                wstate["issued"] += 1
            wstate["used"] += 1
            held[i % NW] = pieces[i]
            return i % NW

        def mm(out, lhsT, rhs, start, stop, reads, writes, sgc=False):
            S.op("pe", lambda e: e.matmul(out, lhsT=lhsT, rhs=rhs, start=start, stop=stop, skip_group_check=sgc),
                 reads=reads, writes=writes)

        def tr(out, in_, ident, reads, writes):
            S.op("pe", lambda e: e.transpose(out, in_, ident), reads=reads, writes=writes)

        def act(out, in_, func, reads, writes, **kw):
            S.op("act", lambda e: e.activation(out=out, in_=in_, func=func, **kw), reads=reads, writes=writes)

        def ts(eng, out, in0, s1, s2, op0, op1, reads, writes):
            if s2 is None:
                S.op(eng, lambda e: e.tensor_scalar(out=out, in0=in0, scalar1=s1, scalar2=None, op0=op0),
                     reads=reads, writes=writes)
            else:
                S.op(eng, lambda e: e.tensor_scalar(out=out, in0=in0, scalar1=s1, scalar2=s2, op0=op0, op1=op1),
                     reads=reads, writes=writes)

        def tt(eng, out, in0, in1, op, reads, writes):
            S.op(eng, lambda e: e.tensor_tensor(out=out, in0=in0, in1=in1, op=op), reads=reads, writes=writes)

        def stt(eng, out, in0, scalar, in1, op0, op1, reads, writes):
            S.op(eng, lambda e: e.scalar_tensor_tensor(out=out, in0=in0, scalar=scalar, in1=in1, op0=op0, op1=op1),
                 reads=reads, writes=writes)

        def cp(eng, out, in_, reads, writes):
            if eng == "act":
                S.op("act", lambda e: e.copy(out=out, in_=in_), reads=reads, writes=writes)
            else:
                S.op(eng, lambda e: e.tensor_copy(out=out, in_=in_), reads=reads, writes=writes)

        def recip(ap, key):
            S.op("dve", lambda e: e.reciprocal(out=ap, in_=ap), reads=[key], writes=[key])

        def memset(eng, ap, val, writes):
            S.op(eng, lambda e: e.memset(ap, val), writes=writes)

        def dma(q, out, in_, reads, writes):
            S.dma(q, lambda e: e.dma_start(out=out, in_=in_), reads=reads, writes=writes)

        dma("sp", IDF[:], ident_d, [], ["IDF"])
        dma("pool", IDB[:], ident_d, [], ["IDB"])
        dma("pool", MASK[:], mask_d.rearrange("m p q -> p m q"), [], ["MASK"])
        memset("dve", ONESB[:], 1.0, ["ONESB"])
        memset("dve", EPSC[:], EPS, ["EPSC"])
        memset("dve", ROWS[:], 0.0, ["ROWS"])
        memset("pool", QZ[:], 0.0, [("QZ", 0), ("QZ", 1)])
        memset("pool", VV[:], 1.0, [("VV", e_) for e_ in range(6)])
        memset("pool", VVC[:], 1.0, ["VVC"])
        memset("pool", GVZ[:], 0.0, [("GVZ", i) for i in range(4)])
        memset("pool", XP[:], 0.0, [("XP", 0), ("XP", 1)])
        memset("pool", U[:], 0.0, [("U", 0), ("U", 1)])
        memset("dve", DWROW[:], 0.0, ["DWROW"])
        for l in range(DEPTH):
            r = l * 24
            dma("sp", ROWS[r:r + 8, :], norm1_d[l].rearrange("(c p) -> c p", p=128), [], ["ROWS"])
            dma("sp", ROWS[r + 8:r + 16, :], norm2_d[l].rearrange("(c p) -> c p", p=128), [], ["ROWS"])
            for j, v in enumerate([pool_scale_d, conv_b_d, conv_norm_d, gm_norm_d]):
                dma("sp", ROWS[r + 16 + 2 * j:r + 18 + 2 * j, :], v[l].rearrange("(c p) -> c p", p=128), [], ["ROWS"])
        dma("sp", ROWS[104:112, :], cc_d[0].rearrange("(c p) -> c p", p=128), [], ["ROWS"])
        dma("sp", ROWS[112:120, :], cc_d[1].rearrange("(c p) -> c p", p=128), [], ["ROWS"])
        b = nbank()
        tr(PSB[b][:, 0:128], ROWS[:], IDF[:], ["ROWS", "IDF"], [("ps", b)])
        cp("dve", COLS[:], PSB[b][:, 0:128], [("ps", b)], ["COLS"])
        for s in range(2):
            act(SILUT[:, :, s], COLS[:, 104 + 8 * s:112 + 8 * s], AF.Silu, ["COLS"], ["SILUT"])

        def gates_for_stream(l, s, piece_ids):
            cp("dve", SILUBC[:], SILUT[:, :, s].unsqueeze(2).to_broadcast([128, 8, 128]), ["SILUT"], ["SILUBC"])
            for n, pi in enumerate(piece_ids):
                slot = next_piece("w_ada", l, 0, pi * 512, 512)
                gi = 0 if pi < 6 else 1
                half = pi % 2
                bb = BADA[n % 2]
                dma("pool", bb[:], b_ada_d[l:l + 1, pi * 512:(pi + 1) * 512], [], [("BADA", n % 2)])
                b = nbank()
                for k in range(8):
                    mm(PSB[b][:], SILUBC[:, k, :], WB[slot][:, k, :], k == 0, False,
                       ["SILUBC", ("WB", slot)], [("ps", b)])
                mm(PSB[b][:], ONESB[0:1, :], bb[0:1, :], False, True, ["ONESB", ("BADA", n % 2)], [("ps", b)])
                cp("act", GBC[:, gi, half * 512:(half + 1) * 512], PSB[b][:], [("ps", b)], ["GBC"])
                rel(slot)

        def ada_fm_gen(l):
            r = l * 24
            par = l % 2
            for pi in (0, 1, 2, 3, 6, 7, 8, 9):
                kind = pi // 2
                slot = next_piece("w_ada", l, 0, pi * 512, 512)
                mi = {0: 0, 1: 1, 3: 2, 4: 3}[kind]
                bb = BADA[pi % 2]
                dma("pool", bb[:], b_ada_d[l:l + 1, pi * 512:(pi + 1) * 512], [], [("BADA", pi % 2)])
                b = nbank()
                for fc in range(4):
                    for k in range(8):
                        mm(PSB[b][:, fc * 2:fc * 2 + 2], WB[slot][:, k, fc * 128:(fc + 1) * 128], SILUT[:, k, :],
                           k == 0 and fc == 0, False, ["SILUT", ("WB", slot)], [("ps", b)], sgc=True)
                    mm(PSB[b][:, fc * 2:fc * 2 + 2], bb[0:1, fc * 128:(fc + 1) * 128],
                       ONESB[0:1, 0:2], False, True, ["ONESB", ("BADA", pi % 2)], [("ps", b)], sgc=True)
                    if fc < 3:
                        yield
                rel(slot)
                c0 = mi * 8 + (pi % 2) * 4
                cp("dve", MODT[:, par, c0:c0 + 4, :].rearrange("p a b -> p (a b)"), PSB[b][:, 0:8], [("ps", b)], [("MODT", par)])
                yield
            for n in range(2):
                for s_ in range(2):
                    stt("dve", AB[:, par, n, 0, :, s_], MODT[:, par, (2 * n + 1) * 8:(2 * n + 2) * 8, s_], 1.0,
                        COLS[:, r + 8 * n:r + 8 * n + 8], ALU.add, ALU.mult, [("MODT", par), "COLS"], [("AB", par)])
                    cp("dve", AB[:, par, n, 1, :, s_], MODT[:, par, (2 * n) * 8:(2 * n + 1) * 8, s_], [("MODT", par)], [("AB", par)])
            yield

        def small_setup_gen(l):
            r = l * 24
            dma("pool", CPW[:], conv_pw_d[l].rearrange("(k p) c -> p k c", p=128), [], ["CPW"])
            memset("dve", PBD[:], 0.0, ["PBD"])
            for g in range(4):
                c, gl = g // 2, g % 2
                dma("pool", PBD[gl * 64:(gl + 1) * 64, c, gl * 64:(gl + 1) * 64], pool_w_d[l, g], [], ["PBD"])
            dma("sp", DWROW[0:31, :], conv_dw_d[l], [], ["DWROW"])
            for c in range(2):
                b = nbank()
                tr(PSB[b][:, 0:32], DWROW[:, c * 128:(c + 1) * 128], IDF[0:32, 0:32], ["DWROW", "IDF"], [("ps", b)])
                cp("dve", DWT[:, c, :], PSB[b][:, 0:32], [("ps", b)], ["DWT"])
                tt("dve", DG[:, c], IDB[:].unsqueeze(1).to_broadcast([128, 31, 128]),
                   DWT[:, c, 0:31].unsqueeze(2).to_broadcast([128, 31, 128]), ALU.mult, ["IDB", "DWT"], ["DG"])
            yield
            dma("sp", WSR, gm_ws_d[l].rearrange("g p q -> p g q"), [], ["PSA"])
            for g in range(4):
                b = nbank()
                tr(PSB[b][:, 0:128], WSR[:, g, :], IDF[:], ["PSA", "IDF"], [("ps", b)])
                cp("dve", WST[:, g, :], PSB[b][:, 0:128], [("ps", b)], ["WST"])
            for g in range(4):
                c, gl = g // 2, g % 2
                dma("sp", GBT[gl * 64:(gl + 1) * 64, c, :], gm_b_d[l, g:g + 1, :].partition_broadcast(64), [], ["GBT"])
            yield
            dma("sp", CKR, ck_d[l].rearrange("(c s) f -> s c f", s=128), [], ["GV"])
            for c in range(2):
                cp("dve", CKD[:, c], CKR[:, c, :].rearrange("s (g d) -> s g d", g=2).unsqueeze(2).to_broadcast([128, 2, 2, 64]),
                   ["GV"], ["CKD"])
            for c in range(2):
                for g in range(2):
                    b = nbank()
                    pv = psv16(b)
                    tr(pv[:, 0:128], CKD[:, c, g].rearrange("s j d -> s (j d)"), IDB[:], ["CKD", "IDB"], [("ps", b)])
                    cp("dve", KTC[:, g, c * 128:(c + 1) * 128], pv[:, 0:128], [("ps", b)], ["KTC"])
            dma("sp", CKR, cv_d[l].rearrange("(c s) f -> s c f", s=128), [], ["GV"])
            for c in range(2):
                cp("dve", VVC[:, c, :].rearrange("s (g j d) -> s g j d", g=2, j=2)[:, :, 0, :],
                   CKR[:, c, :].rearrange("s (g d) -> s g d", g=2),
                   ["GV"], ["VVC"])
            dma("sp", SINK[:], sink_d[l:l + 1, :].partition_broadcast(128), [], ["SINK"])
            act(SINK[:], SINK[:], AF.Exp, ["SINK"], ["SINK"])
            yield

        def xn_transposes(xn, xk, dst_fn, dkey, n, s, par, par_l):
            b = nbank()
            pv = psv16(b)
            for k in range(8):
                tr(pv[:, k * 128:(k + 1) * 128], xn[:, k * 128:(k + 1) * 128], IDB[:], [xk, "IDB"], [("ps", b)])
            for k in range(8):
                if True:
                    ts("dve", dst_fn(k), pv[:, k * 128:(k + 1) * 128],
                       AB[:, par_l, n, 0, k, s:s + 1], AB[:, par_l, n, 1, k, s:s + 1], ALU.mult, ALU.add,
                       [("ps", b), ("AB", par_l)], [dkey])
                else:
                    act(dst_fn(k), pv[:, k * 128:(k + 1) * 128], AF.Identity, [("ps", b), ("AB", par_l)], [dkey],
                        scale=AB[:, par_l, n, 0, k, s:s + 1], bias=AB[:, par_l, n, 1, k, s:s + 1])

        def norm1_block(e, slot, s, par_l):
            xsk = ("XS", slot)
            act(JUNK, XS[:, slot, :], AF.Square, [xsk], [("CZ", 0), ("CZ", 1), ("SS", e)], accum_out=SS[:, e:e + 1])
            act(RSTD[:, e:e + 1], SS[:, e:e + 1], AF.Sqrt, [("SS", e), "EPSC"], [("RSTD", e)], scale=1.0 / D, bias=EPSC[:, 0:1])
            recip(RSTD[:, e:e + 1], ("RSTD", e))
            xn, xk = XN[e % 2], ("XN", e % 2)
            act(xn[:], XS[:, slot, :], AF.Copy, [xsk, ("RSTD", e)], [xk], scale=RSTD[:, e:e + 1])
            yield
            xn_transposes(xn, xk, lambda k: XNT1[:, k, e * 128:(e + 1) * 128], ("XNT", e), 0, s, e % 2, par_l)
            yield

        def norm2_blocks(s, par_l):
            for i in range(4):
                act(JUNK, X[:, i, :], AF.Square, [("X", i + 1)], [("CZ", 0), ("CZ", 1), ("SS2", i + 1)], accum_out=SS2[:, i:i + 1])
            act(RSTD2[:], SS2[:], AF.Sqrt, [("SS2", i + 1) for i in range(4)] + ["EPSC"], ["RSTD2"], scale=1.0 / D, bias=EPSC[:, 0:1])
            recip(RSTD2[:], "RSTD2")
            yield
            for i in range(4):
                xn, xk = XN[i % 2], ("XN", i % 2)
                act(xn[:], X[:, i, :], AF.Copy, [("X", i + 1), "RSTD2"], [xk], scale=RSTD2[:, i:i + 1])
                yield
                xn_transposes(xn, xk, lambda k: XNT2[:, k, i * 128:(i + 1) * 128], ("XNT2", i), 1, s, i % 2, par_l)
                yield

        def tile_geom(kind, ti):
            if kind == "p":
                return ti * 512, 128, 640, [(128, 256), (384, 256)]
            return NP_TOK + ti * 512, (0 if ti > 0 else 128), (768 if ti < 7 else 640), [(128, 512)]

        def tile_id(kind, ti):
            return ti if kind == "p" else 2 + ti

        def src_rows(l, r0, n):
            if l == 0:
                if r0 < NP_TOK:
                    return xp_d[r0:r0 + n, :]
                return xs_d[r0 - NP_TOK:r0 - NP_TOK + n, :]
            return rs_d[(l - 1) % 2][r0:r0 + n, :]

        def rs_key(l, tid, e):
            if l == 0:
                return []
            if e == 0:
                return [("rs", (l - 1) % 2, tid - 1, 4)]
            if e == 5:
                return [("rs", (l - 1) % 2, tid + 1, 1)]
            return [("rs", (l - 1) % 2, tid, e)]

        xs_rr = [0]

        def phases_A(l, kind, ti):
            s = 0 if kind == "p" else 1
            r = l * 24
            row0, lo, hi, segs = tile_geom(kind, ti)
            eblocks = list(range(lo // 128, hi // 128))
            cblocks = [1, 2, 3, 4]
            gap = lambda s0: 32 if (kind == "p" and s0 == 384) else 0
            colof = lambda tok, s0: tok - 96 + gap(s0)
            tid = tile_id(kind, ti)
            xnt_r = [("XNT", e) for e in eblocks]

            def fm_proj(wslot, lhs_fn, t_lo, t_hi):
                b = nbank()
                for k in range(8):
                    mm(PSB[b][:, 0:t_hi - t_lo], lhs_fn(k), XNT1[:, k, t_lo:t_hi], k == 0, k == 7,
                       [("WB", wslot)] + xnt_r, [("ps", b)])
                return b

            def rope_t1t2(bp, n, a, bnd):
                tt("dve", T1[:, 0:n], PSB[bp][:, 0:n], ROPE[:, 0, a:bnd], ALU.mult, [("ps", bp), "ROPE"], ["T1"])
                for (o, i) in [(0, 32), (32, 0), (64, 96), (96, 64)]:
                    tt("dve", T2[o:o + 32, 0:n], PSB[bp][i:i + 32, 0:n], ROPE[o:o + 32, 1, a:bnd], ALU.mult,
                       [("ps", bp), "ROPE"], ["T2"])

            def ph_norm1():
                if kind == "s":
                    t0 = ti * 512 - 128 + lo
                    dma("sp", ROPE[:, :, lo:hi], rope_d[:, :, t0:t0 + hi - lo].rearrange("t p n -> p t n"), [], ["ROPE"])
                vi = 0 if kind == "s" and 0 < ti < 7 else (1 if kind == "s" and ti == 0 else (2 if kind == "s" else 3))
                dma("sp", INVC[:], invc_d[vi], [], ["INVC"])
                for e in eblocks:
                    slot = xs_rr[0] % 2
                    xs_rr[0] += 1
                    dma("sp", XS[:, slot, :], src_rows(l, row0 - 128 + e * 128, 128), rs_key(l, tid, e), [("XS", slot)])
                    yield from norm1_block(e, slot, s, l % 2)

            def ph_proj():
                w0 = next_piece("w_in", l, 0, 0, 512)
                W0 = WB[w0]
                for g in range(2):
                    bq = fm_proj(w0, lambda k: W0[:, k, g * 128:(g + 1) * 128], 128, 640)
                    if kind == "s":
                        rope_t1t2(bq, 512, 128, 640)
                        for j in range(2):
                            tt("pool", QZ[j * 64:(j + 1) * 64, g, :, j * 128:(j + 1) * 128],
                               T1[j * 64:(j + 1) * 64, :].rearrange("p (i q) -> p i q", i=4),
                               T2[j * 64:(j + 1) * 64, :].rearrange("p (i q) -> p i q", i=4), ALU.add,
                               ["T1", "T2"], [("QZ", g)])
                    else:
                        for j in range(2):
                            cp("act", QZ[j * 64:(j + 1) * 64, g, :, j * 128:(j + 1) * 128],
                               PSB[bq][j * 64:(j + 1) * 64, :].rearrange("p (i q) -> p i q", i=4),
                               [("ps", bq)], [("QZ", g)])
                    yield
                kranges = [(lo, 384), (384, hi)]
                for (a, bnd) in kranges:
                    n = bnd - a
                    bk = fm_proj(w0, lambda k: W0[:, k, 256:384], a, bnd)
                    if kind == "s":
                        rope_t1t2(bk, n, a, bnd)
                        for g in range(2):
                            for j in range(2):
                                tt("pool", KT[j * 64:(j + 1) * 64, g, a:bnd], T1[g * 64:(g + 1) * 64, 0:n],
                                   T2[g * 64:(g + 1) * 64, 0:n], ALU.add, ["T1", "T2"], [("KT", g)])
                    else:
                        for g in range(2):
                            for j in range(2):
                                cp("act", KT[j * 64:(j + 1) * 64, g, a:bnd], PSB[bk][g * 64:(g + 1) * 64, 0:n],
                                   [("ps", bk)], [("KT", g)])
                    yield
                for e in eblocks:
                    b = nbank()
                    for k in range(8):
                        mm(PSB[b][:, 0:256], XNT1[:, k, e * 128:(e + 1) * 128], W0[:, k, 256:512], k == 0, k == 7,
                           [("WB", w0), ("XNT", e)], [("ps", b)])
                    cp("dve", VV[:, e, :].rearrange("s (g j d) -> s g j d", g=2, j=2)[:, :, 0, :],
                       PSB[b][:, 128:256].rearrange("s (g d) -> s g d", g=2),
                       [("ps", b)], [("VV", e)])
                    if kind == "p":
                        cp("act", KVO[:, e - 1, :], PSB[b][:, 0:256], [("ps", b)], [("CACC", 0), ("CACC", 1)])
                    if e % 2 == 0:
                        yield
                rel(w0)
                if kind == "p":
                    for sq in range(2):
                        seq = ti * 2 + sq
                        dma("sp", nk_d[seq, l].rearrange("(c s) f -> s c f", s=128), KVO[:, 2 * sq:2 * sq + 2, 0:128],
                            [("CACC", 0), ("CACC", 1)], [])
                        dma("sp", nv_d[seq, l].rearrange("(c s) f -> s c f", s=128), KVO[:, 2 * sq:2 * sq + 2, 128:256],
                            [("CACC", 0), ("CACC", 1)], [])

                yield
                w1 = next_piece("w_in", l, 0, 512, 512)
                w2 = next_piece("w_in", l, 0, 1024, 512)
                hranges = [(max(lo, 112), 384, 128), (384, min(hi, 656), 384 if kind == "p" else 128)]
                for c in range(2):
                    for (a, bnd, s0) in hranges:
                        n = bnd - a
                        bx = fm_proj(w1, lambda k: WB[w1][:, k, c * 128:(c + 1) * 128], a, bnd)
                        cp("dve", XP[:, c, colof(a, s0):colof(bnd, s0)], PSB[bx][:, 0:n], [("ps", bx)], [("XP", c)])
                    yield
                for c in range(2):
                    for (a, bnd, s0) in hranges:
                        n = bnd - a
                        bg = fm_proj(w2, lambda k: WB[w2][:, k, c * 128:(c + 1) * 128], a, bnd)
                        act(SG[:, 0:n], PSB[bg][:, 0:n], AF.Sigmoid, [("ps", bg)], ["T1"])
                        ba = fm_proj(w1, lambda k: WB[w1][:, k, 256 + c * 128:256 + (c + 1) * 128], a, bnd)
                        tt("dve", U[:, c, colof(a, s0):colof(bnd, s0)], PSB[ba][:, 0:n], SG[:, 0:n], ALU.mult,
                           [("ps", ba), "T1"], [("U", c)])
                        yield
                rel(w1)
                for c in range(2):
                    bu = fm_proj(w2, lambda k: WB[w2][:, k, 256 + c * 128:256 + (c + 1) * 128], 128, 640)
                    act(GU[:, c, :], PSB[bu][:], AF.Gelu, [("ps", bu)], [("GU", c)])
                    yield
                rel(w2)
                w3 = next_piece("w_in", l, 0, 1536, 256)
                for i, e in enumerate(cblocks):
                    b = nbank()
                    for k in range(8):
                        mm(PSB[b][:, 0:256], XNT1[:, k, e * 128:(e + 1) * 128], WB[w3][:, k, 0:256], k == 0, k == 7,
                           [("WB", w3), ("XNT", e)], [("ps", b)])
                    act(GV[:], PSB[b][:, 0:256], AF.Gelu, [("ps", b)], ["GV"])
                    act(JUNK[:, 0:256], GV[:], AF.Square, ["GV"], [("CZ", 0), ("CZ", 1), "GSS"], accum_out=SS[:, 6:7])
                    act(RSTD[:, 6:7], SS[:, 6:7], AF.Sqrt, ["GSS", "EPSC"], ["GRS"], scale=1.0 / 256, bias=EPSC[:, 0:1])
                    recip(RSTD[:, 6:7], "GRS")
                    for c in range(2):
                        for gl in range(2):
                            ts("dve", GVZ[:, i, c, gl, gl * 64:(gl + 1) * 64], GV[:, c * 128 + gl * 64:c * 128 + (gl + 1) * 64],
                               RSTD[:, 6:7], None, ALU.mult, None, ["GV", "GRS"], [("GVZ", i)])
                    if i % 2 == 1:
                        yield
                rel(w3)
                yield

            def ph_mix():
                xpk = [("XP", 0), ("XP", 1)]
                uk = [("U", 0), ("U", 1)]
                for (s0, sl) in segs:
                    c0 = colof(s0, s0)
                    if kind == "p" or lo == 128:
                        memset("dve", XP[:, :, c0 - 16:c0], 0.0, xpk)
                        memset("dve", U[:, :, c0 - 16:c0], 0.0, uk)
                    if kind == "p" or hi == 640:
                        memset("dve", XP[:, :, c0 + sl:c0 + sl + 16], 0.0, xpk)
                        memset("dve", U[:, :, c0 + sl:c0 + sl + 16], 0.0, uk)

                def pool_chain(c):
                    for (s0, sl) in segs:
                        c0 = colof(s0, s0)
                        o = s0 - 128
                        A_, B_ = c0 - 8, c0 + sl + 8
                        tt("dve", PSA[:, A_:B_], XP[:, c, A_ - 1:B_ - 1], XP[:, c, A_:B_], ALU.add, xpk, ["PSA"])
                        if c == 0:
                            cp("dve", PSBF[0:64, c0:c0 + sl], PSA[0:64, c0:c0 + sl], ["PSA"], ["PSBF"])
                            tt("dve", PSBF[64:128, c0:c0 + sl], PSA[64:128, c0 - 1:c0 + sl - 1], PSA[64:128, c0 + 1:c0 + sl + 1],
                               ALU.add, ["PSA"], ["PSBF"])
                        else:
                            A_, B_ = c0 - 6, c0 + sl + 6
                            tt("dve", PSBF[:, A_:B_], PSA[:, A_ - 1:B_ - 1], PSA[:, A_ + 1:B_ + 1], ALU.add, ["PSA"], ["PSBF"])
                            A_, B_ = c0 - 4, c0 + sl + 4
                            tt("dve", PSA[:, A_:B_], PSBF[:, A_ - 2:B_ - 2], PSBF[:, A_ + 2:B_ + 2], ALU.add, ["PSBF"], ["PSA"])
                            cp("dve", PSBF[0:64, c0:c0 + sl], PSA[0:64, c0:c0 + sl], ["PSA"], ["PSBF"])
                            tt("dve", PSBF[64:128, c0:c0 + sl], PSA[64:128, c0 - 4:c0 + sl - 4], PSA[64:128, c0 + 4:c0 + sl + 4],
                               ALU.add, ["PSA"], ["PSBF"])
                        tt("dve", PSBF[:, c0:c0 + sl], PSBF[:, c0:c0 + sl], INVC[:, c, o:o + sl], ALU.mult, ["PSBF", "INVC"], ["PSBF"])
                        tt("dve", PM[:, c, o:o + sl], PSBF[:, c0:c0 + sl], XP[:, c, c0:c0 + sl], ALU.subtract, ["PSBF"] + xpk, [("PM", c)])

                def conv_mm(c):
                    b = nbank()
                    for (s0, sl) in segs:
                        o = s0 - 128
                        c0 = colof(s0, s0)
                        for j in range(31):
                            mm(PSB[b][:, o:o + sl], DG[:, c, j, :], U[:, c, c0 + j - 15:c0 + j - 15 + sl], j == 0, j == 30,
                               ["DG", ("U", c)], [("ps", b)])
                    act(CACC[:, c, :], PSB[b][:], AF.Identity, [("ps", b), "COLS"], [("CACC", c)],
                        bias=COLS[:, r + 18 + c:r + 19 + c])
                    act(CSQ[:, c, :], CACC[:, c, :], AF.Square, [("CACC", c)], [("CSQ", c)])

                def gmlp_c(c):
                    b = nbank()
                    for i in range(4):
                        for gl in range(2):
                            mm(PSB[b][:, i * 128:(i + 1) * 128], GVZ[:, i, c, gl, :], WST[:, c * 2 + gl, :],
                               gl == 0, gl == 1, [("GVZ", i), "WST"], [("ps", b)])
                    stt("dve", SV[:].rearrange("p (i q) -> p i q", i=4), PSB[b][:].rearrange("p (i q) -> p i q", i=4),
                        COLS[:, r + 22 + c:r + 23 + c], GBT[:, c, :].unsqueeze(1).to_broadcast([128, 4, 128]),
                        ALU.mult, ALU.add, [("ps", b), "COLS", "GBT"], ["T1"])
                    tt("dve", MIXT[:, 6 + c, :], GU[:, c, :], SV[:], ALU.mult, [("GU", c), "T1"], [("MIXT", 6 + c)])

                pool_chain(0)
                conv_mm(0)
                yield
                pool_chain(1)
                conv_mm(1)
                yield
                gmlp_c(0)
                yield
                for c in range(2):
                    b = nbank()
                    mm(PSB[b][:], PBD[:, c, :], PM[:, c, :], True, True, ["PBD", ("PM", c)], [("ps", b)])
                    act(MIXT[:, 2 + c, :], PSB[b][:], AF.Identity, [("ps", b), "COLS"], [("MIXT", 2 + c)],
                        scale=COLS[:, r + 16 + c:r + 17 + c])
                b = nbank()
                for c in range(2):
                    mm(PSB[b][:], ONESB[:], CSQ[:, c, :], c == 0, c == 1, ["ONESB", ("CSQ", c)], [("ps", b)])
                act(CR[:], PSB[b][:], AF.Sqrt, [("ps", b), "EPSC"], ["T2"], scale=1.0 / 256, bias=EPSC[:, 0:1])
                recip(CR[:], "T2")
                for c in range(2):
                    tt("dve", CACC[:, c, :], CACC[:, c, :], CR[:], ALU.mult, [("CACC", c), "T2"], [("CACC", c)])
                    act(CZ[:, c, :], CACC[:, c, :], AF.Silu, [("CACC", c), "COLS"], [("CZ", c)],
                        scale=COLS[:, r + 20 + c:r + 21 + c])
                yield
                gmlp_c(1)
                yield
                for m in range(2):
                    b = nbank()
                    for k in range(2):
                        mm(PSB[b][:], CPW[:, k, m * 128:(m + 1) * 128], CZ[:, k, :], k == 0, k == 1,
                           ["CPW", ("CZ", k)], [("ps", b)])
                    cp("act", MIXT[:, 4 + m, :], PSB[b][:], [("ps", b)], [("MIXT", 4 + m)])
                yield

            def ph_attn():
                for i, e in enumerate(cblocks):
                    if kind == "s":
                        chunks = []
                        if (e - 1) * 128 >= lo:
                            chunks.append(("loc", e - 1, 0))
                        chunks.append(("loc", e, None))
                        if (e + 1) * 128 < hi:
                            chunks.append(("loc", e + 1, 1))
                        chunks += [("ctx", 0, None), ("ctx", 1, None)]
                    else:
                        s0 = 1 if e <= 2 else 3
                        chunks = [("loc", s0, None), ("loc", s0 + 1, None)]
                    nch = len(chunks)
                    for g in range(2):
                        b5 = 7
                        for ci, (ck, idx, mk) in enumerate(chunks):
                            pb = 4 + ci // 2 if ci < 4 else b5
                            col = (ci % 2) * 256
                            if ck == "loc":
                                lhs, rd = KT[:, g, idx * 128:(idx + 1) * 128], [("KT", g)]
                            else:
                                lhs, rd = KTC[:, g, idx * 128:(idx + 1) * 128], ["KTC"]
                            mm(PSB[pb][:, col:col + 256], lhs, QZ[:, g, i, :], True, True, rd + [("QZ", g)], [("ps", pb)])
                        for ci, (ck, idx, mk) in enumerate(chunks):
                            pb = 4 + ci // 2 if ci < 4 else b5
                            col = (ci % 2) * 256
                            act(PT[:, ci, :], PSB[pb][:, col:col + 256], AF.Exp, [("ps", pb)], [("PT", ci)], scale=0.125)
                            if mk is not None:
                                tt("pool", PT[:, ci, :], PT[:, ci, :], MASK[:, mk, :], ALU.mult, [("PT", ci), "MASK"], [("PT", ci)])
                        yield
                        for ci, (ck, idx, mk) in enumerate(chunks):
                            if ck == "loc":
                                lv, rd = VV[:, idx, g * 128:(g + 1) * 128], [("VV", idx)]
                            else:
                                lv, rd = VVC[:, idx, g * 128:(g + 1) * 128], ["VVC"]
                            mm(PSB[6][:, 0:256], lv, PT[:, ci, :], ci == 0, ci == nch - 1, rd + [("PT", ci)], [("ps", 6)])
                        snk = lambda p0: SINK[p0:p0 + 64, 2 * g:2 * g + 2].unsqueeze(2).to_broadcast([64, 2, 128])
                        for p0 in (0, 64):
                            tt("dve", DEN[p0:p0 + 64, :].rearrange("p (j q) -> p j q", j=2),
                               PSB[6][64:128, 0:256].rearrange("p (j q) -> p j q", j=2), snk(p0), ALU.add,
                               [("ps", 6), "SINK"], ["DEN"])
                        recip(DEN[:], "DEN")
                        for j in range(2):
                            tt("dve", MIXT[j * 64:(j + 1) * 64, g, i * 128:(i + 1) * 128],
                               PSB[6][0:64, j * 128:(j + 1) * 128],
                               DEN[j * 64:(j + 1) * 64, j * 128:(j + 1) * 128], ALU.mult,
                               [("ps", 6), "DEN"], [("MIXT", g)])
                        yield


            return [ph_norm1(), ph_proj(), ph_mix(), ph_attn()]

        def phases_B(l, kind, ti):
            s = 0 if kind == "p" else 1
            last = (l == N_LAYERS - 1)
            r = l * 24
            row0, lo, hi, segs = tile_geom(kind, ti)
            cblocks = [1, 2, 3, 4]
            tid = tile_id(kind, ti)
            tcount = [0]

            def resid_update(b, e, hf, gi):
                t = TMPO[tcount[0] % 2]
                tk = ("TMPO", tcount[0] % 2)
                tcount[0] += 1
                tt("dve", t[:], PSB[b][:], GBC[:, gi, hf * 512:(hf + 1) * 512], ALU.mult,
                   [("ps", b), "GBC"], [tk])
                tt("pool", X[:, e - 1, hf * 512:(hf + 1) * 512], X[:, e - 1, hf * 512:(hf + 1) * 512], t[:], ALU.add,
                   [tk, ("X", e)], [("X", e)])

            def ph_wout():
                for e in cblocks:
                    dma("sp", X[:, e - 1, :], src_rows(l, row0 + (e - 1) * 128, 128), rs_key(l, tid, e), [("X", e)])
                if (kind, ti) == TILES[0] or [(kind, ti)] == [t for t in TILES if t[0] == "s"][:1]:
                    gates_for_stream(l, s, (4, 5, 10, 11))
                for hf in range(2):
                    wo = next_piece("w_out", l, 0, hf * 512, 512)
                    for i, e in enumerate(cblocks):
                        b = nbank()
                        for k in range(8):
                            mm(PSB[b][:], MIXT[:, k, i * 128:(i + 1) * 128], WB[wo][:, k, :], k == 0, k == 7,
                               [("WB", wo), ("MIXT", k)], [("ps", b)])
                        resid_update(b, e, hf, 0)
                        if i == 3:
                            rel(wo)
                        yield
                yield from norm2_blocks(s, l % 2)

            def ph_mlp1(hh):
                for pi in range(4):
                    w = next_piece("w_mlp1", l, 0, (hh * 4 + pi) * 512, 512)
                    for fc in range(4):
                        hc = pi * 4 + fc
                        b = nbank()
                        for k in range(8):
                            mm(PSB[b][:], WB[w][:, k, fc * 128:(fc + 1) * 128], XNT2[:, k, :], k == 0, k == 7,
                               [("WB", w)] + [("XNT2", i_) for i_ in range(4)], [("ps", b)])
                        rl = RELU[hc % 2]
                        rk = ("RELU", hc % 2)
                        act(rl[:], PSB[b][:], AF.Relu, [("ps", b)], [rk])
                        tt("dve", HID[:, hc, :], rl[:], rl[:], ALU.mult, [rk], [("HID", hc)])
                        if fc == 3:
                            rel(w)
                        yield

            def ph_mlp2(hh):
                for hf in range(2):
                    for pc in range(2):
                        w = next_piece("w_mlp2", l, (hh * 2 + pc) * 1024, hf * 512, 512)
                        for i in range(4):
                            for kk in range(8):
                                hc = pc * 8 + kk
                                mm(PSB[4 + i][:], HID[:, hc, i * 128:(i + 1) * 128], WB[w][:, kk, :],
                                   pc == 0 and kk == 0, pc == 1 and kk == 7, [("WB", w), ("HID", hc)], [("ps", 4 + i)])
                            if i == 3:
                                rel(w)
                            yield
                    for i, e in enumerate(cblocks):
                        resid_update(4 + i, e, hf, 1)
                if hh == 1:
                    if not last:
                        for e in cblocks:
                            dma("sp", rs_d[l % 2][row0 + (e - 1) * 128:row0 + e * 128, :], X[:, e - 1, :],
                                [("X", e)], [("rs", l % 2, tile_id(kind, ti), e)])
                    else:
                        for e in cblocks:
                            act(JUNK, X[:, e - 1, :], AF.Square, [("X", e)], [("CZ", 0), ("CZ", 1), ("SS2", e)], accum_out=SS2[:, e - 1:e])
                        act(RSTD2[:], SS2[:], AF.Sqrt, [("SS2", e) for e in cblocks] + ["EPSC"], ["RSTD2"],
                            scale=1.0 / D, bias=EPSC[:, 0:1])
                        recip(RSTD2[:], "RSTD2")
                        dma("sp", FNBC, fnorm_d.partition_broadcast(128), [], [("GU", 0), ("GU", 1)])
                        for e in cblocks:
                            stt("dve", X[:, e - 1, :], X[:, e - 1, :], RSTD2[:, e - 1:e], FNBC, ALU.mult, ALU.mult,
                                [("X", e), "RSTD2", ("GU", 0), ("GU", 1)], [("X", e)])
                            dst = yp_d if kind == "p" else ys_d
                            dma("sp", dst[ti * 512 + (e - 1) * 128:ti * 512 + e * 128, :], X[:, e - 1, :], [("X", e)], [])


            return [ph_wout(), ph_mlp1(0), ph_mlp2(0), ph_mlp1(1), ph_mlp2(1)]

        def interleave(*gens):
            live = [g for g in gens if g is not None]
            while live:
                for g in list(live):
                    try:
                        next(g)
                    except StopIteration:
                        live.remove(g)

        def conversions(l):
            out = []
            for name in ("w_in", "w_out", "w_mlp1", "w_mlp2"):
                cr = CV_ROWS[name]
                nrows = wshape[name][0]
                for j in range(nrows // cr):
                    def th(name=name, j=j, cr=cr):
                        src = wd[name][l, j * cr:(j + 1) * cr, :]
                        dst = wbf[name][l, j * cr:(j + 1) * cr, :]
                        S.dma("pool", lambda e: e.dma_start(out=dst, in_=src), writes=[("wbf", name, l, j)], cv=True)
                    out.append(th)
            return out

        def run_all():
            G = [(l, kind, ti) for l in range(N_LAYERS) for (kind, ti) in TILES]
            cvq = []
            nt = len(TILES)
            ng = len(G)
            interleave(ada_fm_gen(0))
            interleave(small_setup_gen(0))
            pa = {0: phases_A(*G[0])}
            for g in pa[0]:
                interleave(g)
            if ng > 1:
                pa[1] = phases_A(*G[1])
                interleave(pa[1][0])
            for m in range(ng):
                l, kind, ti = G[m]
                tpos = m % nt
                pb = phases_B(*G[m])
                nx = pa.get(m + 1)
                if m + 2 < ng:
                    pa[m + 2] = phases_A(*G[m + 2])
                nxt_l = l + 1 if l + 1 < N_LAYERS else None
                ada = ada_fm_gen(nxt_l) if (nxt_l is not None and tpos == max(nt - 4, 0)) else None
                small = small_setup_gen(nxt_l) if (nxt_l is not None and tpos == nt - 2) else None
                if nt == 1 and nxt_l is not None:
                    ada, small = ada_fm_gen(nxt_l), None
                if nxt_l is not None:
                    if tpos == 0:
                        cvq = conversions(nxt_l)
                    for _ in range(3 if tpos < nt - 1 else len(cvq)):
                        if cvq:
                            cvq.pop(0)()
                interleave(pb[0], nx[1] if nx else None)
                interleave(pb[1], nx[2] if nx else None)
                interleave(pb[2], ada)
                interleave(pb[3], nx[3] if nx else None)
                if nt == 1 and nxt_l is not None:
                    interleave(small_setup_gen(nxt_l))
                interleave(pb[4], small, pa[m + 2][0] if m + 2 < ng else None)

        try:
            run_all()
        except _Stop:
            pass
        S.wait_all("sp")
        S.emit(block)
    if record:
        return pieces
    return nc


def _constants():
    ident = np.eye(128, dtype=np.float32)
    t = np.arange(NS_TOK)
    row = (t // 64).astype(np.float32)
    col = (t % 64).astype(np.float32)
    inv = (10000.0 ** (-np.arange(16, dtype=np.float32) / 16)).astype(np.float32)
    ang = np.concatenate([row[:, None] * inv[None, :], col[:, None] * inv[None, :]], axis=1)
    cos = np.cos(ang).astype(np.float32).T
    sin = np.sin(ang).astype(np.float32).T
    cos64 = np.concatenate([cos, cos], 0)
    sin64 = np.concatenate([-sin, sin], 0)
    rope = np.stack([np.concatenate([cos64, cos64], 0), np.concatenate([sin64, sin64], 0)], 0).astype(np.float32)
    sizes = (2, 4, 8, 16)

    def inv_tab(n, t0, length):
        tt_ = np.arange(t0, t0 + length)
        out = np.zeros((128, 2, length), np.float32)
        for g, sz in enumerate(sizes):
            h = sz // 2
            lo = np.clip(tt_ - h, 0, n)
            hi = np.clip(tt_ + h, 0, n)
            c, gl = g // 2, g % 2
            out[gl * 64:(gl + 1) * 64, c, :] = (1.0 / (hi - lo).astype(np.float32))[None, :]
        return out

    invc = np.stack([
        inv_tab(4096, 512, 512),
        inv_tab(4096, 0, 512),
        inv_tab(4096, 4096 - 512, 512),
        np.concatenate([inv_tab(256, 0, 256), inv_tab(256, 0, 256)], axis=2),
    ], 0).astype(np.float32)
    sidx = np.arange(128)[:, None]
    qidx = np.arange(128)[None, :]
    mp = (sidx >= qidx).astype(np.float32)
    mn = (sidx <= qidx).astype(np.float32)
    masks = np.stack([np.concatenate([mp, mp], 1), np.concatenate([mn, mn], 1)], 0).astype(np.float32)
    return ident, rope, invc, masks


_NC_CACHE = {}


def kernel(x_prompt, x_sample, cache_k, cache_v, c, c_ctx, w_ada, b_ada, norm1, norm2,
           w_in, w_out, attn_sink, pool_w, pool_scale, conv_dw, conv_b, conv_norm, conv_pw,
           gm_norm, gm_ws, gm_b, w_mlp1, w_mlp2, final_norm):
    f = lambda a: np.ascontiguousarray(np.asarray(a, dtype=np.float32))
    if "nc" not in _NC_CACHE:
        _NC_CACHE["nc"] = build_program(build_program())
    nc = _NC_CACHE["nc"]
    ident, rope, invc, masks = _constants()
    shared = {
        "w_ada": f(w_ada), "w_in": f(w_in), "w_out": f(w_out), "w_mlp1": f(w_mlp1), "w_mlp2": f(w_mlp2),
        "b_ada": f(b_ada), "norm1": f(norm1), "norm2": f(norm2), "attn_sink": f(attn_sink),
        "pool_w": f(pool_w), "pool_scale": f(pool_scale), "conv_dw": f(conv_dw), "conv_b": f(conv_b),
        "conv_norm": f(conv_norm), "conv_pw": f(conv_pw), "gm_norm": f(gm_norm), "gm_ws": f(gm_ws),
        "gm_b": f(gm_b), "final_norm": f(final_norm).reshape(1, D),
        "ident": ident, "rope": rope, "invcnt": invc, "masks": masks,
    }
    x_prompt = f(x_prompt)
    x_sample = f(x_sample)
    cache_k = f(cache_k)
    cache_v = f(cache_v)
    c = f(c)
    c_ctx = f(c_ctx)
    in_maps = []
    for i in range(N_CORES):
        m = dict(shared)
        m["xs"] = x_sample[i]
        m["xp"] = x_prompt[4 * i:4 * i + 4].reshape(NP_TOK, D)
        m["ck"] = cache_k[i].reshape(DEPTH, 256, 128)
        m["cv"] = cache_v[i].reshape(DEPTH, 256, 128)
        m["cc"] = np.stack([c_ctx, c[i]], 0)
        in_maps.append(m)
    res = run_bass_kernel_spmd(nc, in_maps, core_ids=list(range(N_CORES)))
    rr = res.results
    y_prompt = np.concatenate([r["yp"].reshape(4, 256, D) for r in rr], 0)
    y_sample = np.stack([r["ys"] for r in rr], 0)
    new_k = np.concatenate([r["nk"].reshape(4, DEPTH, 256, 2, 64) for r in rr], 0)
    new_v = np.concatenate([r["nv"].reshape(4, DEPTH, 256, 2, 64) for r in rr], 0)
    return (y_prompt.astype(np.float32), y_sample.astype(np.float32),
            new_k.astype(np.float32), new_v.astype(np.float32))
```

```python
from contextlib import ExitStack
import numpy as np
import concourse.bass as bass
import concourse.mybir as mybir
from concourse.bass_utils import run_bass_kernel_spmd

F32 = mybir.dt.float32
BF16 = mybir.dt.bfloat16
AF = mybir.ActivationFunctionType
ALU = mybir.AluOpType

D = 1024
DEPTH = 4
NP_TOK = 1024
NS_TOK = 4096
NTOK = NP_TOK + NS_TOK
IN_W = 1792
EPS = 1e-6
N_CORES = 8

SAME_ENGINE_SYNC = True
N_DMA_SEMS = 24
STRICT_KEYS = {("CZ", 0), ("CZ", 1)}
N_CV_SEMS = 6
NW = 4


class Sched:
    ENGS = ("pe", "act", "dve", "pool", "sp")

    def __init__(self, nc, stack):
        self.nc = nc
        self.prog = {e: [] for e in self.ENGS}
        self.cnt = {e: 0 for e in self.ENGS}
        self.sem = {e: stack.enter_context(nc.semaphore("s_" + e)) for e in self.ENGS}
        self.dsem = [stack.enter_context(nc.semaphore("d%d" % i)) for i in range(N_DMA_SEMS + N_CV_SEMS)]
        self.dcnt = [0] * (N_DMA_SEMS + N_CV_SEMS)
        self.dnext = {"hw": 0, "sw": 0, "cv": 0}
        self.waited = {}
        self.lastw = {}
        self.readers = {}

    def _need(self, eng, ev):
        if ev is None:
            return
        sk, val, prod = ev
        if prod == eng and (eng == "pe" or not SAME_ENGINE_SYNC):
            return
        if self.waited.get((eng, sk), 0) >= val:
            return
        self.waited[(eng, sk)] = val
        semh = self.sem[sk] if isinstance(sk, str) else self.dsem[sk]
        self.prog[eng].append(("wait", semh, val))

    def _deps(self, eng, reads, writes, is_dma=False):
        for r in reads:
            self._need(eng, self.lastw.get(r))
            if isinstance(r, tuple) and r[0] == "ps":
                for ev in self.readers.get(r, ()):
                    if ev[2] != eng:
                        self._need(eng, ev)
        for w in writes:
            strict = is_dma or w in STRICT_KEYS
            ev = self.lastw.get(w)
            if ev is not None and (strict or ev[2] != eng):
                self._need(eng, ev)
            for ev in self.readers.get(w, ()):
                if strict or ev[2] != eng:
                    self._need(eng, ev)

    def _commit(self, ev, reads, writes):
        for r in reads:
            lst = self.readers.setdefault(r, [])
            lst[:] = [x for x in lst if x[0] != ev[0]]
            lst.append(ev)
        for w in writes:
            self.lastw[w] = ev
            self.readers[w] = []

    def op(self, eng, fn, reads=(), writes=()):
        self._deps(eng, reads, writes)
        self.cnt[eng] += 1
        ev = (eng, self.cnt[eng], eng)
        self.prog[eng].append(("op", fn, self.sem[eng], 1))
        self._commit(ev, reads, writes)
        return ev

    def dma(self, q, fn, reads=(), writes=(), cv=False):
        half = N_DMA_SEMS // 2
        if cv:
            i = N_DMA_SEMS + self.dnext["cv"]
            self.dnext["cv"] = (self.dnext["cv"] + 1) % N_CV_SEMS
        else:
            kq = "sw" if q == "pool" else "hw"
            i = self.dnext[kq] + (half if kq == "sw" else 0)
            self.dnext[kq] = (self.dnext[kq] + 1) % half
        if self.dcnt[i] > 0:
            self._need(q, (i, self.dcnt[i], None))
        self._deps(q, reads, writes, is_dma=True)
        self.dcnt[i] += 16
        ev = (i, self.dcnt[i], None)
        self.prog[q].append(("op", fn, self.dsem[i], 16))
        self._commit(ev, reads, writes)
        return ev

    def wait_all(self, eng):
        for i in range(N_DMA_SEMS + N_CV_SEMS):
            if self.dcnt[i]:
                self._need(eng, (i, self.dcnt[i], None))
        for e in self.ENGS:
            if e != eng and self.cnt[e]:
                self._need(eng, (e, self.cnt[e], e))

    def emit(self, block):
        mapping = {"pe": block.tensor, "act": block.scalar, "dve": block.vector,
                   "pool": block.gpsimd, "sp": block.sync}
        for e in self.ENGS:
            prog = self.prog[e]

            def body(engine, prog=prog):
                pend = None
                for it in prog:
                    if it[0] == "wait":
                        if pend is not None:
                            engine.wait_ge(pend[1], pend[2])
                        pend = it
                    else:
                        ins = it[1](engine)
                        if pend is not None:
                            ins = ins._wait_ge(pend[1], pend[2])
                            pend = None
                        ins.then_inc(it[2], it[3])
                if pend is not None:
                    engine.wait_ge(pend[1], pend[2])

            mapping[e](body)


TILES = [("p", 0), ("p", 1)] + [("s", t) for t in range(8)]
N_LAYERS = DEPTH
STOP_AT = None


class _Stop(Exception):
    pass


def ckpt(name):
    if STOP_AT == name:
        raise _Stop()


def build_program(pieces=None):
    record = pieces is None
    if record:
        pieces = []
    nc = bass.Bass("TRN2", target_bir_lowering=False)
    dt_in = lambda name, shape: nc.dram_tensor(name, list(shape), F32, kind="ExternalInput").ap()
    dt_out = lambda name, shape: nc.dram_tensor(name, list(shape), F32, kind="ExternalOutput").ap()

    xs_d = dt_in("xs", [NS_TOK, D])
    xp_d = dt_in("xp", [NP_TOK, D])
    ck_d = dt_in("ck", [DEPTH, 256, 128])
    cv_d = dt_in("cv", [DEPTH, 256, 128])
    cc_d = dt_in("cc", [2, D])
    wd = {
        "w_ada": dt_in("w_ada", [DEPTH, D, 6 * D]),
        "w_in": dt_in("w_in", [DEPTH, D, IN_W]),
        "w_out": dt_in("w_out", [DEPTH, D, D]),
        "w_mlp1": dt_in("w_mlp1", [DEPTH, D, 4 * D]),
        "w_mlp2": dt_in("w_mlp2", [DEPTH, 4 * D, D]),
    }
    b_ada_d = dt_in("b_ada", [DEPTH, 6 * D])
    norm1_d = dt_in("norm1", [DEPTH, D])
    norm2_d = dt_in("norm2", [DEPTH, D])
    sink_d = dt_in("attn_sink", [DEPTH, 4])
    pool_w_d = dt_in("pool_w", [DEPTH, 4, 64, 64])
    pool_scale_d = dt_in("pool_scale", [DEPTH, 256])
    conv_dw_d = dt_in("conv_dw", [DEPTH, 31, 256])
    conv_b_d = dt_in("conv_b", [DEPTH, 256])
    conv_norm_d = dt_in("conv_norm", [DEPTH, 256])
    conv_pw_d = dt_in("conv_pw", [DEPTH, 256, 256])
    gm_norm_d = dt_in("gm_norm", [DEPTH, 256])
    gm_ws_d = dt_in("gm_ws", [DEPTH, 4, 128, 128])
    gm_b_d = dt_in("gm_b", [DEPTH, 4, 128])
    fnorm_d = dt_in("final_norm", [1, D])
    ident_d = dt_in("ident", [128, 128])
    rope_d = dt_in("rope", [2, 128, NS_TOK])
    invc_d = dt_in("invcnt", [4, 128, 2, 512])
    mask_d = dt_in("masks", [2, 128, 256])

    yp_d = dt_out("yp", [NP_TOK, D])
    ys_d = dt_out("ys", [NS_TOK, D])
    nk_d = dt_out("nk", [4, DEPTH, 256, 128])
    nv_d = dt_out("nv", [4, DEPTH, 256, 128])
    rs_d = [nc.dram_tensor("rs%d" % i, [NTOK, D], F32).ap() for i in range(2)]
    wshape = {"w_in": [D, IN_W], "w_out": [D, D], "w_mlp1": [D, 4 * D], "w_mlp2": [4 * D, D]}
    wbf = {k: nc.dram_tensor("bf_" + k, [DEPTH] + v, BF16).ap() for k, v in wshape.items()}
    CV_ROWS = {"w_in": 512, "w_out": 512, "w_mlp1": 256, "w_mlp2": 1024}

    with ExitStack() as st:
        S = Sched(nc, st)
        sb = lambda name, shape, dt=F32: st.enter_context(nc.sbuf_tensor(name, list(shape), dt))
        pst = lambda name, shape, dt=F32: st.enter_context(nc.psum_tensor(name, list(shape), dt))

        X = sb("X", [128, 4, D])
        XS = sb("XS", [128, 2, D])
        XN = [sb("XN%d" % i, [128, D], BF16) for i in range(2)]
        XNT1 = sb("XNT1", [128, 8, 768], BF16)
        XNT2 = sb("XNT2", [128, 8, 512], BF16)
        WB = [sb("WB%d" % i, [128, 8, 512], BF16) for i in range(NW)]
        HID = sb("HID", [128, 16, 512], BF16)
        RELU = [sb("RELU%d" % i, [128, 512], BF16) for i in range(2)]
        MIXT = sb("MIXT", [128, 8, 512], BF16)
        QZ = sb("QZ", [128, 2, 4, 256], BF16)
        KT = sb("KT", [128, 2, 768], BF16)
        VV = sb("VV", [128, 6, 256], BF16)
        ROPE = sb("ROPE", [128, 2, 768])
        T1 = sb("T1", [128, 512])
        T2 = sb("T2", [128, 512])
        PW = 608
        XP = sb("XP", [128, 2, PW])
        PSA = sb("PSA", [128, PW])
        PSBF = sb("PSBF", [128, PW])
        INVC = sb("INVC", [128, 2, 512])
        PM = sb("PM", [128, 2, 512], BF16)
        U = sb("U", [128, 2, PW], BF16)
        DG = sb("DG", [128, 2, 31, 128], BF16)
        CACC = sb("CACC", [128, 2, 512])
        CSQ = sb("CSQ", [128, 2, 512], BF16)
        CZ = sb("CZ", [128, 2, 512], BF16)
        GU = sb("GU", [128, 2, 512])
        GV = sb("GV", [128, 256])
        GVZ = sb("GVZ", [128, 4, 2, 2, 128], BF16)
        PT = sb("PT", [128, 5, 256], BF16)
        DEN = sb("DEN", [128, 256])
        TMPO = [sb("TMPO%d" % i, [128, 512]) for i in range(2)]
        SS = sb("SS", [128, 8])
        RSTD = sb("RSTD", [128, 8])
        SS2 = sb("SS2", [128, 4])
        RSTD2 = sb("RSTD2", [128, 4])
        JUNK = CZ[:].rearrange("p c n -> p (c n)")
        SG = T1
        CR = T2
        SV = T1
        KVO = CACC[:].rearrange("p c (h n) -> p (c h) n", h=2)
        IDF = sb("IDF", [128, 128])
        IDB = sb("IDB", [128, 128], BF16)
        ONESB = sb("ONESB", [128, 128], BF16)
        MASK = sb("MASK", [128, 2, 256], BF16)
        ROWS = sb("ROWS", [128, 128])
        COLS = sb("COLS", [128, 128])
        DWROW = sb("DWROW", [32, 256])
        DWT = sb("DWT", [128, 2, 32])
        EPSC = sb("EPSC", [128, 1])
        SILUT = sb("SILUT", [128, 8, 2], BF16)
        SILUBC = sb("SILUBC", [128, 8, 128], BF16)
        BADA = [sb("BADA%d" % i, [1, 512], BF16) for i in range(2)]
        MODT = sb("MODT", [128, 2, 32, 2])
        AB = sb("AB", [128, 2, 2, 2, 8, 2])
        GBC = sb("GBC", [128, 2, D])
        FNBC = GU[:].rearrange("p c n -> p (c n)")
        CPW = sb("CPW", [128, 2, 256], BF16)
        PBD = sb("PBD", [128, 2, 128], BF16)
        WSR = PSA[:, 0:512].rearrange("p (g q) -> p g q", g=4)
        WST = sb("WST", [128, 4, 128], BF16)
        GBT = sb("GBT", [128, 2, 128])
        CKR = GV[:].rearrange("p (c f) -> p c f", c=2)
        CKD = sb("CKD", [128, 2, 2, 2, 64], BF16)
        KTC = sb("KTC", [128, 2, 256], BF16)
        VVC = sb("VVC", [128, 2, 256], BF16)
        SINK = sb("SINK", [128, 4])

        PSB = [pst("PSB%d" % i, [128, 512]) for i in range(8)]

        block = st.enter_context(nc.Block())

        bank_rr = [0]

        def nbank():
            b = bank_rr[0]
            bank_rr[0] = (b + 1) % 4
            return b

        def psv16(b):
            return PSB[b][:].bitcast(BF16)

        wstate = {"issued": 0, "used": 0}

        held = {}

        def rel(slot):
            held.pop(slot, None)

        def issue_piece(i):
            name, l, r0, c0, ncol = pieces[i]
            slot = i % NW
            assert record or slot not in held, ("weight ring slot still in use", i, pieces[i], held)
            dst = WB[slot][:, :, 0:ncol]
            if name != "w_ada" and l >= 1:
                src = wbf[name][l, r0:r0 + 1024, c0:c0 + ncol].rearrange("(k p) c -> p k c", p=128)
                cr = CV_ROWS[name]
                rk = [("wbf", name, l, j) for j in range(r0 // cr, (r0 + 1024) // cr)]
                S.dma("sp", lambda e, dst=dst, src=src: e.dma_start(out=dst, in_=src),
                      reads=rk, writes=[("WB", slot)])
            else:
                src = wd[name][l, r0:r0 + 1024, c0:c0 + ncol].rearrange("(k p) c -> p k c", p=128)
                S.dma("pool", lambda e, dst=dst, src=src: e.dma_start(out=dst, in_=src),
                      writes=[("WB", slot)])

        def next_piece(name, l, r0, c0, ncol):
            i = wstate["used"]
            if record:
                pieces.append((name, l, r0, c0, ncol))
            assert pieces[i] == (name, l, r0, c0, ncol), (pieces[i], name, l, r0, c0, ncol)
            while wstate["issued"] < min(len(pieces), i + NW - 1):
                issue_piece(wstate["issued"])
                wstate["issued"] += 1
            wstate["used"] += 1
            held[i % NW] = pieces[i]
            return i % NW

        def mm(out, lhsT, rhs, start, stop, reads, writes, sgc=False):
            S.op("pe", lambda e: e.matmul(out, lhsT=lhsT, rhs=rhs, start=start, stop=stop, skip_group_check=sgc),
                 reads=reads, writes=writes)

        def tr(out, in_, ident, reads, writes):
            S.op("pe", lambda e: e.transpose(out, in_, ident), reads=reads, writes=writes)

        def act(out, in_, func, reads, writes, **kw):
            S.op("act", lambda e: e.activation(out=out, in_=in_, func=func, **kw), reads=reads, writes=writes)

        def ts(eng, out, in0, s1, s2, op0, op1, reads, writes):
            if s2 is None:
                S.op(eng, lambda e: e.tensor_scalar(out=out, in0=in0, scalar1=s1, scalar2=None, op0=op0),
                     reads=reads, writes=writes)
            else:
                S.op(eng, lambda e: e.tensor_scalar(out=out, in0=in0, scalar1=s1, scalar2=s2, op0=op0, op1=op1),
                     reads=reads, writes=writes)

        def tt(eng, out, in0, in1, op, reads, writes):
            S.op(eng, lambda e: e.tensor_tensor(out=out, in0=in0, in1=in1, op=op), reads=reads, writes=writes)

        def stt(eng, out, in0, scalar, in1, op0, op1, reads, writes):
            S.op(eng, lambda e: e.scalar_tensor_tensor(out=out, in0=in0, scalar=scalar, in1=in1, op0=op0, op1=op1),
                 reads=reads, writes=writes)

        def cp(eng, out, in_, reads, writes):
            if eng == "act":
                S.op("act", lambda e: e.copy(out=out, in_=in_), reads=reads, writes=writes)
            else:
                S.op(eng, lambda e: e.tensor_copy(out=out, in_=in_), reads=reads, writes=writes)

        def recip(ap, key):
            S.op("dve", lambda e: e.reciprocal(out=ap, in_=ap), reads=[key], writes=[key])

        def memset(eng, ap, val, writes):
            S.op(eng, lambda e: e.memset(ap, val), writes=writes)

        def dma(q, out, in_, reads, writes):
            S.dma(q, lambda e: e.dma_start(out=out, in_=in_), reads=reads, writes=writes)

        dma("sp", IDF[:], ident_d, [], ["IDF"])
        dma("pool", IDB[:], ident_d, [], ["IDB"])
        dma("pool", MASK[:], mask_d.rearrange("m p q -> p m q"), [], ["MASK"])
        memset("dve", ONESB[:], 1.0, ["ONESB"])
        memset("dve", EPSC[:], EPS, ["EPSC"])
        memset("dve", ROWS[:], 0.0, ["ROWS"])
        memset("pool", QZ[:], 0.0, [("QZ", 0), ("QZ", 1)])
        memset("pool", VV[:], 1.0, [("VV", e_) for e_ in range(6)])
        memset("pool", VVC[:], 1.0, ["VVC"])
        memset("pool", GVZ[:], 0.0, [("GVZ", i) for i in range(4)])
        memset("pool", XP[:], 0.0, [("XP", 0), ("XP", 1)])
        memset("pool", U[:], 0.0, [("U", 0), ("U", 1)])
        memset("dve", DWROW[:], 0.0, ["DWROW"])
        for l in range(DEPTH):
            r = l * 24
            dma("sp", ROWS[r:r + 8, :], norm1_d[l].rearrange("(c p) -> c p", p=128), [], ["ROWS"])
            dma("sp", ROWS[r + 8:r + 16, :], norm2_d[l].rearrange("(c p) -> c p", p=128), [], ["ROWS"])
            for j, v in enumerate([pool_scale_d, conv_b_d, conv_norm_d, gm_norm_d]):
                dma("sp", ROWS[r + 16 + 2 * j:r + 18 + 2 * j, :], v[l].rearrange("(c p) -> c p", p=128), [], ["ROWS"])
        dma("sp", ROWS[104:112, :], cc_d[0].rearrange("(c p) -> c p", p=128), [], ["ROWS"])
        dma("sp", ROWS[112:120, :], cc_d[1].rearrange("(c p) -> c p", p=128), [], ["ROWS"])
        b = nbank()
        tr(PSB[b][:, 0:128], ROWS[:], IDF[:], ["ROWS", "IDF"], [("ps", b)])
        cp("dve", COLS[:], PSB[b][:, 0:128], [("ps", b)], ["COLS"])
        for s in range(2):
            act(SILUT[:, :, s], COLS[:, 104 + 8 * s:112 + 8 * s], AF.Silu, ["COLS"], ["SILUT"])

        def gates_for_stream(l, s, piece_ids):
            cp("dve", SILUBC[:], SILUT[:, :, s].unsqueeze(2).to_broadcast([128, 8, 128]), ["SILUT"], ["SILUBC"])
            for n, pi in enumerate(piece_ids):
                slot = next_piece("w_ada", l, 0, pi * 512, 512)
                gi = 0 if pi < 6 else 1
                half = pi % 2
                bb = BADA[n % 2]
                dma("pool", bb[:], b_ada_d[l:l + 1, pi * 512:(pi + 1) * 512], [], [("BADA", n % 2)])
                b = nbank()
                for k in range(8):
                    mm(PSB[b][:], SILUBC[:, k, :], WB[slot][:, k, :], k == 0, False,
                       ["SILUBC", ("WB", slot)], [("ps", b)])
                mm(PSB[b][:], ONESB[0:1, :], bb[0:1, :], False, True, ["ONESB", ("BADA", n % 2)], [("ps", b)])
                cp("act", GBC[:, gi, half * 512:(half + 1) * 512], PSB[b][:], [("ps", b)], ["GBC"])
                rel(slot)

        def ada_fm_gen(l):
            r = l * 24
            par = l % 2
            for pi in (0, 1, 2, 3, 6, 7, 8, 9):
                kind = pi // 2
                slot = next_piece("w_ada", l, 0, pi * 512, 512)
                mi = {0: 0, 1: 1, 3: 2, 4: 3}[kind]
                bb = BADA[pi % 2]
                dma("pool", bb[:], b_ada_d[l:l + 1, pi * 512:(pi + 1) * 512], [], [("BADA", pi % 2)])
                b = nbank()
                for fc in range(4):
                    for k in range(8):
                        mm(PSB[b][:, fc * 2:fc * 2 + 2], WB[slot][:, k, fc * 128:(fc + 1) * 128], SILUT[:, k, :],
                           k == 0 and fc == 0, False, ["SILUT", ("WB", slot)], [("ps", b)], sgc=True)
                    mm(PSB[b][:, fc * 2:fc * 2 + 2], bb[0:1, fc * 128:(fc + 1) * 128],
                       ONESB[0:1, 0:2], False, True, ["ONESB", ("BADA", pi % 2)], [("ps", b)], sgc=True)
                    if fc < 3:
                        yield
                rel(slot)
                c0 = mi * 8 + (pi % 2) * 4
                cp("dve", MODT[:, par, c0:c0 + 4, :].rearrange("p a b -> p (a b)"), PSB[b][:, 0:8], [("ps", b)], [("MODT", par)])
                yield
            for n in range(2):
                for s_ in range(2):
                    stt("dve", AB[:, par, n, 0, :, s_], MODT[:, par, (2 * n + 1) * 8:(2 * n + 2) * 8, s_], 1.0,
                        COLS[:, r + 8 * n:r + 8 * n + 8], ALU.add, ALU.mult, [("MODT", par), "COLS"], [("AB", par)])
                    cp("dve", AB[:, par, n, 1, :, s_], MODT[:, par, (2 * n) * 8:(2 * n + 1) * 8, s_], [("MODT", par)], [("AB", par)])
            yield

        def small_setup_gen(l):
            r = l * 24
            dma("pool", CPW[:], conv_pw_d[l].rearrange("(k p) c -> p k c", p=128), [], ["CPW"])
            memset("dve", PBD[:], 0.0, ["PBD"])
            for g in range(4):
                c, gl = g // 2, g % 2
                dma("pool", PBD[gl * 64:(gl + 1) * 64, c, gl * 64:(gl + 1) * 64], pool_w_d[l, g], [], ["PBD"])
            dma("sp", DWROW[0:31, :], conv_dw_d[l], [], ["DWROW"])
            for c in range(2):
                b = nbank()
                tr(PSB[b][:, 0:32], DWROW[:, c * 128:(c + 1) * 128], IDF[0:32, 0:32], ["DWROW", "IDF"], [("ps", b)])
                cp("dve", DWT[:, c, :], PSB[b][:, 0:32], [("ps", b)], ["DWT"])
                tt("dve", DG[:, c], IDB[:].unsqueeze(1).to_broadcast([128, 31, 128]),
                   DWT[:, c, 0:31].unsqueeze(2).to_broadcast([128, 31, 128]), ALU.mult, ["IDB", "DWT"], ["DG"])
            yield
            dma("sp", WSR, gm_ws_d[l].rearrange("g p q -> p g q"), [], ["PSA"])
            for g in range(4):
                b = nbank()
                tr(PSB[b][:, 0:128], WSR[:, g, :], IDF[:], ["PSA", "IDF"], [("ps", b)])
                cp("dve", WST[:, g, :], PSB[b][:, 0:128], [("ps", b)], ["WST"])
            for g in range(4):
                c, gl = g // 2, g % 2
                dma("sp", GBT[gl * 64:(gl + 1) * 64, c, :], gm_b_d[l, g:g + 1, :].partition_broadcast(64), [], ["GBT"])
            yield
            dma("sp", CKR, ck_d[l].rearrange("(c s) f -> s c f", s=128), [], ["GV"])
            for c in range(2):
                cp("dve", CKD[:, c], CKR[:, c, :].rearrange("s (g d) -> s g d", g=2).unsqueeze(2).to_broadcast([128, 2, 2, 64]),
                   ["GV"], ["CKD"])
            for c in range(2):
                for g in range(2):
                    b = nbank()
                    pv = psv16(b)
                    tr(pv[:, 0:128], CKD[:, c, g].rearrange("s j d -> s (j d)"), IDB[:], ["CKD", "IDB"], [("ps", b)])
                    cp("dve", KTC[:, g, c * 128:(c + 1) * 128], pv[:, 0:128], [("ps", b)], ["KTC"])
            dma("sp", CKR, cv_d[l].rearrange("(c s) f -> s c f", s=128), [], ["GV"])
            for c in range(2):
                cp("dve", VVC[:, c, :].rearrange("s (g j d) -> s g j d", g=2, j=2)[:, :, 0, :],
                   CKR[:, c, :].rearrange("s (g d) -> s g d", g=2),
                   ["GV"], ["VVC"])
            dma("sp", SINK[:], sink_d[l:l + 1, :].partition_broadcast(128), [], ["SINK"])
            act(SINK[:], SINK[:], AF.Exp, ["SINK"], ["SINK"])
            yield

        def xn_transposes(xn, xk, dst_fn, dkey, n, s, par, par_l):
            b = nbank()
            pv = psv16(b)
            for k in range(8):
                tr(pv[:, k * 128:(k + 1) * 128], xn[:, k * 128:(k + 1) * 128], IDB[:], [xk, "IDB"], [("ps", b)])
            for k in range(8):
                if True:
                    ts("dve", dst_fn(k), pv[:, k * 128:(k + 1) * 128],
                       AB[:, par_l, n, 0, k, s:s + 1], AB[:, par_l, n, 1, k, s:s + 1], ALU.mult, ALU.add,
                       [("ps", b), ("AB", par_l)], [dkey])
                else:
                    act(dst_fn(k), pv[:, k * 128:(k + 1) * 128], AF.Identity, [("ps", b), ("AB", par_l)], [dkey],
                        scale=AB[:, par_l, n, 0, k, s:s + 1], bias=AB[:, par_l, n, 1, k, s:s + 1])

        def norm1_block(e, slot, s, par_l):
            xsk = ("XS", slot)
            act(JUNK, XS[:, slot, :], AF.Square, [xsk], [("CZ", 0), ("CZ", 1), ("SS", e)], accum_out=SS[:, e:e + 1])
            act(RSTD[:, e:e + 1], SS[:, e:e + 1], AF.Sqrt, [("SS", e), "EPSC"], [("RSTD", e)], scale=1.0 / D, bias=EPSC[:, 0:1])
            recip(RSTD[:, e:e + 1], ("RSTD", e))
            xn, xk = XN[e % 2], ("XN", e % 2)
            act(xn[:], XS[:, slot, :], AF.Copy, [xsk, ("RSTD", e)], [xk], scale=RSTD[:, e:e + 1])
            yield
            xn_transposes(xn, xk, lambda k: XNT1[:, k, e * 128:(e + 1) * 128], ("XNT", e), 0, s, e % 2, par_l)
            yield

        def norm2_blocks(s, par_l):
            for i in range(4):
                act(JUNK, X[:, i, :], AF.Square, [("X", i + 1)], [("CZ", 0), ("CZ", 1), ("SS2", i + 1)], accum_out=SS2[:, i:i + 1])
            act(RSTD2[:], SS2[:], AF.Sqrt, [("SS2", i + 1) for i in range(4)] + ["EPSC"], ["RSTD2"], scale=1.0 / D, bias=EPSC[:, 0:1])
            recip(RSTD2[:], "RSTD2")
            yield
            for i in range(4):
                xn, xk = XN[i % 2], ("XN", i % 2)
                act(xn[:], X[:, i, :], AF.Copy, [("X", i + 1), "RSTD2"], [xk], scale=RSTD2[:, i:i + 1])
                yield
                xn_transposes(xn, xk, lambda k: XNT2[:, k, i * 128:(i + 1) * 128], ("XNT2", i), 1, s, i % 2, par_l)
                yield

        def tile_geom(kind, ti):
            if kind == "p":
                return ti * 512, 128, 640, [(128, 256), (384, 256)]
            return NP_TOK + ti * 512, (0 if ti > 0 else 128), (768 if ti < 7 else 640), [(128, 512)]

        def tile_id(kind, ti):
            return ti if kind == "p" else 2 + ti

        def src_rows(l, r0, n):
            if l == 0:
                if r0 < NP_TOK:
                    return xp_d[r0:r0 + n, :]
                return xs_d[r0 - NP_TOK:r0 - NP_TOK + n, :]
            return rs_d[(l - 1) % 2][r0:r0 + n, :]

        def rs_key(l, tid, e):
            if l == 0:
                return []
            if e == 0:
                return [("rs", (l - 1) % 2, tid - 1, 4)]
            if e == 5:
                return [("rs", (l - 1) % 2, tid + 1, 1)]
            return [("rs", (l - 1) % 2, tid, e)]

        xs_rr = [0]

        def phases_A(l, kind, ti):
            s = 0 if kind == "p" else 1
            r = l * 24
            row0, lo, hi, segs = tile_geom(kind, ti)
            eblocks = list(range(lo // 128, hi // 128))
            cblocks = [1, 2, 3, 4]
            gap = lambda s0: 32 if (kind == "p" and s0 == 384) else 0
            colof = lambda tok, s0: tok - 96 + gap(s0)
            tid = tile_id(kind, ti)
            xnt_r = [("XNT", e) for e in eblocks]

            def fm_proj(wslot, lhs_fn, t_lo, t_hi):
                b = nbank()
                for k in range(8):
                    mm(PSB[b][:, 0:t_hi - t_lo], lhs_fn(k), XNT1[:, k, t_lo:t_hi], k == 0, k == 7,
                       [("WB", wslot)] + xnt_r, [("ps", b)])
                return b

            def rope_t1t2(bp, n, a, bnd):
                tt("dve", T1[:, 0:n], PSB[bp][:, 0:n], ROPE[:, 0, a:bnd], ALU.mult, [("ps", bp), "ROPE"], ["T1"])
                for (o, i) in [(0, 32), (32, 0), (64, 96), (96, 64)]:
                    tt("dve", T2[o:o + 32, 0:n], PSB[bp][i:i + 32, 0:n], ROPE[o:o + 32, 1, a:bnd], ALU.mult,
                       [("ps", bp), "ROPE"], ["T2"])

            def ph_norm1():
                if kind == "s":
                    t0 = ti * 512 - 128 + lo
                    dma("sp", ROPE[:, :, lo:hi], rope_d[:, :, t0:t0 + hi - lo].rearrange("t p n -> p t n"), [], ["ROPE"])
                vi = 0 if kind == "s" and 0 < ti < 7 else (1 if kind == "s" and ti == 0 else (2 if kind == "s" else 3))
                dma("sp", INVC[:], invc_d[vi], [], ["INVC"])
                for e in eblocks:
                    slot = xs_rr[0] % 2
                    xs_rr[0] += 1
                    dma("sp", XS[:, slot, :], src_rows(l, row0 - 128 + e * 128, 128), rs_key(l, tid, e), [("XS", slot)])
                    yield from norm1_block(e, slot, s, l % 2)

            def ph_proj():
                w0 = next_piece("w_in", l, 0, 0, 512)
                W0 = WB[w0]
                for g in range(2):
                    bq = fm_proj(w0, lambda k: W0[:, k, g * 128:(g + 1) * 128], 128, 640)
                    if kind == "s":
                        rope_t1t2(bq, 512, 128, 640)
                        for j in range(2):
                            tt("pool", QZ[j * 64:(j + 1) * 64, g, :, j * 128:(j + 1) * 128],
                               T1[j * 64:(j + 1) * 64, :].rearrange("p (i q) -> p i q", i=4),
                               T2[j * 64:(j + 1) * 64, :].rearrange("p (i q) -> p i q", i=4), ALU.add,
                               ["T1", "T2"], [("QZ", g)])
                    else:
                        for j in range(2):
                            cp("act", QZ[j * 64:(j + 1) * 64, g, :, j * 128:(j + 1) * 128],
                               PSB[bq][j * 64:(j + 1) * 64, :].rearrange("p (i q) -> p i q", i=4),
                               [("ps", bq)], [("QZ", g)])
                    yield
                kranges = [(lo, 384), (384, hi)]
                for (a, bnd) in kranges:
                    n = bnd - a
                    bk = fm_proj(w0, lambda k: W0[:, k, 256:384], a, bnd)
                    if kind == "s":
                        rope_t1t2(bk, n, a, bnd)
                        for g in range(2):
                            for j in range(2):
                                tt("pool", KT[j * 64:(j + 1) * 64, g, a:bnd], T1[g * 64:(g + 1) * 64, 0:n],
                                   T2[g * 64:(g + 1) * 64, 0:n], ALU.add, ["T1", "T2"], [("KT", g)])
                    else:
                        for g in range(2):
                            for j in range(2):
                                cp("act", KT[j * 64:(j + 1) * 64, g, a:bnd], PSB[bk][g * 64:(g + 1) * 64, 0:n],
                                   [("ps", bk)], [("KT", g)])
                    yield
                for e in eblocks:
                    b = nbank()
                    for k in range(8):
                        mm(PSB[b][:, 0:256], XNT1[:, k, e * 128:(e + 1) * 128], W0[:, k, 256:512], k == 0, k == 7,
                           [("WB", w0), ("XNT", e)], [("ps", b)])
                    cp("dve", VV[:, e, :].rearrange("s (g j d) -> s g j d", g=2, j=2)[:, :, 0, :],
                       PSB[b][:, 128:256].rearrange("s (g d) -> s g d", g=2),
                       [("ps", b)], [("VV", e)])
                    if kind == "p":
                        cp("act", KVO[:, e - 1, :], PSB[b][:, 0:256], [("ps", b)], [("CACC", 0), ("CACC", 1)])
                    if e % 2 == 0:
                        yield
                rel(w0)
                if kind == "p":
                    for sq in range(2):
                        seq = ti * 2 + sq
                        dma("sp", nk_d[seq, l].rearrange("(c s) f -> s c f", s=128), KVO[:, 2 * sq:2 * sq + 2, 0:128],
                            [("CACC", 0), ("CACC", 1)], [])
                        dma("sp", nv_d[seq, l].rearrange("(c s) f -> s c f", s=128), KVO[:, 2 * sq:2 * sq + 2, 128:256],
                            [("CACC", 0), ("CACC", 1)], [])

                yield
                w1 = next_piece("w_in", l, 0, 512, 512)
                w2 = next_piece("w_in", l, 0, 1024, 512)
                hranges = [(max(lo, 112), 384, 128), (384, min(hi, 656), 384 if kind == "p" else 128)]
                for c in range(2):
                    for (a, bnd, s0) in hranges:
                        n = bnd - a
                        bx = fm_proj(w1, lambda k: WB[w1][:, k, c * 128:(c + 1) * 128], a, bnd)
                        cp("dve", XP[:, c, colof(a, s0):colof(bnd, s0)], PSB[bx][:, 0:n], [("ps", bx)], [("XP", c)])
                    yield
                for c in range(2):
                    for (a, bnd, s0) in hranges:
                        n = bnd - a
                        bg = fm_proj(w2, lambda k: WB[w2][:, k, c * 128:(c + 1) * 128], a, bnd)
                        act(SG[:, 0:n], PSB[bg][:, 0:n], AF.Sigmoid, [("ps", bg)], ["T1"])
                        ba = fm_proj(w1, lambda k: WB[w1][:, k, 256 + c * 128:256 + (c + 1) * 128], a, bnd)
                        tt("dve", U[:, c, colof(a, s0):colof(bnd, s0)], PSB[ba][:, 0:n], SG[:, 0:n], ALU.mult,
                           [("ps", ba), "T1"], [("U", c)])
                        yield
                rel(w1)
                for c in range(2):
                    bu = fm_proj(w2, lambda k: WB[w2][:, k, 256 + c * 128:256 + (c + 1) * 128], 128, 640)
                    act(GU[:, c, :], PSB[bu][:], AF.Gelu, [("ps", bu)], [("GU", c)])
                    yield
                rel(w2)
                w3 = next_piece("w_in", l, 0, 1536, 256)
                for i, e in enumerate(cblocks):
                    b = nbank()
                    for k in range(8):
                        mm(PSB[b][:, 0:256], XNT1[:, k, e * 128:(e + 1) * 128], WB[w3][:, k, 0:256], k == 0, k == 7,
                           [("WB", w3), ("XNT", e)], [("ps", b)])
                    act(GV[:], PSB[b][:, 0:256], AF.Gelu, [("ps", b)], ["GV"])
                    act(JUNK[:, 0:256], GV[:], AF.Square, ["GV"], [("CZ", 0), ("CZ", 1), "GSS"], accum_out=SS[:, 6:7])
                    act(RSTD[:, 6:7], SS[:, 6:7], AF.Sqrt, ["GSS", "EPSC"], ["GRS"], scale=1.0 / 256, bias=EPSC[:, 0:1])
                    recip(RSTD[:, 6:7], "GRS")
                    for c in range(2):
                        for gl in range(2):
                            ts("dve", GVZ[:, i, c, gl, gl * 64:(gl + 1) * 64], GV[:, c * 128 + gl * 64:c * 128 + (gl + 1) * 64],
                               RSTD[:, 6:7], None, ALU.mult, None, ["GV", "GRS"], [("GVZ", i)])
                    if i % 2 == 1:
                        yield
                rel(w3)
                yield

            def ph_mix():
                xpk = [("XP", 0), ("XP", 1)]
                uk = [("U", 0), ("U", 1)]
                for (s0, sl) in segs:
                    c0 = colof(s0, s0)
                    if kind == "p" or lo == 128:
                        memset("dve", XP[:, :, c0 - 16:c0], 0.0, xpk)
                        memset("dve", U[:, :, c0 - 16:c0], 0.0, uk)
                    if kind == "p" or hi == 640:
                        memset("dve", XP[:, :, c0 + sl:c0 + sl + 16], 0.0, xpk)
                        memset("dve", U[:, :, c0 + sl:c0 + sl + 16], 0.0, uk)

                def pool_chain(c):
                    for (s0, sl) in segs:
                        c0 = colof(s0, s0)
                        o = s0 - 128
                        A_, B_ = c0 - 8, c0 + sl + 8
                        tt("dve", PSA[:, A_:B_], XP[:, c, A_ - 1:B_ - 1], XP[:, c, A_:B_], ALU.add, xpk, ["PSA"])
                        if c == 0:
                            cp("dve", PSBF[0:64, c0:c0 + sl], PSA[0:64, c0:c0 + sl], ["PSA"], ["PSBF"])
                            tt("dve", PSBF[64:128, c0:c0 + sl], PSA[64:128, c0 - 1:c0 + sl - 1], PSA[64:128, c0 + 1:c0 + sl + 1],
                               ALU.add, ["PSA"], ["PSBF"])
                        else:
                            A_, B_ = c0 - 6, c0 + sl + 6
                            tt("dve", PSBF[:, A_:B_], PSA[:, A_ - 1:B_ - 1], PSA[:, A_ + 1:B_ + 1], ALU.add, ["PSA"], ["PSBF"])
                            A_, B_ = c0 - 4, c0 + sl + 4
                            tt("dve", PSA[:, A_:B_], PSBF[:, A_ - 2:B_ - 2], PSBF[:, A_ + 2:B_ + 2], ALU.add, ["PSBF"], ["PSA"])
                            cp("dve", PSBF[0:64, c0:c0 + sl], PSA[0:64, c0:c0 + sl], ["PSA"], ["PSBF"])
                            tt("dve", PSBF[64:128, c0:c0 + sl], PSA[64:128, c0 - 4:c0 + sl - 4], PSA[64:128, c0 + 4:c0 + sl + 4],
                               ALU.add, ["PSA"], ["PSBF"])
                        tt("dve", PSBF[:, c0:c0 + sl], PSBF[:, c0:c0 + sl], INVC[:, c, o:o + sl], ALU.mult, ["PSBF", "INVC"], ["PSBF"])
                        tt("dve", PM[:, c, o:o + sl], PSBF[:, c0:c0 + sl], XP[:, c, c0:c0 + sl], ALU.subtract, ["PSBF"] + xpk, [("PM", c)])

                def conv_mm(c):
                    b = nbank()
                    for (s0, sl) in segs:
                        o = s0 - 128
                        c0 = colof(s0, s0)
                        for j in range(31):
                            mm(PSB[b][:, o:o + sl], DG[:, c, j, :], U[:, c, c0 + j - 15:c0 + j - 15 + sl], j == 0, j == 30,
                               ["DG", ("U", c)], [("ps", b)])
                    act(CACC[:, c, :], PSB[b][:], AF.Identity, [("ps", b), "COLS"], [("CACC", c)],
                        bias=COLS[:, r + 18 + c:r + 19 + c])
                    act(CSQ[:, c, :], CACC[:, c, :], AF.Square, [("CACC", c)], [("CSQ", c)])

                def gmlp_c(c):
                    b = nbank()
                    for i in range(4):
                        for gl in range(2):
                            mm(PSB[b][:, i * 128:(i + 1) * 128], GVZ[:, i, c, gl, :], WST[:, c * 2 + gl, :],
                               gl == 0, gl == 1, [("GVZ", i), "WST"], [("ps", b)])
                    stt("dve", SV[:].rearrange("p (i q) -> p i q", i=4), PSB[b][:].rearrange("p (i q) -> p i q", i=4),
                        COLS[:, r + 22 + c:r + 23 + c], GBT[:, c, :].unsqueeze(1).to_broadcast([128, 4, 128]),
                        ALU.mult, ALU.add, [("ps", b), "COLS", "GBT"], ["T1"])
                    tt("dve", MIXT[:, 6 + c, :], GU[:, c, :], SV[:], ALU.mult, [("GU", c), "T1"], [("MIXT", 6 + c)])

                pool_chain(0)
                conv_mm(0)
                yield
                pool_chain(1)
                conv_mm(1)
                yield
                gmlp_c(0)
                yield
                for c in range(2):
                    b = nbank()
                    mm(PSB[b][:], PBD[:, c, :], PM[:, c, :], True, True, ["PBD", ("PM", c)], [("ps", b)])
                    act(MIXT[:, 2 + c, :], PSB[b][:], AF.Identity, [("ps", b), "COLS"], [("MIXT", 2 + c)],
                        scale=COLS[:, r + 16 + c:r + 17 + c])
                b = nbank()
                for c in range(2):
                    mm(PSB[b][:], ONESB[:], CSQ[:, c, :], c == 0, c == 1, ["ONESB", ("CSQ", c)], [("ps", b)])
                act(CR[:], PSB[b][:], AF.Sqrt, [("ps", b), "EPSC"], ["T2"], scale=1.0 / 256, bias=EPSC[:, 0:1])
                recip(CR[:], "T2")
                for c in range(2):
                    tt("dve", CACC[:, c, :], CACC[:, c, :], CR[:], ALU.mult, [("CACC", c), "T2"], [("CACC", c)])
                    act(CZ[:, c, :], CACC[:, c, :], AF.Silu, [("CACC", c), "COLS"], [("CZ", c)],
                        scale=COLS[:, r + 20 + c:r + 21 + c])
                yield
                gmlp_c(1)
                yield
                for m in range(2):
                    b = nbank()
                    for k in range(2):
                        mm(PSB[b][:], CPW[:, k, m * 128:(m + 1) * 128], CZ[:, k, :], k == 0, k == 1,
                           ["CPW", ("CZ", k)], [("ps", b)])
                    cp("act", MIXT[:, 4 + m, :], PSB[b][:], [("ps", b)], [("MIXT", 4 + m)])
                yield

            def ph_attn():
                for i, e in enumerate(cblocks):
                    if kind == "s":
                        chunks = []
                        if (e - 1) * 128 >= lo:
                            chunks.append(("loc", e - 1, 0))
                        chunks.append(("loc", e, None))
                        if (e + 1) * 128 < hi:
                            chunks.append(("loc", e + 1, 1))
                        chunks += [("ctx", 0, None), ("ctx", 1, None)]
                    else:
                        s0 = 1 if e <= 2 else 3
                        chunks = [("loc", s0, None), ("loc", s0 + 1, None)]
                    nch = len(chunks)
                    for g in range(2):
                        b5 = 7
                        for ci, (ck, idx, mk) in enumerate(chunks):
                            pb = 4 + ci // 2 if ci < 4 else b5
                            col = (ci % 2) * 256
                            if ck == "loc":
                                lhs, rd = KT[:, g, idx * 128:(idx + 1) * 128], [("KT", g)]
                            else:
                                lhs, rd = KTC[:, g, idx * 128:(idx + 1) * 128], ["KTC"]
                            mm(PSB[pb][:, col:col + 256], lhs, QZ[:, g, i, :], True, True, rd + [("QZ", g)], [("ps", pb)])
                        for ci, (ck, idx, mk) in enumerate(chunks):
                            pb = 4 + ci // 2 if ci < 4 else b5
                            col = (ci % 2) * 256
                            act(PT[:, ci, :], PSB[pb][:, col:col + 256], AF.Exp, [("ps", pb)], [("PT", ci)], scale=0.125)
                            if mk is not None:
                                tt("pool", PT[:, ci, :], PT[:, ci, :], MASK[:, mk, :], ALU.mult, [("PT", ci), "MASK"], [("PT", ci)])
                        yield
                        for ci, (ck, idx, mk) in enumerate(chunks):
                            if ck == "loc":
                                lv, rd = VV[:, idx, g * 128:(g + 1) * 128], [("VV", idx)]
                            else:
                                lv, rd = VVC[:, idx, g * 128:(g + 1) * 128], ["VVC"]
                            mm(PSB[6][:, 0:256], lv, PT[:, ci, :], ci == 0, ci == nch - 1, rd + [("PT", ci)], [("ps", 6)])
                        snk = lambda p0: SINK[p0:p0 + 64, 2 * g:2 * g + 2].unsqueeze(2).to_broadcast([64, 2, 128])
                        for p0 in (0, 64):
                            tt("dve", DEN[p0:p0 + 64, :].rearrange("p (j q) -> p j q", j=2),
                               PSB[6][64:128, 0:256].rearrange("p (j q) -> p j q", j=2), snk(p0), ALU.add,
                               [("ps", 6), "SINK"], ["DEN"])
                        recip(DEN[:], "DEN")
                        for j in range(2):
                            tt("dve", MIXT[j * 64:(j + 1) * 64, g, i * 128:(i + 1) * 128],
                               PSB[6][0:64, j * 128:(j + 1) * 128],
                               DEN[j * 64:(j + 1) * 64, j * 128:(j + 1) * 128], ALU.mult,
                               [("ps", 6), "DEN"], [("MIXT", g)])
                        yield


            return [ph_norm1(), ph_proj(), ph_mix(), ph_attn()]

        def phases_B(l, kind, ti):
            s = 0 if kind == "p" else 1
            last = (l == N_LAYERS - 1)
            r = l * 24
            row0, lo, hi, segs = tile_geom(kind, ti)
            cblocks = [1, 2, 3, 4]
            tid = tile_id(kind, ti)
            tcount = [0]

            def resid_update(b, e, hf, gi):
                t = TMPO[tcount[0] % 2]
                tk = ("TMPO", tcount[0] % 2)
                tcount[0] += 1
                tt("dve", t[:], PSB[b][:], GBC[:, gi, hf * 512:(hf + 1) * 512], ALU.mult,
                   [("ps", b), "GBC"], [tk])
                tt("pool", X[:, e - 1, hf * 512:(hf + 1) * 512], X[:, e - 1, hf * 512:(hf + 1) * 512], t[:], ALU.add,
                   [tk, ("X", e)], [("X", e)])

            def ph_wout():
                for e in cblocks:
                    dma("sp", X[:, e - 1, :], src_rows(l, row0 + (e - 1) * 128, 128), rs_key(l, tid, e), [("X", e)])
                if (kind, ti) == TILES[0] or [(kind, ti)] == [t for t in TILES if t[0] == "s"][:1]:
                    gates_for_stream(l, s, (4, 5, 10, 11))
                for hf in range(2):
                    wo = next_piece("w_out", l, 0, hf * 512, 512)
                    for i, e in enumerate(cblocks):
                        b = nbank()
                        for k in range(8):
                            mm(PSB[b][:], MIXT[:, k, i * 128:(i + 1) * 128], WB[wo][:, k, :], k == 0, k == 7,
                               [("WB", wo), ("MIXT", k)], [("ps", b)])
                        resid_update(b, e, hf, 0)
                        if i == 3:
                            rel(wo)
                        yield
                yield from norm2_blocks(s, l % 2)

            def ph_mlp1(hh):
                for pi in range(4):
                    w = next_piece("w_mlp1", l, 0, (hh * 4 + pi) * 512, 512)
                    for fc in range(4):
                        hc = pi * 4 + fc
                        b = nbank()
                        for k in range(8):
                            mm(PSB[b][:], WB[w][:, k, fc * 128:(fc + 1) * 128], XNT2[:, k, :], k == 0, k == 7,
                               [("WB", w)] + [("XNT2", i_) for i_ in range(4)], [("ps", b)])
                        rl = RELU[hc % 2]
                        rk = ("RELU", hc % 2)
                        act(rl[:], PSB[b][:], AF.Relu, [("ps", b)], [rk])
                        tt("dve", HID[:, hc, :], rl[:], rl[:], ALU.mult, [rk], [("HID", hc)])
                        if fc == 3:
                            rel(w)
                        yield

            def ph_mlp2(hh):
                for hf in range(2):
                    for pc in range(2):
                        w = next_piece("w_mlp2", l, (hh * 2 + pc) * 1024, hf * 512, 512)
                        for i in range(4):
                            for kk in range(8):
                                hc = pc * 8 + kk
                                mm(PSB[4 + i][:], HID[:, hc, i * 128:(i + 1) * 128], WB[w][:, kk, :],
                                   pc == 0 and kk == 0, pc == 1 and kk == 7, [("WB", w), ("HID", hc)], [("ps", 4 + i)])
                            if i == 3:
                                rel(w)
                            yield
                    for i, e in enumerate(cblocks):
                        resid_update(4 + i, e, hf, 1)
                if hh == 1:
                    if not last:
                        for e in cblocks:
                            dma("sp", rs_d[l % 2][row0 + (e - 1) * 128:row0 + e * 128, :], X[:, e - 1, :],
                                [("X", e)], [("rs", l % 2, tile_id(kind, ti), e)])
                    else:
                        for e in cblocks:
                            act(JUNK, X[:, e - 1, :], AF.Square, [("X", e)], [("CZ", 0), ("CZ", 1), ("SS2", e)], accum_out=SS2[:, e - 1:e])
                        act(RSTD2[:], SS2[:], AF.Sqrt, [("SS2", e) for e in cblocks] + ["EPSC"], ["RSTD2"],
                            scale=1.0 / D, bias=EPSC[:, 0:1])
                        recip(RSTD2[:], "RSTD2")
                        dma("sp", FNBC, fnorm_d.partition_broadcast(128), [], [("GU", 0), ("GU", 1)])
                        for e in cblocks:
                            stt("dve", X[:, e - 1, :], X[:, e - 1, :], RSTD2[:, e - 1:e], FNBC, ALU.mult, ALU.mult,
                                [("X", e), "RSTD2", ("GU", 0), ("GU", 1)], [("X", e)])
                            dst = yp_d if kind == "p" else ys_d
                            dma("sp", dst[ti * 512 + (e - 1) * 128:ti * 512 + e * 128, :], X[:, e - 1, :], [("X", e)], [])


            return [ph_wout(), ph_mlp1(0), ph_mlp2(0), ph_mlp1(1), ph_mlp2(1)]

        def interleave(*gens):
            live = [g for g in gens if g is not None]
            while live:
                for g in list(live):
                    try:
                        next(g)
                    except StopIteration:
                        live.remove(g)

        def conversions(l):
            out = []
            for name in ("w_in", "w_out", "w_mlp1", "w_mlp2"):
                cr = CV_ROWS[name]
                nrows = wshape[name][0]
                for j in range(nrows // cr):
                    def th(name=name, j=j, cr=cr):
                        src = wd[name][l, j * cr:(j + 1) * cr, :]
                        dst = wbf[name][l, j * cr:(j + 1) * cr, :]
                        S.dma("pool", lambda e: e.dma_start(out=dst, in_=src), writes=[("wbf", name, l, j)], cv=True)
                    out.append(th)
            return out

        def run_all():
            G = [(l, kind, ti) for l in range(N_LAYERS) for (kind, ti) in TILES]
            cvq = []
            nt = len(TILES)
            ng = len(G)
            interleave(ada_fm_gen(0))
            interleave(small_setup_gen(0))
            pa = {0: phases_A(*G[0])}
            for g in pa[0]:
                interleave(g)
            if ng > 1:
                pa[1] = phases_A(*G[1])
                interleave(pa[1][0])
            for m in range(ng):
                l, kind, ti = G[m]
                tpos = m % nt
                pb = phases_B(*G[m])
                nx = pa.get(m + 1)
                if m + 2 < ng:
                    pa[m + 2] = phases_A(*G[m + 2])
                nxt_l = l + 1 if l + 1 < N_LAYERS else None
                ada = ada_fm_gen(nxt_l) if (nxt_l is not None and tpos == max(nt - 4, 0)) else None
                small = small_setup_gen(nxt_l) if (nxt_l is not None and tpos == nt - 2) else None
                if nt == 1 and nxt_l is not None:
                    ada, small = ada_fm_gen(nxt_l), None
                if nxt_l is not None:
                    if tpos == 0:
                        cvq = conversions(nxt_l)
                    for _ in range(3 if tpos < nt - 1 else len(cvq)):
                        if cvq:
                            cvq.pop(0)()
                interleave(pb[0], nx[1] if nx else None)
                interleave(pb[1], nx[2] if nx else None)
                interleave(pb[2], ada)
                interleave(pb[3], nx[3] if nx else None)
                if nt == 1 and nxt_l is not None:
                    interleave(small_setup_gen(nxt_l))
                interleave(pb[4], small, pa[m + 2][0] if m + 2 < ng else None)

        try:
            run_all()
        except _Stop:
            pass
        S.wait_all("sp")
        S.emit(block)
    if record:
        return pieces
    return nc


def _constants():
    ident = np.eye(128, dtype=np.float32)
    t = np.arange(NS_TOK)
    row = (t // 64).astype(np.float32)
    col = (t % 64).astype(np.float32)
    inv = (10000.0 ** (-np.arange(16, dtype=np.float32) / 16)).astype(np.float32)
    ang = np.concatenate([row[:, None] * inv[None, :], col[:, None] * inv[None, :]], axis=1)
    cos = np.cos(ang).astype(np.float32).T
    sin = np.sin(ang).astype(np.float32).T
    cos64 = np.concatenate([cos, cos], 0)
    sin64 = np.concatenate([-sin, sin], 0)
    rope = np.stack([np.concatenate([cos64, cos64], 0), np.concatenate([sin64, sin64], 0)], 0).astype(np.float32)
    sizes = (2, 4, 8, 16)

    def inv_tab(n, t0, length):
        tt_ = np.arange(t0, t0 + length)
        out = np.zeros((128, 2, length), np.float32)
        for g, sz in enumerate(sizes):
            h = sz // 2
            lo = np.clip(tt_ - h, 0, n)
            hi = np.clip(tt_ + h, 0, n)
            c, gl = g // 2, g % 2
            out[gl * 64:(gl + 1) * 64, c, :] = (1.0 / (hi - lo).astype(np.float32))[None, :]
        return out

    invc = np.stack([
        inv_tab(4096, 512, 512),
        inv_tab(4096, 0, 512),
        inv_tab(4096, 4096 - 512, 512),
        np.concatenate([inv_tab(256, 0, 256), inv_tab(256, 0, 256)], axis=2),
    ], 0).astype(np.float32)
    sidx = np.arange(128)[:, None]
    qidx = np.arange(128)[None, :]
    mp = (sidx >= qidx).astype(np.float32)
    mn = (sidx <= qidx).astype(np.float32)
    masks = np.stack([np.concatenate([mp, mp], 1), np.concatenate([mn, mn], 1)], 0).astype(np.float32)
    return ident, rope, invc, masks


_NC_CACHE = {}


def kernel(x_prompt, x_sample, cache_k, cache_v, c, c_ctx, w_ada, b_ada, norm1, norm2,
           w_in, w_out, attn_sink, pool_w, pool_scale, conv_dw, conv_b, conv_norm, conv_pw,
           gm_norm, gm_ws, gm_b, w_mlp1, w_mlp2, final_norm):
    f = lambda a: np.ascontiguousarray(np.asarray(a, dtype=np.float32))
    if "nc" not in _NC_CACHE:
        _NC_CACHE["nc"] = build_program(build_program())
    nc = _NC_CACHE["nc"]
    ident, rope, invc, masks = _constants()
    shared = {
        "w_ada": f(w_ada), "w_in": f(w_in), "w_out": f(w_out), "w_mlp1": f(w_mlp1), "w_mlp2": f(w_mlp2),
        "b_ada": f(b_ada), "norm1": f(norm1), "norm2": f(norm2), "attn_sink": f(attn_sink),
        "pool_w": f(pool_w), "pool_scale": f(pool_scale), "conv_dw": f(conv_dw), "conv_b": f(conv_b),
        "conv_norm": f(conv_norm), "conv_pw": f(conv_pw), "gm_norm": f(gm_norm), "gm_ws": f(gm_ws),
        "gm_b": f(gm_b), "final_norm": f(final_norm).reshape(1, D),
        "ident": ident, "rope": rope, "invcnt": invc, "masks": masks,
    }
    x_prompt = f(x_prompt)
    x_sample = f(x_sample)
    cache_k = f(cache_k)
    cache_v = f(cache_v)
    c = f(c)
    c_ctx = f(c_ctx)
    in_maps = []
    for i in range(N_CORES):
        m = dict(shared)
        m["xs"] = x_sample[i]
        m["xp"] = x_prompt[4 * i:4 * i + 4].reshape(NP_TOK, D)
        m["ck"] = cache_k[i].reshape(DEPTH, 256, 128)
        m["cv"] = cache_v[i].reshape(DEPTH, 256, 128)
        m["cc"] = np.stack([c_ctx, c[i]], 0)
        in_maps.append(m)
    res = run_bass_kernel_spmd(nc, in_maps, core_ids=list(range(N_CORES)))
    rr = res.results
    y_prompt = np.concatenate([r["yp"].reshape(4, 256, D) for r in rr], 0)
    y_sample = np.stack([r["ys"] for r in rr], 0)
    new_k = np.concatenate([r["nk"].reshape(4, DEPTH, 256, 2, 64) for r in rr], 0)
    new_v = np.concatenate([r["nv"].reshape(4, DEPTH, 256, 2, 64) for r in rr], 0)
    return (y_prompt.astype(np.float32), y_sample.astype(np.float32),
            new_k.astype(np.float32), new_v.astype(np.float32))
```
